# Optimizing a Trainium2 kernel written in Bass

```python
import functools
import jax, jax.numpy as jnp
from jax import lax
import numpy as np

D_MODEL = 1024
BATCH = 2
SEQ = 8192
DEPTH = 4
DEC_BATCH = 32
DEC_SEQ = 16
PAST_LEN = 2048

CHUNK = 64
CONV_W = 31
CA = 512
HB = 8
DHB = 64
DB = HB * DHB
HC = 4
DK = 128
DV = 128
DC = HC * DV
D_FF = 4 * D_MODEL
QBLK = 128
EPS = 1e-6
IN_WIDTH = 2 * CA + 3 * DB + HB + 2 * HC * DK + 2 * DC + 3 * D_MODEL

kernel_name = 'hybrid_streaming_conv_fox_hgrn2'


def rms_norm(x, g):
    xf = x.astype(jnp.float32)
    y = xf * lax.rsqrt(jnp.mean(xf * xf, axis=-1, keepdims=True) + EPS)
    return (y * g.astype(jnp.float32)).astype(x.dtype)


def layer_norm(x, g, b):
    xf = x.astype(jnp.float32)
    mu = jnp.mean(xf, axis=-1, keepdims=True)
    xc = xf - mu
    y = xc * lax.rsqrt(jnp.mean(xc * xc, axis=-1, keepdims=True) + EPS)
    return (y * g.astype(jnp.float32) + b.astype(jnp.float32)).astype(x.dtype)


def split_points():
    widths = [CA, CA, DB, DB, DB, HB, HC * DK, HC * DK, DC, DC, D_MODEL, D_MODEL, D_MODEL]
    return [int(v) for v in np.cumsum(widths)[:-1]]


def conformer_conv(a_v, a_g, hist, conv_w, conv_b, ln_g, ln_b, w_out):
    u = a_v * jax.nn.sigmoid(a_g)
    upad = jnp.concatenate([hist.astype(u.dtype), u], axis=1)
    y = lax.conv_general_dilated(upad, conv_w[:, None, :].astype(u.dtype), window_strides=(1,),
                                 padding='VALID', dimension_numbers=('NWC', 'WIO', 'NWC'),
                                 feature_group_count=CA) + conv_b
    y = jax.nn.silu(layer_norm(y, ln_g, ln_b))
    return y @ w_out, upad[:, -(CONV_W - 1):]


def fox_attend_prompt(q, k, v, logf):
    B, S, H, Dh = q.shape
    nb = S // QBLK
    c = jnp.cumsum(logf.astype(jnp.float32), axis=1).swapaxes(1, 2)
    qb = q.reshape(B, nb, QBLK, H, Dh).swapaxes(0, 1)
    cb = c.reshape(B, H, nb, QBLK).transpose(2, 0, 1, 3)
    kpos = jnp.arange(S)
    scale = DHB ** -0.5

    def block(args):
        qi, ci, bi = args
        s = jnp.einsum('bqhd,bkhd->bhqk', qi, k).astype(jnp.float32) * scale
        s = s + ci[..., :, None] - c[..., None, :]
        qpos = bi * QBLK + jnp.arange(QBLK)
        s = jnp.where(kpos[None, :] <= qpos[:, None], s, -jnp.inf)
        p = jax.nn.softmax(s, axis=-1)
        return jnp.einsum('bhqk,bkhd->bqhd', p.astype(v.dtype), v)

    o = lax.map(block, (qb, cb, jnp.arange(nb)))
    return o.swapaxes(0, 1).reshape(B, S, H, Dh)


def fox_attend_cached(q, k, v, logf, k_cache, v_cache, lf_cache):
    P = k_cache.shape[1]
    L = q.shape[1]
    k_all = jnp.concatenate([k_cache.astype(k.dtype), k], axis=1)
    v_all = jnp.concatenate([v_cache.astype(v.dtype), v], axis=1)
    c = jnp.cumsum(jnp.concatenate([lf_cache.astype(jnp.float32), logf.astype(jnp.float32)], axis=1),
                   axis=1).swapaxes(1, 2)
    s = jnp.einsum('bqhd,bkhd->bhqk', q, k_all).astype(jnp.float32) * (DHB ** -0.5)
    s = s + c[:, :, P:, None] - c[:, :, None, :]
    mask = jnp.arange(P + L)[None, :] <= (P + jnp.arange(L))[:, None]
    s = jnp.where(mask, s, -jnp.inf)
    p = jax.nn.softmax(s, axis=-1)
    return jnp.einsum('bhqk,bkhd->bqhd', p.astype(v.dtype), v_all)


def hgrn2_scan(q, kk, v, logf, S0, chunk):
    B, L, H, K = q.shape
    V = v.shape[-1]
    n = L // chunk

    def split(t):
        return t.reshape(B, n, chunk, *t.shape[2:]).swapaxes(0, 1)

    tri = jnp.tril(jnp.ones((chunk, chunk), dtype=bool))

    def step(S, xs):
        qc, kc, vc, gc = xs
        b = jnp.cumsum(gc, axis=1)
        diff = b[:, :, None] - b[:, None, :]
        dec = jnp.exp(jnp.where(tri[None, :, :, None, None], diff, -jnp.inf))
        A = jnp.einsum('bthk,btshk,bshk->bhts', qc, dec, kc)
        o = jnp.einsum('bhts,bshv->bthv', A, vc) + jnp.einsum('bthk,bhkv->bthv', qc * jnp.exp(b), S)
        bl = b[:, -1]
        S = jnp.exp(bl)[..., None] * S + jnp.einsum('bshk,bshv->bhkv', kc * jnp.exp(bl[:, None] - b), vc)
        return S, o

    S, o = lax.scan(step, S0, (split(q), split(kk), split(v), split(logf)))
    return o.swapaxes(0, 1).reshape(B, L, H, V), S


def hgrn2_lower_bounds(p):
    s = jax.nn.softmax(p.astype(jnp.float32), axis=0)
    return jnp.maximum(jnp.cumsum(s, axis=0) - s[0], 0.0)


def hgrn2_branch(q_c, f_c, i_c, o_c, lb, S0, norm_g, w_out):
    B, L, _ = q_c.shape
    f32 = jnp.float32
    q = jax.nn.silu(q_c.astype(f32)).reshape(B, L, HC, DK)
    zf = f_c.astype(f32).reshape(B, L, HC, DK)
    lbh = lb.reshape(HC, DK)
    logf = jnp.logaddexp(jnp.log(lbh), jnp.log1p(-lbh) + jax.nn.log_sigmoid(zf))
    kk = (1.0 - lbh) * jax.nn.sigmoid(-zf)
    v = i_c.astype(f32).reshape(B, L, HC, DV)
    o, S = hgrn2_scan(q, kk, v, logf, S0.astype(f32), min(CHUNK, L))
    o = o * lax.rsqrt(jnp.mean(o * o, axis=-1, keepdims=True) + EPS) * norm_g.astype(f32)
    o = o * jax.nn.silu(o_c.astype(f32).reshape(B, L, HC, DV))
    return o.reshape(B, L, DC).astype(q_c.dtype) @ w_out, S


def trunk_layer(x, attend, conv_hist, S0, lb, g1, w_in, conv_w, conv_b, ln_g, ln_b, w_a_out,
                fox_bf, w_b_out, hn_g, w_c_out, w_o, g2, w_up, w_down):
    B, L, _ = x.shape
    h = rms_norm(x, g1)
    z = h @ w_in
    a_v, a_g, q_b, k_b, v_b, f_b, q_c, f_c, i_c, o_c, g_a, g_b, g_c = jnp.split(z, split_points(), axis=-1)
    y_a, conv_new = conformer_conv(a_v, a_g, conv_hist, conv_w, conv_b, ln_g, ln_b, w_a_out)
    qb = q_b.reshape(B, L, HB, DHB)
    kb = k_b.reshape(B, L, HB, DHB)
    vb = v_b.reshape(B, L, HB, DHB)
    logf_b = jax.nn.log_sigmoid((f_b + fox_bf).astype(jnp.float32))
    o_b = attend(qb, kb, vb, logf_b)
    y_b = o_b.reshape(B, L, DB) @ w_b_out
    y_c, S_new = hgrn2_branch(q_c, f_c, i_c, o_c, lb, S0, hn_g, w_c_out)
    m = jax.nn.sigmoid(g_a) * y_a + jax.nn.sigmoid(g_b) * y_b + jax.nn.sigmoid(g_c) * y_c
    x = x + m @ w_o
    u = rms_norm(x, g2) @ w_up
    x = x + jnp.square(jax.nn.relu(u)) @ w_down
    return x, kb, vb, logf_b, conv_new, S_new


def setup_inputs(seed: int = 0) -> dict:
    key = jax.random.key(seed)
    ks = jax.random.split(key, 32)

    def nrm(k, shape, scale):
        return jax.random.normal(k, shape, jnp.float32) * scale

    return {
        'x_prompt': nrm(ks[0], (BATCH, SEQ, D_MODEL), 1.0),
        'x_sample': nrm(ks[1], (DEC_BATCH, DEC_SEQ, D_MODEL), 1.0),
        'cache_fox_k': nrm(ks[2], (DEPTH, DEC_BATCH, PAST_LEN, HB, DHB), 1.0),
        'cache_fox_v': nrm(ks[3], (DEPTH, DEC_BATCH, PAST_LEN, HB, DHB), 1.0),
        'cache_fox_logf': jax.nn.log_sigmoid(2.0 + nrm(ks[4], (DEPTH, DEC_BATCH, PAST_LEN, HB), 1.0)),
        'state_conv': nrm(ks[5], (DEPTH, DEC_BATCH, CONV_W - 1, CA), 0.5),
        'state_hgrn': nrm(ks[6], (DEPTH, DEC_BATCH, HC, DK, DV), 0.5),
        'norm1_g': 1.0 + nrm(ks[7], (DEPTH, D_MODEL), 0.02),
        'w_in': nrm(ks[8], (DEPTH, D_MODEL, IN_WIDTH), D_MODEL ** -0.5),
        'conv_w': nrm(ks[9], (DEPTH, CONV_W, CA), CONV_W ** -0.5),
        'conv_b': nrm(ks[10], (DEPTH, CA), 0.02),
        'conv_ln_g': 1.0 + nrm(ks[11], (DEPTH, CA), 0.02),
        'conv_ln_b': nrm(ks[12], (DEPTH, CA), 0.02),
        'w_a_out': nrm(ks[13], (DEPTH, CA, D_MODEL), CA ** -0.5),
        'fox_bf': 2.0 + nrm(ks[14], (DEPTH, HB), 0.5),
        'w_b_out': nrm(ks[15], (DEPTH, DB, D_MODEL), DB ** -0.5),
        'hgrn_lb_param': nrm(ks[16], (DEPTH, HC * DK), 0.5),
        'hgrn_norm_g': 1.0 + nrm(ks[17], (DEPTH, DV), 0.02),
        'w_c_out': nrm(ks[18], (DEPTH, DC, D_MODEL), DC ** -0.5),
        'w_o': nrm(ks[19], (DEPTH, D_MODEL, D_MODEL), D_MODEL ** -0.5),
        'norm2_g': 1.0 + nrm(ks[20], (DEPTH, D_MODEL), 0.02),
        'w_up': nrm(ks[21], (DEPTH, D_MODEL, D_FF), D_MODEL ** -0.5),
        'w_down': nrm(ks[22], (DEPTH, D_FF, D_MODEL), D_FF ** -0.5),
        'final_g': 1.0 + nrm(ks[23], (D_MODEL,), 0.02),
    }


def reference(x_prompt, x_sample, cache_fox_k, cache_fox_v, cache_fox_logf, state_conv, state_hgrn,
              norm1_g, w_in, conv_w, conv_b, conv_ln_g, conv_ln_b, w_a_out, fox_bf, w_b_out,
              hgrn_lb_param, hgrn_norm_g, w_c_out, w_o, norm2_g, w_up, w_down, final_g):
    lbs = hgrn2_lower_bounds(hgrn_lb_param)
    xp, xs = x_prompt, x_sample
    Bp = xp.shape[0]
    kp_l, vp_l, lfp_l, cp_l, hp_l = [], [], [], [], []
    ks_l, vs_l, lfs_l, cs_l, hs_l = [], [], [], [], []
    for l in range(DEPTH):
        w = (lbs[l], norm1_g[l], w_in[l], conv_w[l], conv_b[l], conv_ln_g[l], conv_ln_b[l], w_a_out[l],
             fox_bf[l], w_b_out[l], hgrn_norm_g[l], w_c_out[l], w_o[l], norm2_g[l], w_up[l], w_down[l])
        zero_hist = jnp.zeros((Bp, CONV_W - 1, CA), xp.dtype)
        zero_S = jnp.zeros((Bp, HC, DK, DV), jnp.float32)
        xp, kp, vp, lfp, cp, hp = trunk_layer(xp, fox_attend_prompt, zero_hist, zero_S, *w)
        attend_s = functools.partial(fox_attend_cached, k_cache=cache_fox_k[l], v_cache=cache_fox_v[l],
                                     lf_cache=cache_fox_logf[l])
        xs, k_s, v_s, lfs, cs, hs = trunk_layer(xs, attend_s, state_conv[l], state_hgrn[l], *w)
        kp_l.append(kp); vp_l.append(vp); lfp_l.append(lfp); cp_l.append(cp); hp_l.append(hp)
        ks_l.append(k_s); vs_l.append(v_s); lfs_l.append(lfs); cs_l.append(cs); hs_l.append(hs)
    y_prompt = rms_norm(xp, final_g)
    y_sample = rms_norm(xs, final_g)
    return (y_prompt, y_sample,
            jnp.stack(kp_l), jnp.stack(vp_l), jnp.stack(lfp_l), jnp.stack(cp_l), jnp.stack(hp_l),
            jnp.stack(ks_l), jnp.stack(vs_l), jnp.stack(lfs_l), jnp.stack(cs_l), jnp.stack(hs_l))
```

```python
import contextlib
import numpy as np
import concourse.bass as bass
import concourse.mybir as mybir
from concourse.bass_utils import run_bass_kernel_spmd

F32 = mybir.dt.float32
BF16 = mybir.dt.bfloat16
ALU = mybir.AluOpType
AF = mybir.ActivationFunctionType
AX = mybir.AxisListType

ENGS = ("pe", "act", "dve", "pool", "sp")
SAME_ENGINE_SYNC = {"pool", "dve", "act"}
EPS = 1e-6


class Buf:
    __slots__ = ("w", "r", "name", "dw", "dr")

    def __init__(self, name=""):
        self.w = None
        self.r = {}
        self.name = name
        self.dw = None
        self.dr = None


class DmaSem:
    def __init__(self, handle):
        self.h = handle
        self.n = 0


class _Rec:
    def __init__(self):
        self.call = None

    def __getattr__(self, name):
        def f(*a, **k):
            self.call = (name, a, k)
            return self
        return f


class Sched:
    def __init__(self, nc, stack):
        self.nc = nc
        self.stack = stack
        self.ops = {e: [] for e in ENGS}
        self.count = {e: 0 for e in ENGS}
        self.seen = {e: {} for e in ENGS}
        self.esem = {e: stack.enter_context(nc.semaphore("es_" + e)) for e in ENGS}
        self.dsems = []
        self.nwaits = 0

    def dma_sem(self, name):
        d = DmaSem(self.stack.enter_context(self.nc.semaphore(name)))
        self.dsems.append(d)
        return d

    def op(self, eng, fn, reads=(), writes=(), dsem=None, raw=()):
        waits = {}
        seen = self.seen[eng]

        def need(k, v):
            if dsem is None and k == eng and eng not in SAME_ENGINE_SYNC:
                return
            if seen.get(k, 0) >= v:
                return
            if waits.get(k, 0) < v:
                waits[k] = v

        for b in reads:
            if b.w is not None:
                need(*b.w)
        for b in raw:
            if b.w is not None:
                need(*b.w)
        for b in writes:
            if b.w is not None:
                need(*b.w)
            for k, v in b.r.items():
                need(k, v)
        for k, v in waits.items():
            seen[k] = v
        if dsem is None:
            self.count[eng] += 1
            ev = (eng, self.count[eng])
        else:
            dsem.n += 16
            ev = (dsem, dsem.n)
        for b in reads:
            if b.r.get(ev[0], 0) < ev[1]:
                b.r[ev[0]] = ev[1]
        for b in writes:
            b.w = ev
            b.r = {}
        self.nwaits += len(waits)
        rec = _Rec()
        fn(rec)
        self.ops[eng].append((list(waits.items()), rec.call, ev))

    def pe(self, fn, reads=(), writes=()):
        self.op("pe", fn, reads, writes)

    def act(self, fn, reads=(), writes=()):
        self.op("act", fn, reads, writes)

    def dve(self, fn, reads=(), writes=()):
        self.op("dve", fn, reads, writes)

    def pool(self, fn, reads=(), writes=()):
        self.op("pool", fn, reads, writes)

    def auto_sem(self, reads, writes):
        if writes:
            b = writes[0]
            if b.dw is None:
                b.dw = self.dma_sem("dw%d" % len(self.dsems))
            return b.dw
        b = reads[0]
        if b.dr is None:
            b.dr = self.dma_sem("dr%d" % len(self.dsems))
        return b.dr

    def dma(self, q, dsem, out, in_, reads=(), writes=(), raw=(), **kw):
        if dsem is None:
            dsem = self.auto_sem(reads, writes)
        self.op(q, lambda e: e.dma_start(out=out, in_=in_, **kw), reads, writes, dsem=dsem, raw=raw)

    def dma_group(self, q, dsem, pairs, reads=(), writes=(), raw=(), **kw):
        for i, (out, in_) in enumerate(pairs):
            if i == 0:
                self.op(q, lambda e, out=out, in_=in_: e.dma_start(out=out, in_=in_, **kw), reads, writes, dsem=dsem,
                        raw=raw)
            else:
                self.op(q, lambda e, out=out, in_=in_: e.dma_start(out=out, in_=in_, **kw), (), (), dsem=dsem)
        ev = (dsem, dsem.n)
        for b in reads:
            b.r[dsem] = dsem.n
        for b in writes:
            b.w = ev
            b.r = {}

    def emit(self):
        nc = self.nc
        fin = []
        for e in ENGS:
            if self.count[e]:
                fin.append((e, self.count[e]))
        for d in self.dsems:
            if d.n:
                fin.append((d, d.n))

        def semh(k):
            return self.esem[k] if isinstance(k, str) else k.h

        def run(engname, engobj):
            for waits, fn, ev in self.ops[engname]:
                for k, v in waits:
                    engobj.wait_ge(semh(k), v)
                ins = getattr(engobj, fn[0])(*fn[1], **fn[2])
                if isinstance(ev[0], str):
                    ins.then_inc(self.esem[ev[0]], 1)
                else:
                    ins.then_inc(ev[0].h, 16)
            if engname == "sp":
                for k, v in fin:
                    engobj.wait_ge(semh(k), v)

        with nc.Block() as block:
            @block.tensor
            def _(e):
                run("pe", e)

            @block.scalar
            def _(e):
                run("act", e)

            @block.vector
            def _(e):
                run("dve", e)

            @block.gpsimd
            def _(e):
                run("pool", e)

            @block.sync
            def _(e):
                run("sp", e)


class Rot:
    def __init__(self, tiles):
        self.tiles = tiles
        self.bufs = [Buf() for _ in tiles]
        self.i = 0

    def get(self):
        k = self.i % len(self.tiles)
        self.i += 1
        return self.tiles[k], self.bufs[k]


D = 1024
INW = 7688
NPAST = 2048
NSS = 4
LS = 16
VW = 68
C_AV, C_AG, C_Q, C_K, C_V, C_F = 0, 512, 1024, 1536, 2048, 2560
C_HQ, C_HF, C_HI, C_HO = 2568, 3080, 3592, 4104
C_GA, C_GB, C_GC = 4616, 5640, 6664

WNAMES = ["norm1_g", "w_in", "conv_w", "conv_b", "conv_ln_g", "conv_ln_b", "w_a_out", "fox_bf", "w_b_out",
          "hgrn_lb_param", "hgrn_norm_g", "w_c_out", "w_o", "norm2_g", "w_up", "w_down", "final_g"]


def build_program(SEQ=8192, L=4, T=512, SAMPLE=True, PROMPT=True):
    nc = bass.Bass("TRN2", target_bir_lowering=False)
    NT = SEQ // T
    NSUB = T // 128
    NCH = T // 64

    def din(name, shape):
        return nc.dram_tensor(name, shape, F32, kind="ExternalInput").ap()

    def dout(name, shape):
        return nc.dram_tensor(name, shape, F32, kind="ExternalOutput").ap()

    xp = din("xp", [SEQ, D])
    xs = din("xs", [NSS * LS, D])
    ck = din("ck", [L, NSS, NPAST, 512])
    cv = din("cv", [L, NSS, NPAST, 512])
    clf = din("clf", [L, NSS, NPAST, 8])
    sconv = din("sconv", [L, NSS, 30, 512])
    shg = din("shg", [L, NSS, 4, 128, 128])
    Wd = {
        "norm1_g": din("norm1_g", [L, D]), "w_in": din("w_in", [L, D, INW]), "conv_w": din("conv_w", [L, 31, 512]),
        "conv_b": din("conv_b", [L, 512]), "conv_ln_g": din("conv_ln_g", [L, 512]),
        "conv_ln_b": din("conv_ln_b", [L, 512]), "w_a_out": din("w_a_out", [L, 512, D]),
        "fox_bf": din("fox_bf", [L, 8]), "w_b_out": din("w_b_out", [L, 512, D]),
        "hgrn_lb_param": din("hgrn_lb_param", [L, 512]), "hgrn_norm_g": din("hgrn_norm_g", [L, 128]),
        "w_c_out": din("w_c_out", [L, 512, D]), "w_o": din("w_o", [L, D, D]), "norm2_g": din("norm2_g", [L, D]),
        "w_up": din("w_up", [L, D, 4096]), "w_down": din("w_down", [L, 4096, D]), "final_g": din("final_g", [1, D]),
    }
    yp = dout("yp", [SEQ, D])
    ys = dout("ys", [NSS * LS, D])
    kp = dout("kp", [L, SEQ, 512])
    vp = dout("vp", [L, SEQ, 512])
    lfp = dout("lfp", [L, SEQ, 8])
    cp = dout("cp", [L, 30, 512])
    hp = dout("hp", [L, 4, 128, 128])
    ks = dout("ks", [L, NSS, LS, 512])
    vs = dout("vs", [L, NSS, LS, 512])
    lfs = dout("lfs", [L, NSS, LS, 8])
    cs = dout("cs", [L, NSS, 30, 512])
    hs = dout("hs", [L, NSS, 4, 128, 128])
    BIGW = ["w_in", "w_a_out", "w_b_out", "w_c_out", "w_o", "w_up", "w_down"]
    Wb = {n: nc.dram_tensor(n + "_bf", list(Wd[n].shape), BF16, kind="Internal").ap() for n in BIGW}
    kscr = nc.dram_tensor("kscr", [L, 68, 8, SEQ], BF16, kind="Internal").ap()
    vscr = nc.dram_tensor("vscr", [L, 128, 8, SEQ // 128, VW], BF16, kind="Internal").ap()

    with contextlib.ExitStack() as st:
        S = Sched(nc, st)
        _n = [0]

        def sb(shape, dt, name=None):
            _n[0] += 1
            return st.enter_context(nc.sbuf_tensor(name or ("t%d" % _n[0]), shape, dt))

        def pst(shape, dt, name=None):
            _n[0] += 1
            return st.enter_context(nc.psum_tensor(name or ("p%d" % _n[0]), shape, dt))

        xT = sb([128, 8, T], F32, "xT")
        Bx = [Buf() for _ in range(8)]
        hT = sb([128, 8, T], BF16, "hT")
        Bh = [Buf() for _ in range(8)]
        sq = sb([128, 8, T], BF16, "sq")
        Bsq = [Buf() for _ in range(8)]
        mT = sb([128, 8, T], F32, "mT")
        Bm = [Buf() for _ in range(8)]
        brT = sb([128, 4, T], BF16, "brT")
        Bbr = [Buf() for _ in range(4)]
        stg = Rot([sb([128, 1024], F32, "stage%d" % i) for i in range(2)])
        NW = 4
        WSZ = 8 * 520
        wrot = Rot([sb([128, WSZ], BF16, "w%d" % i) for i in range(NW)])
        wsem = [S.dma_sem("wsem%d" % i) for i in range(NW)]
        ft = Rot([sb([128, T], F32, "ft%d" % i) for i in range(8)])
        psg = Rot([pst([128, 512], F32, "psg%d" % i) for i in range(4)])
        psS = Rot([pst([128, 512], F32, "psS%d" % i) for i in range(2)])
        psO = [pst([128, 4, 128], F32, "psO%d" % i) for i in range(2)]
        BpsO = [Buf(), Buf()]
        ident = sb([128, 128], F32, "ident")
        triu = sb([128, 128], F32, "triu")
        triub = sb([128, 128], BF16, "triub")
        onesD = sb([128, 128], BF16, "onesD")
        onesC = sb([128, 128], BF16, "onesC")
        onesV = sb([128, 128], BF16, "onesV")
        ones = sb([128, T], F32, "ones")
        g1 = sb([128, L, 8], F32, "g1")
        g2 = sb([128, L, 8], F32, "g2")
        gf = sb([128, 8], F32, "gf")
        cw = sb([128, L, 4, 31], F32, "cw")
        cb = sb([128, L, 4], F32, "cb")
        lng = sb([128, L, 4], F32, "lng")
        lnb = sb([128, L, 4], F32, "lnb")
        nbf = sb([8, L], F32, "nbf")
        bfb = sb([128, L, 8], F32, "bfb")
        lbp = sb([128, 4, L], F32, "lbp")
        lb = sb([128, 4, L], F32, "lb")
        oml = sb([128, 4, L], F32, "oml")
        noml = sb([128, 4, L], F32, "noml")
        lbt = sb([128, 4, L], F32, "lbt")
        lbm = sb([128, 4], F32, "lbm")
        hng = sb([128, L], F32, "hng")
        epsb = sb([128, 1], F32, "epsb")
        Bc = Buf()
        csem = S.dma_sem("csem")
        hist = sb([128, L, 4, 30], F32, "hist")
        Bhist = [Buf() for _ in range(L)]
        Sst = sb([128, L, 4, 128], F32, "Sst")
        BS = [[Buf() for _ in range(4)] for _ in range(L)]
        Sbf = Rot([sb([128, 128], BF16, "Sbf%d" % i) for i in range(2)])
        ccar = sb([8, L], F32, "ccar")
        Bccar = [Buf() for _ in range(L)]
        ubuf = sb([128, 4, 30 + T], F32, "ubuf")
        Bub = [Buf() for _ in range(4)]
        yc = sb([128, 4, T], F32, "yc")
        Byc = [Buf() for _ in range(4)]
        ybf = sq[:, 0:4, :]
        ysq = sq[:, 4:8, :]
        Bybf = Bsq[0:4]
        Bysq = Bsq[4:8]
        Qa = sb([68, 8, T], BF16, "Qa")
        Ka = sb([68, 8, T], BF16, "Ka")
        BQ = [Buf() for _ in range(8)]
        BK = [Buf() for _ in range(8)]
        Va = sb([128, 8, NSUB, VW], BF16, "Va")
        BV = Buf()
        ksrc = Rot([sb([68, 2, 512], BF16, "ksrc%d" % i) for i in range(2)])
        vsrc = Rot([sb([128, 2, 4, VW], BF16, "vsrc%d" % i) for i in range(2)])
        kvsem = [S.dma_sem("kvsem%d" % i) for i in range(2)]
        Prot = Rot([sb([128, 512], BF16, "P%d" % i) for i in range(3)])
        otok = yc
        Botok = Byc
        rc = Rot([sb([128, 4], F32, "rc%d" % i) for i in range(2)])
        fT = sb([8, T], F32, "fT")
        csp = sb([8, T], F32, "csp")
        chi = sb([8, T], BF16, "chi")
        clo = sb([8, T], BF16, "clo")
        nhi = sb([8, T], BF16, "nhi")
        nlo = sb([8, T], BF16, "nlo")
        Bf = Buf()
        lftok = sb([128, NSUB, 8], F32, "lftok")
        Blftok = Buf()
        scrsem = S.dma_sem("scrsem")
        Bkscr = [Buf() for _ in range(L)]
        Bvscr = [Buf() for _ in range(L)]
        osem = S.dma_sem("osem")
        stsem = S.dma_sem("stsem")
        lfsem = S.dma_sem("lfsem")
        ksem_ = S.dma_sem("kscrsem")
        vsem_ = S.dma_sem("vscrsem")
        rowsem = S.dma_sem("rowsem")
        xsem = S.dma_sem("xsem")
        qs = sb([128, T], F32, "qs")
        kk = sb([128, T], F32, "kk")
        bg = sb([128, T], F32, "bg")
        Bqs, Bkk, Bbg = Buf(), Buf(), Buf()
        qt = sb([128, T], BF16, "qt")
        kt_ = sb([128, T], BF16, "kt_")
        qh = sb([128, T], BF16, "qh")
        kh = sb([128, T], F32, "kh")
        Bqt, Bkt, Bqh, Bkh = Buf(), Buf(), Buf(), Buf()
        vtok = sb([64, NCH, 512], BF16, "vtok")
        Bvtok = [Buf() for _ in range(NCH)]
        khtok = sb([64, NCH, 128], BF16, "khtok")
        Bkhtok = [Buf() for _ in range(NCH)]
        Abf = sb([64, NCH, 64], BF16, "Abf")
        BAbf = Buf()
        sqo = sb([128, T], BF16, "sqo")
        Bsqo = Buf()

        slm = sb([128, 128], F32, "slm")
        bmask = sb([64, 64], BF16, "bmask")
        lfc = sb([128, 16, 8], F32, "lfc")
        rk = sb([128, 16, 8], F32, "rk")
        rkt = sb([128, 16, 8], F32, "rkt")
        Brk = Buf()
        Ssm = Rot([sb([128, 128], F32, "Ssm%d" % i) for i in range(2)])

        PQ = "sp"
        WQ = "pool"

        def init_consts():
            S.pool(lambda e: e.memset(ident[:], 1.0), writes=[Bc])
            S.pool(lambda e: e.affine_select(ident[:], ident[:], [[-1, 128]], ALU.is_equal, 0.0, base=0,
                                             channel_multiplier=1), writes=[Bc])
            S.pool(lambda e: e.memset(triu[:], 1.0), writes=[Bc])
            S.pool(lambda e: e.affine_select(triu[:], triu[:], [[1, 128]], ALU.is_ge, 0.0, base=0,
                                             channel_multiplier=-1), writes=[Bc])
            S.pool(lambda e: e.tensor_copy(triub[:], triu[:]), writes=[Bc])
            S.pool(lambda e: e.memset(slm[:], 1.0), writes=[Bc])
            S.pool(lambda e: e.affine_select(slm[:], slm[:], [[-1, 128]], ALU.is_gt, 0.0, base=0,
                                             channel_multiplier=1), writes=[Bc])
            S.pool(lambda e: e.tensor_copy(bmask[:], triu[0:64, 0:64]), writes=[Bc])
            for b_ in range(1, 4):
                S.pool(lambda e, b_=b_: e.memset(bmask[0:16 * b_, 16 * b_:16 * b_ + 16], 0.0), writes=[Bc])
            S.pool(lambda e: e.memset(onesD[:], 1.0 / 1024), writes=[Bc])
            S.pool(lambda e: e.memset(onesC[:], 1.0 / 512), writes=[Bc])
            S.pool(lambda e: e.memset(onesV[:], 1.0 / 128), writes=[Bc])
            S.pool(lambda e: e.memset(ones[:], 1.0), writes=[Bc])
            S.pool(lambda e: e.memset(epsb[:], EPS), writes=[Bc])
            S.pool(lambda e: e.memset(Qa[64:68, :, :], 1.0), writes=BQ)
            S.pool(lambda e: e.memset(Ka[64:68, :, :], 1.0), writes=BK)
            S.pool(lambda e: e.memset(Va[:, :, :, 64:VW], 1.0), writes=[BV])
            S.pool(lambda e: e.memset(hist[:], 0.0), writes=Bhist)
            S.pool(lambda e: e.memset(Sst[:], 0.0), writes=[b for bl in BS for b in bl])
            S.pool(lambda e: e.memset(ccar[:], 0.0), writes=Bccar)
            W = Wd
            S.dma(PQ, csem, g1[:], W["norm1_g"].rearrange("l (c p) -> p l c", p=128), writes=[Bc])
            S.dma(PQ, csem, g2[:], W["norm2_g"].rearrange("l (c p) -> p l c", p=128), writes=[Bc])
            S.dma(PQ, csem, gf[:], W["final_g"][0].rearrange("(c p) -> p c", p=128), writes=[Bc])
            for l in range(L):
                for c in range(4):
                    S.dma(PQ, csem, cw[:, l, c, :], W["conv_w"][l][:, c * 128:(c + 1) * 128].rearrange("j p -> p j"),
                          writes=[Bc])
            S.dma(PQ, csem, cb[:], W["conv_b"].rearrange("l (c p) -> p l c", p=128), writes=[Bc])
            S.dma(PQ, csem, lng[:], W["conv_ln_g"].rearrange("l (c p) -> p l c", p=128), writes=[Bc])
            S.dma(PQ, csem, lnb[:], W["conv_ln_b"].rearrange("l (c p) -> p l c", p=128), writes=[Bc])
            S.dma(PQ, csem, nbf[:], W["fox_bf"].rearrange("l h -> h l"), writes=[Bc])
            for l in range(L):
                S.dma(PQ, csem, bfb[:, l:l + 1, :], W["fox_bf"][l:l + 1, :].partition_broadcast(128), writes=[Bc])
            for hd_ in range(4):
                S.dma(PQ, csem, lbp[:, hd_, :], W["hgrn_lb_param"][:, hd_ * 128:(hd_ + 1) * 128].rearrange("l p -> p l"),
                      writes=[Bc])
            S.dma(PQ, csem, hng[:], W["hgrn_norm_g"].rearrange("l v -> v l"), writes=[Bc])
            S.dve(lambda e: e.tensor_scalar(nbf[:], nbf[:], -1.0, None, ALU.mult), reads=[Bc], writes=[Bc])
            S.dve(lambda e: e.tensor_reduce(lbm[:], lbp[:], AX.X, ALU.max), reads=[Bc], writes=[Bc])
            S.dve(lambda e: e.tensor_tensor(lbt[:], lbp[:], lbm[:].unsqueeze(2).to_broadcast([128, 4, L]),
                                            ALU.subtract), writes=[Bc])
            S.act(lambda e: e.activation(lbt[:], lbt[:], AF.Exp), reads=[Bc], writes=[Bc])
            S.dve(lambda e: e.tensor_reduce(lbm[:], lbt[:], AX.X, ALU.add), reads=[Bc], writes=[Bc])
            S.dve(lambda e: e.reciprocal(lbm[:], lbm[:]), writes=[Bc])
            S.dve(lambda e: e.tensor_tensor(lbt[:], lbt[:], lbm[:].unsqueeze(2).to_broadcast([128, 4, L]),
                                            ALU.mult), writes=[Bc])
            S.dve(lambda e: e.memset(lb[:], 0.0), writes=[Bc])
            for l in range(1, L):
                S.dve(lambda e, l=l: e.tensor_tensor(lb[:, :, l:l + 1], lb[:, :, l - 1:l], lbt[:, :, l:l + 1],
                                                     ALU.add), writes=[Bc])
            S.dve(lambda e: e.tensor_scalar(oml[:], lb[:], -1.0, 1.0, ALU.mult, ALU.add), writes=[Bc])
            S.dve(lambda e: e.tensor_scalar(noml[:], oml[:], -1.0, None, ALU.mult), writes=[Bc])

        def rsqrt_eps(out_ap, in_ap, reads, wbuf):
            S.act(lambda e: e.activation(out_ap, in_ap, AF.Sqrt, bias=epsb[:, 0:1]), reads=list(reads) + [Bc],
                  writes=[wbuf])
            S.dve(lambda e: e.reciprocal(out_ap, out_ap), writes=[wbuf])

        Bwb = {n: [Buf() for _ in range(L)] for n in BIGW}
        precast_sems = [S.dma_sem("precast%d" % l_) for l_ in range(L)]

        def precast_weights():
            for l in range(L):
                pairs = []
                for n in BIGW:
                    R_, C_ = Wd[n].shape[1], Wd[n].shape[2]
                    npc = (C_ + 2047) // 2048
                    pc = (C_ + npc - 1) // npc
                    for r0 in range(0, R_, 128):
                        for c0 in range(0, C_, pc):
                            c1 = min(C_, c0 + pc)
                            pairs.append((Wb[n][l, r0:r0 + 128, c0:c1], Wd[n][l, r0:r0 + 128, c0:c1]))
                S.dma_group(WQ, precast_sems[l], pairs, writes=[Bwb[n][l] for n in BIGW])

        def wload(name, l_, rs, cs_, kc, cols):
            t, b = wrot.get()
            sem = wsem[(wrot.i - 1) % NW]
            view = t[:, 0:kc * cols].rearrange("p (k c) -> p k c", c=cols)
            src = Wb[name][l_]
            r0, r1 = rs if rs is not None else (0, src.shape[0])
            c0, c1 = cs_ if cs_ is not None else (0, src.shape[1])
            src2d = src[r0:r1, c0:c1]
            S.dma(WQ, sem, view, src2d.rearrange("(k p) c -> p k c", p=128), writes=[b], raw=[Bwb[name][l_]])
            return view, b

        def mm_fm(ps, M, N, wt, wb, col0, nk, act, Bact, extra_w=()):
            for k in range(nk):
                S.pe(lambda e, k=k: e.matmul(ps[0:M, 0:N], lhsT=wt[:, k, col0:col0 + M], rhs=act[:, k, 0:N],
                                             start=(k == 0), stop=(k == nk - 1)),
                     reads=[wb, Bact[k]], writes=list(extra_w))

        def rmsnorm(gt, l, N):
            for c in range(8):
                S.act(lambda e, c=c: e.activation(sq[:, c, 0:N], xT[:, c, 0:N], AF.Square),
                      reads=[Bx[c]], writes=[Bsq[c]])
            ps, bp = psg.get()
            for c in range(8):
                S.pe(lambda e, c=c: e.matmul(ps[:, 0:N], lhsT=onesD[:], rhs=sq[:, c, 0:N], start=(c == 0),
                                             stop=(c == 7)), reads=[Bc, Bsq[c]], writes=[bp])
            rs, brs = ft.get()
            rsqrt_eps(rs[:, 0:N], ps[:, 0:N], [bp], brs)
            for c in range(8):
                gap = gt[:, l, c:c + 1] if l is not None else gt[:, c:c + 1]
                S.dve(lambda e, c=c, gap=gap: e.scalar_tensor_tensor(hT[:, c, 0:N], xT[:, c, 0:N], gap, rs[:, 0:N],
                                                                     ALU.mult, ALU.mult),
                      reads=[Bx[c], brs, Bc], writes=[Bh[c]])

        def branch_out(l, wname, gcol, N, first, last):
            wo_, bwo = wload(wname, l, None, None, 4, 1024)
            wg = [wload("w_in", l, None, (gcol + i * 512, gcol + (i + 1) * 512), 8, 512) for i in range(2)]
            for oc in range(8):
                psy, bpy = psg.get()
                for k in range(4):
                    S.pe(lambda e, k=k, oc=oc: e.matmul(psy[:, 0:N], lhsT=wo_[:, k, oc * 128:(oc + 1) * 128],
                                                        rhs=brT[:, k, 0:N], start=(k == 0), stop=(k == 3)),
                         reads=[bwo, Bbr[k]], writes=[bpy])
                psgt, bpg = psg.get()
                wgt, bwg = wg[oc // 4]
                mm_fm(psgt, 128, N, wgt, bwg, (oc % 4) * 128, 8, hT, Bh, extra_w=[bpg])
                sg, bsg = ft.get()
                S.act(lambda e: e.activation(sg[:, 0:N], psgt[:, 0:N], AF.Sigmoid), reads=[bpg], writes=[bsg])
                if first:
                    S.dve(lambda e, oc=oc: e.tensor_tensor(mT[:, oc, 0:N], sg[:, 0:N], psy[:, 0:N], ALU.mult),
                          reads=[bsg, bpy], writes=[Bm[oc]])
                else:
                    S.dve(lambda e: e.tensor_tensor(sg[:, 0:N], sg[:, 0:N], psy[:, 0:N], ALU.mult),
                          reads=[bpy], writes=[bsg])
                    if last:
                        S.dve(lambda e, oc=oc: e.tensor_tensor(sq[:, oc, 0:N], mT[:, oc, 0:N], sg[:, 0:N], ALU.add),
                              reads=[bsg, Bm[oc]], writes=[Bsq[oc]])
                    else:
                        S.dve(lambda e, oc=oc: e.tensor_tensor(mT[:, oc, 0:N], mT[:, oc, 0:N], sg[:, 0:N], ALU.add),
                              reads=[bsg], writes=[Bm[oc]])

        def conv_branch(l, N, segs):
            wA, bA = wload("w_in", l, None, (C_AV, C_AV + 512), 8, 512)
            wG, bG = wload("w_in", l, None, (C_AG, C_AG + 512), 8, 512)
            for c in range(4):
                psv, bpv = psg.get()
                mm_fm(psv, 128, N, wA, bA, c * 128, 8, hT, Bh, extra_w=[bpv])
                psg_, bpg = psg.get()
                mm_fm(psg_, 128, N, wG, bG, c * 128, 8, hT, Bh, extra_w=[bpg])
                sg, bsg = ft.get()
                S.act(lambda e: e.activation(sg[:, 0:N], psg_[:, 0:N], AF.Sigmoid), reads=[bpg], writes=[bsg])
                segs(l, c, psv, bpv, sg, bsg)

        def conv_post(l, N):
            for c in range(4):
                S.act(lambda e, c=c: e.activation(ybf[:, c, 0:N], yc[:, c, 0:N], AF.Copy), reads=[Byc[c]],
                      writes=[Bybf[c]])
                S.act(lambda e, c=c: e.activation(ysq[:, c, 0:N], yc[:, c, 0:N], AF.Square), reads=[Byc[c]],
                      writes=[Bysq[c]])
            psm, bpm = psg.get()
            for c in range(4):
                S.pe(lambda e, c=c: e.matmul(psm[:, 0:N], lhsT=onesC[:], rhs=ybf[:, c, 0:N], start=(c == 0),
                                             stop=(c == 3)), reads=[Bc, Bybf[c]], writes=[bpm])
            pss, bps = psg.get()
            for c in range(4):
                S.pe(lambda e, c=c: e.matmul(pss[:, 0:N], lhsT=onesC[:], rhs=ysq[:, c, 0:N], start=(c == 0),
                                             stop=(c == 3)), reads=[Bc, Bysq[c]], writes=[bps])
            mean, bmean = ft.get()
            S.dve(lambda e: e.tensor_copy(mean[:, 0:N], psm[:, 0:N]), reads=[bpm], writes=[bmean])
            var, bvar = ft.get()
            S.dve(lambda e: e.tensor_tensor(var[:, 0:N], mean[:, 0:N], mean[:, 0:N], ALU.mult), reads=[bmean],
                  writes=[bvar])
            S.dve(lambda e: e.tensor_tensor(var[:, 0:N], pss[:, 0:N], var[:, 0:N], ALU.subtract), reads=[bps],
                  writes=[bvar])
            rsqrt_eps(var[:, 0:N], var[:, 0:N], [], bvar)
            S.dve(lambda e: e.scalar_tensor_tensor(mean[:, 0:N], mean[:, 0:N], -1.0, var[:, 0:N], ALU.mult, ALU.mult),
                  reads=[bvar], writes=[bmean])
            for c in range(4):
                t, bt = ft.get()
                S.dve(lambda e, c=c: e.tensor_tensor(t[:, 0:N], yc[:, c, 0:N], var[:, 0:N], ALU.mult),
                      reads=[Byc[c], bvar], writes=[bt])
                S.dve(lambda e: e.tensor_tensor(t[:, 0:N], t[:, 0:N], mean[:, 0:N], ALU.add), reads=[bmean],
                      writes=[bt])
                S.act(lambda e, c=c: e.activation(brT[:, c, 0:N], t[:, 0:N], AF.Silu, bias=lnb[:, l, c:c + 1],
                                                  scale=lng[:, l, c:c + 1]), reads=[bt, Bc], writes=[Bbr[c]])

        def conv_taps(l, c, u0, y0, n):
            S.dve(lambda e: e.tensor_scalar(yc[:, c, y0:y0 + n], ubuf[:, c, u0:u0 + n], cw[:, l, c, 0:1],
                                            cb[:, l, c:c + 1], ALU.mult, ALU.add),
                  reads=[Bub[c], Bc], writes=[Byc[c]])
            for j in range(1, 31):
                S.dve(lambda e, j=j: e.scalar_tensor_tensor(yc[:, c, y0:y0 + n], ubuf[:, c, u0 + j:u0 + j + n],
                                                            cw[:, l, c, j:j + 1], yc[:, c, y0:y0 + n],
                                                            ALU.mult, ALU.add), reads=[Bub[c]], writes=[Byc[c]])

        def load_x_tile(src, N):
            nsub = (N + 127) // 128
            for i in range(nsub):
                n = min(128, N - i * 128)
                stt, bst_ = stg.get()
                S.dma(PQ, None, stt[0:n, :], src[i * 128:i * 128 + n, :], writes=[bst_])
                for half in range(2):
                    ps, bp = psg.get()
                    for cc in range(4):
                        c = half * 4 + cc
                        S.pe(lambda e, cc=cc, c=c, n=n, ps=ps, stt=stt: e.transpose(
                            ps[:, cc * 128:cc * 128 + n], stt[0:n, c * 128:(c + 1) * 128], ident[0:n, 0:n]),
                             reads=[bst_, Bc], writes=[bp])
                    S.act(lambda e, i=i, n=n, half=half, ps=ps: e.activation(
                        xT[:, half * 4:half * 4 + 4, i * 128:i * 128 + n],
                        ps[:, :].rearrange("p (c t) -> p c t", t=128)[:, :, 0:n], AF.Copy),
                          reads=[bp], writes=Bx[half * 4:half * 4 + 4])

        def store_y_tile(dst, N):
            for c in range(8):
                S.act(lambda e, c=c: e.activation(sq[:, c, 0:N], xT[:, c, 0:N], AF.Square),
                      reads=[Bx[c]], writes=[Bsq[c]])
            ps, bp = psg.get()
            for c in range(8):
                S.pe(lambda e, c=c: e.matmul(ps[:, 0:N], lhsT=onesD[:], rhs=sq[:, c, 0:N], start=(c == 0),
                                             stop=(c == 7)), reads=[Bc, Bsq[c]], writes=[bp])
            rs, brs = ft.get()
            rsqrt_eps(rs[:, 0:N], ps[:, 0:N], [bp], brs)
            for c in range(8):
                S.dve(lambda e, c=c: e.scalar_tensor_tensor(mT[:, c, 0:N], xT[:, c, 0:N], gf[:, c:c + 1], rs[:, 0:N],
                                                            ALU.mult, ALU.mult),
                      reads=[Bx[c], brs, Bc], writes=[Bm[c]])
            nsub = (N + 127) // 128
            for i in range(nsub):
                n = min(128, N - i * 128)
                stt, bst_ = stg.get()
                for half in range(2):
                    ps, bp = psg.get()
                    for cc in range(4):
                        c = half * 4 + cc
                        S.pe(lambda e, c=c, cc=cc, i=i, n=n, ps=ps: e.transpose(ps[0:n, cc * 128:(cc + 1) * 128],
                                                                         mT[:, c, i * 128:i * 128 + n], ident[:]),
                             reads=[Bm[c], Bc], writes=[bp])
                    S.act(lambda e, n=n, half=half, ps=ps, stt=stt: e.activation(
                        stt[0:n, half * 512:(half + 1) * 512], ps[0:n, :], AF.Copy), reads=[bp], writes=[bst_])
                S.dma(PQ, None, dst[i * 128:i * 128 + n, :], stt[0:n, :], reads=[bst_])

        def ffn_and_wo(l, N):
            wO = [wload("w_o", l, None, (i * 512, (i + 1) * 512), 8, 512) for i in range(2)]
            for oc in range(8):
                ps, bp = psg.get()
                wt, bw = wO[oc // 4]
                mm_fm(ps, 128, N, wt, bw, (oc % 4) * 128, 8, sq, Bsq, extra_w=[bp])
                S.dve(lambda e, oc=oc: e.tensor_tensor(xT[:, oc, 0:N], xT[:, oc, 0:N], ps[:, 0:N], ALU.add),
                      reads=[bp], writes=[Bx[oc]])
            rmsnorm(g2, l, N)
            for q in range(4):
                for j in range(2):
                    wU, bU = wload("w_up", l, None, ((q * 2 + j) * 512, (q * 2 + j + 1) * 512), 8, 512)
                    for cc in range(4):
                        ps, bp = psg.get()
                        mm_fm(ps, 128, N, wU, bU, cc * 128, 8, hT, Bh, extra_w=[bp])
                        r1, br1 = ft.get()
                        S.act(lambda e: e.activation(r1[:, 0:N], ps[:, 0:N], AF.Relu), reads=[bp], writes=[br1])
                        S.dve(lambda e, j=j, cc=cc: e.tensor_tensor(sq[:, j * 4 + cc, 0:N], r1[:, 0:N], r1[:, 0:N],
                                                                    ALU.mult), reads=[br1], writes=[Bsq[j * 4 + cc]])
                wDn = [wload("w_down", l, (q * 1024, (q + 1) * 1024), (i * 512, (i + 1) * 512), 8, 512) for i in range(2)]
                for oc in range(8):
                    ps, bp = psg.get()
                    wt, bw = wDn[oc // 4]
                    mm_fm(ps, 128, N, wt, bw, (oc % 4) * 128, 8, sq, Bsq, extra_w=[bp])
                    S.dve(lambda e, oc=oc: e.tensor_tensor(xT[:, oc, 0:N], xT[:, oc, 0:N], ps[:, 0:N], ALU.add),
                          reads=[bp], writes=[Bx[oc]])

        def fox_qkv(l, N, tok0, kdst, vdst, lfdst, carry=True):
            wQ, bQ = wload("w_in", l, None, (C_Q, C_Q + 512), 8, 512)
            wK, bK = wload("w_in", l, None, (C_K, C_K + 512), 8, 512)
            wV, bV = wload("w_in", l, None, (C_V, C_V + 520), 8, 520)
            for hh in range(8):
                ps, bp = psg.get()
                mm_fm(ps, 64, N, wQ, bQ, hh * 64, 8, hT, Bh, extra_w=[bp])
                S.act(lambda e, hh=hh: e.activation(Qa[0:64, hh, 0:N], ps[0:64, 0:N], AF.Copy, scale=0.125),
                      reads=[bp], writes=[BQ[hh]])
                ps2, bp2 = psg.get()
                mm_fm(ps2, 64, N, wK, bK, hh * 64, 8, hT, Bh, extra_w=[bp2])
                S.dve(lambda e, hh=hh: e.tensor_copy(Ka[0:64, hh, 0:N], ps2[0:64, 0:N]), reads=[bp2], writes=[BK[hh]])
            ps, bp = psg.get()
            mm_fm(ps, 8, N, wV, bV, 512, 8, hT, Bh, extra_w=[bp])
            S.act(lambda e: e.activation(fT[:, 0:N], ps[0:8, 0:N], AF.Exp, bias=nbf[:, l:l + 1], scale=-1.0),
                  reads=[bp, Bc], writes=[Bf])
            S.act(lambda e: e.activation(fT[:, 0:N], fT[:, 0:N], AF.Ln, bias=1.0), writes=[Bf])
            return wK, bK, wV, bV

        def fox_scan_rows(l, N, c0, n, init_ap, Binit):
            S.dve(lambda e: e.tensor_tensor_scan(csp[:, c0:c0 + n], ones[0:8, 0:n], fT[:, c0:c0 + n], init_ap,
                                                 ALU.mult, ALU.add), reads=[Bf, Bc] + Binit, writes=[Bf])

        def fox_split_rows(N):
            S.dve(lambda e: e.tensor_copy(chi[:, 0:N], csp[:, 0:N]), writes=[Bf])
            S.dve(lambda e: e.tensor_tensor(clo[:, 0:N], csp[:, 0:N], chi[:, 0:N], ALU.subtract), writes=[Bf])
            S.dve(lambda e: e.tensor_scalar(nhi[:, 0:N], chi[:, 0:N], -1.0, None, ALU.mult), writes=[Bf])
            S.dve(lambda e: e.tensor_scalar(nlo[:, 0:N], clo[:, 0:N], -1.0, None, ALU.mult), writes=[Bf])
            pairs = []
            for hh in range(8):
                pairs.append((Qa[64:65, hh, 0:N], nhi[hh:hh + 1, 0:N]))
                pairs.append((Qa[65:66, hh, 0:N], nlo[hh:hh + 1, 0:N]))
                pairs.append((Ka[66:67, hh, 0:N], chi[hh:hh + 1, 0:N]))
                pairs.append((Ka[67:68, hh, 0:N], clo[hh:hh + 1, 0:N]))
            S.dma_group(PQ, rowsem, pairs, reads=[Bf], writes=BQ + BK)

        def fox_tok_outputs(l, N, wK, bK, wV, bV, kdst_fn, vdst_fn, lfdst_fn, va_fn):
            nsub = (N + 127) // 128
            for i in range(nsub):
                n = min(128, N - i * 128)
                stt, bst_ = stg.get()
                psk, bpk = psg.get()
                for k in range(8):
                    S.pe(lambda e, k=k, i=i, n=n, psk=psk: e.matmul(psk[0:n, :], lhsT=hT[:, k, i * 128:i * 128 + n],
                                                           rhs=wK[:, k, 0:512], start=(k == 0), stop=(k == 7)),
                         reads=[bK, Bh[k]], writes=[bpk])
                S.act(lambda e, n=n, psk=psk, stt=stt: e.activation(stt[0:n, 0:512], psk[0:n, :], AF.Copy),
                      reads=[bpk], writes=[bst_])
                psv, bpv = psg.get()
                for k in range(8):
                    S.pe(lambda e, k=k, i=i, n=n, psv=psv: e.matmul(psv[0:n, :], lhsT=hT[:, k, i * 128:i * 128 + n],
                                                           rhs=wV[:, k, 0:512], start=(k == 0), stop=(k == 7)),
                         reads=[bV, Bh[k]], writes=[bpv])
                S.act(lambda e, n=n, psv=psv, stt=stt: e.activation(stt[0:n, 512:1024], psv[0:n, :], AF.Copy),
                      reads=[bpv], writes=[bst_])
                if "nova" not in DBG2:
                    va_fn(i, n, stt, bst_)
                if "nolf" in DBG2:
                    kdst_fn(i, n, stt, bst_)
                    vdst_fn(i, n, stt, bst_)
                    continue
                psf, bpf = psg.get()
                for k in range(8):
                    S.pe(lambda e, k=k, i=i, n=n, psf=psf: e.matmul(psf[0:n, 0:8], lhsT=hT[:, k, i * 128:i * 128 + n],
                                                           rhs=wV[:, k, 512:520], start=(k == 0), stop=(k == 7)),
                         reads=[bV, Bh[k]], writes=[bpf])
                S.dve(lambda e, i=i, n=n, psf=psf: e.tensor_tensor(lftok[0:n, i, :], psf[0:n, 0:8], bfb[0:n, l, :], ALU.add),
                      reads=[bpf, Bc], writes=[Blftok])
                S.act(lambda e, i=i, n=n: e.activation(lftok[0:n, i, :], lftok[0:n, i, :], AF.Exp, scale=-1.0),
                      writes=[Blftok])
                S.act(lambda e, i=i, n=n: e.activation(lftok[0:n, i, :], lftok[0:n, i, :], AF.Ln, bias=1.0),
                      writes=[Blftok])
                S.dve(lambda e, i=i, n=n: e.tensor_scalar(lftok[0:n, i, :], lftok[0:n, i, :], -1.0, None, ALU.mult),
                      writes=[Blftok])
                if "nokv" not in DBG2:
                    kdst_fn(i, n, stt, bst_)
                    vdst_fn(i, n, stt, bst_)
                if "nolfdma" not in DBG2:
                    lfdst_fn(i, n)

        def fox_attend_prompt(l, j):
            for g in range(4):
                pairs = []
                for jj in range(j + 1):
                    for hl in range(2):
                        for kt in range(4):
                            pairs.append((jj, hl, kt))
                src = {}
                state = {}

                def get_src(jj):
                    if jj not in src:
                        kt_t, bkt = ksrc.get()
                        ksm = kvsem[(ksrc.i - 1) % 2]
                        vt_t, _unused = vsrc.get()
                        S.dma_group(PQ, ksm, [(kt_t[:], kscr[l, :, 2 * g:2 * g + 2, jj * T:(jj + 1) * T]),
                                              (vt_t[:], vscr[l, :, 2 * g:2 * g + 2, jj * 4:(jj + 1) * 4, :])],
                                    raw=[Bkscr[l], Bvscr[l]], writes=[bkt])
                        src[jj] = (kt_t, vt_t, bkt)
                    return src[jj]

                def emit_qk(n):
                    jj, hl, kt = pairs[n]
                    kt_t, vt_t, bkt = get_src(jj)
                    hh = 2 * g + hl
                    q0 = kt * 128 if jj == j else 0
                    ps, bp = psS.get()
                    S.pe(lambda e: e.matmul(ps[:, q0:T], lhsT=kt_t[:, hl, kt * 128:(kt + 1) * 128],
                                            rhs=Qa[:, hh, q0:T], start=True, stop=True),
                         reads=[bkt, BQ[hh]], writes=[bp])
                    state[n] = (ps, bp, q0)

                def emit_rest(n):
                    jj, hl, kt = pairs[n]
                    kt_t, vt_t, bkt = get_src(jj)
                    ps, bp, q0 = state.pop(n)
                    P, bP = Prot.get()
                    S.act(lambda e: e.activation(P[:, q0:T], ps[:, q0:T], AF.Exp), reads=[bp], writes=[bP])
                    if jj == j:
                        S.dve(lambda e: e.tensor_tensor(P[:, q0:q0 + 128], P[:, q0:q0 + 128], triub[:], ALU.mult),
                              reads=[Bc], writes=[bP])
                    for i in range(q0 // 128, 4):
                        S.pe(lambda e, i=i: e.matmul(psO[hl][:, i, 0:65], lhsT=P[:, i * 128:(i + 1) * 128],
                                                     rhs=vt_t[:, hl, kt, 0:65], start=first[hl], stop=False,
                                                     skip_group_check=True),
                             reads=[bP, bkt], writes=[BpsO[hl]])
                        first[hl] = False

                first = [True, True]
                emit_qk(0)
                for n in range(len(pairs)):
                    if n + 1 < len(pairs):
                        emit_qk(n + 1)
                    emit_rest(n)
                for hl in range(2):
                    hh = 2 * g + hl
                    r, br = rc.get()
                    S.dve(lambda e, hl=hl, r=r: e.reciprocal(r[:], psO[hl][:, :, 64]), reads=[BpsO[hl]], writes=[br])
                    for i in range(4):
                        S.dve(lambda e, hl=hl, hh=hh, i=i, r=r: e.tensor_scalar(
                            otok[:, i, hh * 64:(hh + 1) * 64], psO[hl][:, i, 0:64], r[:, i:i + 1], None, ALU.mult),
                              reads=[BpsO[hl], br], writes=[Botok[i]])

        def fox_o_transpose(N):
            nsub = (N + 127) // 128
            for fc in range(4):
                ps, bp = psg.get()
                for i in range(nsub):
                    n = min(128, N - i * 128)
                    S.pe(lambda e, i=i, n=n, fc=fc: e.transpose(ps[:, i * 128:i * 128 + n],
                                                                otok[0:n, i, fc * 128:(fc + 1) * 128],
                                                                ident[0:n, 0:n]),
                         reads=[Botok[i], Bc], writes=[bp])
                S.act(lambda e, fc=fc: e.activation(brT[:, fc, 0:N], ps[:, 0:N], AF.Copy), reads=[bp],
                      writes=[Bbr[fc]])

        def hgrn_branch(l, N, C, segs, S_in_fn, S_out_fn):
            nch = N // C
            wq, bq = wload("w_in", l, None, (C_HQ, C_HQ + 512), 8, 512)
            wf, bf_ = wload("w_in", l, None, (C_HF, C_HF + 512), 8, 512)
            wi, bi = wload("w_in", l, None, (C_HI, C_HI + 512), 8, 512)
            for ci in range(nch):
                ps, bp = psg.get()
                for k in range(8):
                    S.pe(lambda e, k=k, ci=ci: e.matmul(ps[0:C, :], lhsT=hT[:, k, ci * C:(ci + 1) * C],
                                                        rhs=wi[:, k, :], start=(k == 0), stop=(k == 7)),
                         reads=[bi, Bh[k]], writes=[bp])
                S.act(lambda e, ci=ci: e.activation(vtok[0:C, ci, :], ps[0:C, :], AF.Copy), reads=[bp],
                      writes=[Bvtok[ci]])
            wo_, bo = wload("w_in", l, None, (C_HO, C_HO + 512), 8, 512)
            mid = C // 2 - 1
            for hd in range(4):
                ps, bp = psg.get()
                mm_fm(ps, 128, N, wq, bq, hd * 128, 8, hT, Bh, extra_w=[bp])
                S.act(lambda e: e.activation(qs[:, 0:N], ps[:, 0:N], AF.Silu), reads=[bp], writes=[Bqs])
                ps2, bp2 = psg.get()
                mm_fm(ps2, 128, N, wf, bf_, hd * 128, 8, hT, Bh, extra_w=[bp2])
                sg, bsg = ft.get()
                S.act(lambda e: e.activation(sg[:, 0:N], ps2[:, 0:N], AF.Sigmoid), reads=[bp2], writes=[bsg])
                lg, blg = ft.get()
                S.act(lambda e, hd=hd: e.activation(lg[:, 0:N], sg[:, 0:N], AF.Ln, bias=lb[:, hd, l:l + 1],
                                                    scale=oml[:, hd, l:l + 1]), reads=[bsg, Bc], writes=[blg])
                S.dve(lambda e, hd=hd: e.tensor_scalar(kk[:, 0:N], sg[:, 0:N], noml[:, hd, l:l + 1],
                                                       oml[:, hd, l:l + 1], ALU.mult, ALU.add),
                      reads=[bsg, Bc], writes=[Bkk])
                for (c0, ncs, si) in segs:
                    S.dve(lambda e, c0=c0, ncs=ncs: e.tensor_tensor_scan(
                        bg[:, c0 * C:(c0 + ncs) * C], ones[:, 0:ncs * C], lg[:, c0 * C:(c0 + ncs) * C], 0.0,
                        ALU.mult, ALU.add), reads=[blg, Bc], writes=[Bbg])
                bg3 = bg[:, 0:N].rearrange("p (c s) -> p c s", s=C)
                dm, bdm = ft.get()
                dm3 = dm[:, 0:N].rearrange("p (c s) -> p c s", s=C)
                S.dve(lambda e: e.tensor_tensor(dm3, bg3, bg[:, mid:N:C].unsqueeze(2).to_broadcast([128, nch, C]),
                                                ALU.subtract), reads=[Bbg], writes=[bdm])
                e1, be1 = ft.get()
                S.act(lambda e: e.activation(e1[:, 0:N], dm[:, 0:N], AF.Exp), reads=[bdm], writes=[be1])
                S.dve(lambda e: e.tensor_tensor(qt[:, 0:N], qs[:, 0:N], e1[:, 0:N], ALU.mult), reads=[Bqs, be1],
                      writes=[Bqt])
                e2, be2 = ft.get()
                S.act(lambda e: e.activation(e2[:, 0:N], dm[:, 0:N], AF.Exp, scale=-1.0), reads=[bdm], writes=[be2])
                S.dve(lambda e: e.tensor_tensor(kt_[:, 0:N], kk[:, 0:N], e2[:, 0:N], ALU.mult), reads=[Bkk, be2],
                      writes=[Bkt])
                bst, bbst = ft.get()
                S.dve(lambda e: e.tensor_tensor(bst[:, 0:nch], bg[:, 0:N:C], lg[:, 0:N:C], ALU.subtract),
                      reads=[Bbg, blg], writes=[bbst])
                ds, bds = ft.get()
                ds3 = ds[:, 0:N].rearrange("p (c s) -> p c s", s=C)
                S.dve(lambda e: e.tensor_tensor(ds3, bg3, bst[:, 0:nch].unsqueeze(2).to_broadcast([128, nch, C]),
                                                ALU.subtract), reads=[bbst, Bbg], writes=[bds])
                e3, be3 = ft.get()
                S.act(lambda e: e.activation(e3[:, 0:N], ds[:, 0:N], AF.Exp), reads=[bds], writes=[be3])
                S.dve(lambda e: e.tensor_tensor(qh[:, 0:N], qs[:, 0:N], e3[:, 0:N], ALU.mult), reads=[Bqs, be3],
                      writes=[Bqh])
                dl, bdl = ft.get()
                dl3 = dl[:, 0:N].rearrange("p (c s) -> p c s", s=C)
                S.dve(lambda e: e.tensor_tensor(dl3, bg3, bg[:, C - 1:N:C].unsqueeze(2).to_broadcast([128, nch, C]),
                                                ALU.subtract), reads=[Bbg], writes=[bdl])
                e4, be4 = ft.get()
                S.act(lambda e: e.activation(e4[:, 0:N], dl[:, 0:N], AF.Exp, scale=-1.0), reads=[bdl], writes=[be4])
                S.dve(lambda e: e.tensor_tensor(kh[:, 0:N], kk[:, 0:N], e4[:, 0:N], ALU.mult), reads=[Bkk, be4],
                      writes=[Bkh])
                for ci in range(nch):
                    pst_, bpt = psg.get()
                    S.pe(lambda e, ci=ci: e.transpose(pst_[0:C, 0:128], kh[:, ci * C:(ci + 1) * C], ident[:]),
                         reads=[Bkh, Bc], writes=[bpt])
                    S.act(lambda e, ci=ci: e.activation(khtok[0:C, ci, :], pst_[0:C, 0:128], AF.Copy), reads=[bpt],
                          writes=[Bkhtok[ci]])
                psA, bpA = psg.get()
                for ci in range(nch):
                    S.pe(lambda e, ci=ci: e.matmul(psA[0:C, ci * C:(ci + 1) * C], lhsT=kt_[:, ci * C:(ci + 1) * C],
                                                   rhs=qt[:, ci * C:(ci + 1) * C], start=True, stop=True),
                         reads=[Bkt, Bqt], writes=[bpA])
                S.dve(lambda e: e.tensor_tensor(Abf[0:C, 0:nch, 0:C],
                                                psA[0:C, 0:nch * C].rearrange("p (c s) -> p c s", s=C),
                                                triu[0:C, 0:C].unsqueeze(1).to_broadcast([C, nch, C]), ALU.mult),
                      reads=[bpA, Bc], writes=[BAbf])
                pso, bpo = psS.get()
                for (c0, ncs, si) in segs:
                    Sap, BSt = S_in_fn(hd, si)
                    sbf, bsbf = Sbf.get()
                    S.act(lambda e, Sap=Sap, sbf=sbf: e.activation(sbf[:], Sap, AF.Copy), reads=[BSt], writes=[bsbf])
                    for ci in range(c0, c0 + ncs):
                        S.pe(lambda e, ci=ci, hd=hd: e.matmul(pso[:, ci * C:(ci + 1) * C],
                                                              lhsT=vtok[0:C, ci, hd * 128:(hd + 1) * 128],
                                                              rhs=Abf[0:C, ci, 0:C], start=True, stop=False),
                             reads=[Bvtok[ci], BAbf], writes=[bpo])
                        S.pe(lambda e, ci=ci, sbf=sbf: e.matmul(pso[:, ci * C:(ci + 1) * C], lhsT=sbf[:],
                                                                rhs=qh[:, ci * C:(ci + 1) * C], start=False, stop=True),
                             reads=[bsbf, Bqh], writes=[bpo])
                        psd, bpd = psg.get()
                        S.pe(lambda e, ci=ci, hd=hd, psd=psd: e.matmul(psd[:, 0:128], lhsT=khtok[0:C, ci, :],
                                                                       rhs=vtok[0:C, ci, hd * 128:(hd + 1) * 128],
                                                                       start=True, stop=True),
                             reads=[Bkhtok[ci], Bvtok[ci]], writes=[bpd])
                        S.dve(lambda e, ci=ci, Sap=Sap, psd=psd, e3=e3: e.scalar_tensor_tensor(
                            Sap, Sap, e3[:, (ci + 1) * C - 1:(ci + 1) * C], psd[:, 0:128], ALU.mult, ALU.add),
                              reads=[bpd, be3], writes=[BSt])
                        if ci < c0 + ncs - 1:
                            sbf, bsbf = Sbf.get()
                            S.act(lambda e, Sap=Sap, sbf=sbf: e.activation(sbf[:], Sap, AF.Copy), reads=[BSt],
                                  writes=[bsbf])
                    S_out_fn(hd, si, Sap, BSt)
                S.act(lambda e: e.activation(sqo[:, 0:N], pso[:, 0:N], AF.Square), reads=[bpo], writes=[Bsqo])
                psm, bpm = psg.get()
                S.pe(lambda e: e.matmul(psm[:, 0:N], lhsT=onesV[:], rhs=sqo[:, 0:N], start=True, stop=True),
                     reads=[Bsqo, Bc], writes=[bpm])
                rs, brs = ft.get()
                rsqrt_eps(rs[:, 0:N], psm[:, 0:N], [bpm], brs)
                on, bon = ft.get()
                S.dve(lambda e: e.scalar_tensor_tensor(on[:, 0:N], pso[:, 0:N], hng[:, l:l + 1], rs[:, 0:N], ALU.mult,
                                                       ALU.mult), reads=[bpo, brs, Bc], writes=[bon])
                psq, bpq = psg.get()
                mm_fm(psq, 128, N, wo_, bo, hd * 128, 8, hT, Bh, extra_w=[bpq])
                go, bgo = ft.get()
                S.act(lambda e: e.activation(go[:, 0:N], psq[:, 0:N], AF.Silu), reads=[bpq], writes=[bgo])
                S.dve(lambda e, hd=hd: e.tensor_tensor(brT[:, hd, 0:N], on[:, 0:N], go[:, 0:N], ALU.mult),
                      reads=[bon, bgo], writes=[Bbr[hd]])

        init_consts()
        precast_weights()

        import os as _os
        DBG = _os.environ.get("KDBG", "full")
        DBG2 = _os.environ.get("KDBG2", "")

        def prompt_tile(j):
            N = T
            tok0 = j * T
            load_x_tile(xp[tok0:tok0 + T, :], N)
            for l in range(L if DBG != "load" else 0):
                rmsnorm(g1, l, N)
                def segs(l_, c, psv, bpv, sg, bsg):
                    S.act(lambda e: e.activation(ubuf[:, c, 0:30], hist[:, l_, c, :], AF.Copy), reads=[Bhist[l_]],
                          writes=[Bub[c]])
                    S.dve(lambda e: e.tensor_tensor(ubuf[:, c, 30:30 + N], psv[:, 0:N], sg[:, 0:N], ALU.mult),
                          reads=[bpv, bsg], writes=[Bub[c]])
                    conv_taps(l_, c, 0, 0, N)
                    S.act(lambda e: e.activation(hist[:, l_, c, :], ubuf[:, c, N:N + 30], AF.Copy), reads=[Bub[c]],
                          writes=[Bhist[l_]])
                if DBG == "norm":
                    continue
                conv_branch(l, N, segs)
                conv_post(l, N)
                branch_out(l, "w_a_out", C_GA, N, True, False)
                if DBG == "conv":
                    continue
                if j == NT - 1:
                    for c in range(4):
                        S.dma(PQ, osem, cp[l][:, c * 128:(c + 1) * 128].rearrange("j p -> p j"), hist[:, l, c, :],
                              reads=[Bhist[l]])
                wK, bK, wV, bV = fox_qkv(l, N, tok0, None, None, None)
                if DBG == "fq1":
                    continue
                fox_scan_rows(l, N, 0, N, ccar[:, l:l + 1], [Bccar[l]])
                S.dve(lambda e: e.tensor_copy(ccar[:, l:l + 1], csp[:, N - 1:N]), reads=[Bf], writes=[Bccar[l]])
                if DBG == "fq2":
                    continue
                fox_split_rows(N)
                if DBG == "fq3":
                    continue

                def va_fn(i, n, stt, bst_):
                    S.dve(lambda e: e.tensor_copy(Va[0:n, :, i, 0:64],
                                                  stt[0:n, 512:1024].rearrange("p (h d) -> p h d", d=64)),
                          reads=[bst_], writes=[BV])

                def kdst(i, n, stt, bst_):
                    S.dma(PQ, None, kp[l, tok0 + i * 128: tok0 + i * 128 + n, :], stt[0:n, 0:512],
                          reads=[bst_])

                def vdst(i, n, stt, bst_):
                    S.dma(PQ, None, vp[l, tok0 + i * 128: tok0 + i * 128 + n, :], stt[0:n, 512:1024],
                          reads=[bst_])

                def lfdst(i, n):
                    if i == NSUB - 1:
                        S.dma(PQ, lfsem, lfp[l, tok0:tok0 + N, :].rearrange("(s p) h -> p s h", p=128), lftok[:],
                              reads=[Blftok])
                fox_tok_outputs(l, N, wK, bK, wV, bV, kdst, vdst, lfdst, va_fn)
                if DBG == "fq4":
                    continue
                S.dma(PQ, ksem_, kscr[l, :, :, tok0:tok0 + N], Ka[:], reads=BK, writes=[Bkscr[l]])
                S.dma(PQ, vsem_, vscr[l, :, :, j * NSUB:(j + 1) * NSUB, :], Va[:], reads=[BV], writes=[Bvscr[l]])
                if DBG == "foxqkv":
                    continue
                fox_attend_prompt(l, j)
                fox_o_transpose(N)
                branch_out(l, "w_b_out", C_GB, N, False, False)
                if DBG == "fox":
                    continue
                def S_in(hd, si):
                    return Sst[:, l, hd, :], BS[l][hd]

                def S_out(hd, si, Sap, BSt):
                    if j == NT - 1:
                        S.dma(PQ, osem, hp[l, hd], Sap, reads=[BSt])
                hgrn_branch(l, N, 64, [(0, NCH, 0)], S_in, S_out)
                branch_out(l, "w_c_out", C_GC, N, False, True)
                if DBG == "hgrn":
                    continue
                ffn_and_wo(l, N)
            store_y_tile(yp[tok0:tok0 + T, :], N)


        def sample_tile():
            N = NSS * LS
            for i_ in range(2):
                kct = ksrc.tiles[i_][:].rearrange("p a (b k) -> p (a b) k", k=128)
                vct = vsrc.tiles[i_][:].rearrange("p a b w -> p (a b) w")
                S.pool(lambda e, kct=kct: e.memset(kct[64:68, :, :], 0.0), writes=[ksrc.bufs[i_]])
                S.pool(lambda e, kct=kct: e.memset(kct[64:66, :, :], 1.0), writes=[ksrc.bufs[i_]])
                S.pool(lambda e, vct=vct: e.memset(vct[:, :, 64:VW], 1.0), writes=[ksrc.bufs[i_]])
            load_x_tile(xs, N)
            ub4 = ubuf[:, :, 0:4 * 46].rearrange("p c (s w) -> p c s w", w=46)
            for l in range(L):
                rmsnorm(g1, l, N)
                for s_ in range(NSS):
                    stt, bst_ = stg.get()
                    S.dma(PQ, None, stt[0:30, 0:512], sconv[l, s_], writes=[bst_])
                    ps, bp = psg.get()
                    for c in range(4):
                        S.pe(lambda e, c=c: e.transpose(ps[:, c * 32:c * 32 + 30], stt[0:30, c * 128:(c + 1) * 128],
                                                        ident[0:30, 0:30]), reads=[bst_, Bc], writes=[bp])
                    S.act(lambda e, s_=s_: e.activation(ub4[:, :, s_, 0:30],
                                                        ps[:, 0:128].rearrange("p (c w) -> p c w", w=32)[:, :, 0:30],
                                                        AF.Copy), reads=[bp], writes=Bub)

                def segs(l_, c, psv, bpv, sg, bsg):
                    S.dve(lambda e: e.tensor_tensor(ub4[:, c, :, 30:46],
                                                    psv[:, 0:N].rearrange("p (s t) -> p s t", t=LS),
                                                    sg[:, 0:N].rearrange("p (s t) -> p s t", t=LS), ALU.mult),
                          reads=[bpv, bsg], writes=[Bub[c]])
                    y4 = yc[:, c, 0:N].rearrange("p (s t) -> p s t", t=LS)
                    S.dve(lambda e: e.tensor_scalar(y4, ub4[:, c, :, 0:LS], cw[:, l_, c, 0:1], cb[:, l_, c:c + 1],
                                                    ALU.mult, ALU.add), reads=[Bub[c], Bc], writes=[Byc[c]])
                    for j in range(1, 31):
                        S.dve(lambda e, j=j: e.scalar_tensor_tensor(y4, ub4[:, c, :, j:j + LS], cw[:, l_, c, j:j + 1],
                                                                    y4, ALU.mult, ALU.add),
                              reads=[Bub[c]], writes=[Byc[c]])
                conv_branch(l, N, segs)
                for s_ in range(NSS):
                    ps, bp = psg.get()
                    for c in range(4):
                        S.pe(lambda e, c=c, s_=s_: e.transpose(ps[0:30, c * 128:(c + 1) * 128], ub4[:, c, s_, 16:46],
                                                               ident[:]), reads=[Bub[c], Bc], writes=[bp])
                    stt, bst_ = stg.get()
                    S.act(lambda e: e.activation(stt[0:30, 0:512], ps[0:30, :], AF.Copy), reads=[bp], writes=[bst_])
                    S.dma(PQ, None, cs[l, s_], stt[0:30, 0:512], reads=[bst_])
                conv_post(l, N)
                branch_out(l, "w_a_out", C_GA, N, True, False)
                wK, bK, wV, bV = fox_qkv(l, N, 0, None, None, None)
                for s_ in range(NSS):
                    fox_scan_rows(l, N, s_ * LS, LS, 0.0, [])
                fox_split_rows(N)

                def va_fn(i, n, stt, bst_):
                    S.dve(lambda e: e.tensor_copy(Va[0:n, :, i, 0:64],
                                                  stt[0:n, 512:1024].rearrange("p (h d) -> p h d", d=64)),
                          reads=[bst_], writes=[BV])

                def kdst(i, n, stt, bst_):
                    S.dma(PQ, None, ks[l].rearrange("s t c -> (s t) c"), stt[0:n, 0:512], reads=[bst_])

                def vdst(i, n, stt, bst_):
                    S.dma(PQ, None, vs[l].rearrange("s t c -> (s t) c"), stt[0:n, 512:1024], reads=[bst_])

                def lfdst(i, n):
                    S.dma(PQ, lfsem, lfs[l].rearrange("s t h -> (s t) h"), lftok[0:n, 0, :], reads=[Blftok])
                fox_tok_outputs(l, N, wK, bK, wV, bV, kdst, vdst, lfdst, va_fn)
                fox_attend_sample(l)
                for fc in range(4):
                    ps, bp = psg.get()
                    for s_ in range(NSS):
                        S.pe(lambda e, s_=s_, fc=fc: e.transpose(ps[:, s_ * LS:(s_ + 1) * LS],
                                                                 otok[0:LS, s_, fc * 128:(fc + 1) * 128],
                                                                 ident[0:LS, 0:LS]),
                             reads=[Botok[s_], Bc], writes=[bp])
                    S.act(lambda e, fc=fc: e.activation(brT[:, fc, 0:N], ps[:, 0:N], AF.Copy), reads=[bp],
                          writes=[Bbr[fc]])
                branch_out(l, "w_b_out", C_GB, N, False, False)
                def S_in(hd, si):
                    t_, b_ = Ssm.get()
                    S.dma(PQ, None, t_[:], shg[l, si, hd], writes=[b_])
                    return t_[:], b_

                def S_out(hd, si, Sap, BSt):
                    S.dma(PQ, None, hs[l, si, hd], Sap, reads=[BSt])
                hgrn_branch(l, N, LS, [(s_, 1, s_) for s_ in range(NSS)], S_in, S_out)
                branch_out(l, "w_c_out", C_GC, N, False, True)
                ffn_and_wo(l, N)
            store_y_tile(ys, N)

        def fox_attend_sample(l):
            N = NSS * LS
            psn, bpn = psS.get()
            for hh in range(8):
                S.pe(lambda e, hh=hh: e.matmul(psn[0:N, hh * N:(hh + 1) * N], lhsT=Ka[:, hh, 0:N], rhs=Qa[:, hh, 0:N],
                                               start=True, stop=True), reads=[BK[hh], BQ[hh]], writes=[bpn])
            Pn, bPn = sqo, Bsqo
            S.act(lambda e: e.activation(Pn[0:N, :], psn[0:N, :], AF.Exp), reads=[bpn], writes=[bPn])
            S.dve(lambda e: e.tensor_tensor(Pn[0:N, :].rearrange("p (h q) -> p h q", q=N),
                                            Pn[0:N, :].rearrange("p (h q) -> p h q", q=N),
                                            bmask[:, :].unsqueeze(1).to_broadcast([N, 8, N]), ALU.mult),
                  reads=[Bc], writes=[bPn])
            for s_ in range(NSS):
                S.dma(PQ, None, lfc[:], clf[l, s_].rearrange("(kt p) h -> p kt h", p=128), writes=[Brk])
                lfc2 = lfc[:].rearrange("p k h -> p (k h)")
                ps1, bp1 = psg.get()
                S.pe(lambda e: e.matmul(ps1[:, 0:128], lhsT=slm[:], rhs=lfc2, start=True, stop=True),
                     reads=[Brk, Bc], writes=[bp1])
                ps2, bp2 = psg.get()
                S.pe(lambda e: e.matmul(ps2[:, 0:128], lhsT=ones[:, 0:128], rhs=lfc2, start=True, stop=True),
                     reads=[Brk, Bc], writes=[bp2])
                S.dve(lambda e: e.tensor_copy(rkt[:].rearrange("p k h -> p (k h)"), ps2[:, 0:128]), reads=[bp2],
                      writes=[Brk])
                S.dve(lambda e: e.tensor_copy(rk[:].rearrange("p k h -> p (k h)"), ps1[:, 0:128]), reads=[bp1],
                      writes=[Brk])
                S.dve(lambda e: e.tensor_copy(lfc[:, 15, :], rkt[:, 15, :]), writes=[Brk])
                S.dve(lambda e: e.memset(rkt[:, 15, :], 0.0), writes=[Brk])
                for kt in range(14, -1, -1):
                    S.dve(lambda e, kt=kt: e.tensor_copy(lfc[:, kt, :], rkt[:, kt, :]), writes=[Brk])
                    S.dve(lambda e, kt=kt: e.tensor_tensor(rkt[:, kt, :], rkt[:, kt + 1, :], lfc[:, kt + 1, :], ALU.add),
                          writes=[Brk])
                S.dve(lambda e: e.tensor_tensor(rk[:], rk[:], rkt[:], ALU.add), writes=[Brk])
                first = [True, True]
                for kt in range(NPAST // 128):
                    stt, bst_ = stg.get()
                    S.dma_group(PQ, None if False else S.auto_sem((), [bst_]),
                                [(stt[:, 0:512], ck[l, s_, kt * 128:(kt + 1) * 128, :]),
                                 (stt[:, 512:1024], cv[l, s_, kt * 128:(kt + 1) * 128, :])], writes=[bst_])
                    kct_t, bkc = ksrc.get()
                    vct_t, _u = vsrc.get()
                    kct = kct_t[:].rearrange("p a (b k) -> p (a b) k", k=128)
                    vct = vct_t[:].rearrange("p a b w -> p (a b) w")
                    for half in range(2):
                        pst_, bpt = psg.get()
                        for hq in range(4):
                            hh = half * 4 + hq
                            S.pe(lambda e, hh=hh, hq=hq, pst_=pst_, stt=stt: e.transpose(
                                pst_[0:64, hq * 128:(hq + 1) * 128], stt[:, hh * 64:(hh + 1) * 64], ident[:]),
                                 reads=[bst_, Bc], writes=[bpt])
                        S.act(lambda e, half=half, pst_=pst_, kct=kct: e.activation(
                            kct[0:64, half * 4:half * 4 + 4, :], pst_[0:64, :].rearrange("p (h k) -> p h k", k=128),
                            AF.Copy), reads=[bpt], writes=[bkc])
                    S.dve(lambda e, vct=vct, stt=stt: e.tensor_copy(
                        vct[:, :, 0:64], stt[:, 512:1024].rearrange("p (h d) -> p h d", d=64)),
                          reads=[bst_], writes=[bkc])
                    pss, bps = psS.get()
                    for hh in range(8):
                        S.pe(lambda e, hh=hh, pss=pss, kct=kct: e.matmul(
                            pss[:, hh * LS:(hh + 1) * LS], lhsT=kct[:, hh, :], rhs=Qa[:, hh, s_ * LS:(s_ + 1) * LS],
                            start=True, stop=True), reads=[bkc, BQ[hh]], writes=[bps])
                    Pt, bPt = Prot.get()
                    Pf, bPf = ft.get()
                    S.dve(lambda e, kt=kt, pss=pss, Pf=Pf: e.tensor_tensor(
                        Pf[:, 0:128].rearrange("p (h q) -> p h q", q=LS),
                        pss[:, 0:128].rearrange("p (h q) -> p h q", q=LS),
                        rk[:, kt, :].unsqueeze(2).to_broadcast([128, 8, LS]), ALU.add),
                          reads=[bps, Brk], writes=[bPf])
                    S.act(lambda e, Pt=Pt, Pf=Pf: e.activation(Pt[:, 0:128], Pf[:, 0:128], AF.Exp), reads=[bPf],
                          writes=[bPt])
                    for hh in range(8):
                        b_ = hh // 4
                        S.pe(lambda e, hh=hh, b_=b_, Pt=Pt, vct=vct, f=first[b_]: e.matmul(
                            psO[b_][0:LS, hh % 4, 0:65], lhsT=Pt[:, hh * LS:(hh + 1) * LS], rhs=vct[:, hh, 0:65],
                            start=f, stop=False, skip_group_check=True), reads=[bPt, bkc], writes=[BpsO[b_]])
                        first[b_] = False
                for hh in range(8):
                    b_ = hh // 4
                    S.pe(lambda e, hh=hh, b_=b_: e.matmul(
                        psO[b_][0:LS, hh % 4, 0:65], lhsT=Pn[0:N, hh * N + s_ * LS: hh * N + (s_ + 1) * LS],
                        rhs=Va[0:N, hh, 0, 0:65], start=False, stop=True, skip_group_check=True),
                         reads=[bPn, BV], writes=[BpsO[b_]])
                for b_ in range(2):
                    r, br = rc.get()
                    S.dve(lambda e, b_=b_, r=r: e.reciprocal(r[0:LS, :], psO[b_][0:LS, :, 64]), reads=[BpsO[b_]],
                          writes=[br])
                    for hq in range(4):
                        hh = b_ * 4 + hq
                        S.dve(lambda e, b_=b_, hq=hq, hh=hh, r=r: e.tensor_scalar(
                            otok[0:LS, s_, hh * 64:(hh + 1) * 64], psO[b_][0:LS, hq, 0:64], r[0:LS, hq:hq + 1], None,
                            ALU.mult), reads=[BpsO[b_], br], writes=[Botok[s_]])

        if PROMPT and DBG != "init":
            for j in range(NT):
                prompt_tile(j)

        if SAMPLE:
            sample_tile()

        with nc.allow_non_contiguous_dma(reason="small strided param/state transfers"):
            S.emit()
    return nc, S


_CACHE = {}


def kernel(**inputs):
    inp = {k: np.ascontiguousarray(np.asarray(v, dtype=np.float32)) for k, v in inputs.items()}
    L = inp["w_in"].shape[0]
    B, SEQ, _ = inp["x_prompt"].shape
    key = (SEQ, L)
    if key not in _CACHE:
        _CACHE[key] = build_program(SEQ=SEQ, L=L)[0]
    nc = _CACHE[key]
    ncores = 8
    in_maps = []
    for c in range(ncores):
        m = {
            "xp": inp["x_prompt"][c % B],
            "xs": inp["x_sample"][4 * c:4 * c + 4].reshape(NSS * LS, D),
            "ck": inp["cache_fox_k"][:, 4 * c:4 * c + 4].reshape(L, NSS, NPAST, 512),
            "cv": inp["cache_fox_v"][:, 4 * c:4 * c + 4].reshape(L, NSS, NPAST, 512),
            "clf": inp["cache_fox_logf"][:, 4 * c:4 * c + 4],
            "sconv": inp["state_conv"][:, 4 * c:4 * c + 4],
            "shg": inp["state_hgrn"][:, 4 * c:4 * c + 4],
        }
        for n in WNAMES:
            m[n] = inp[n] if n != "final_g" else inp[n].reshape(1, D)
        in_maps.append({k: np.ascontiguousarray(v) for k, v in m.items()})
    res = run_bass_kernel_spmd(nc, in_maps, core_ids=list(range(ncores)))
    R = res.results
    y_prompt = np.stack([R[b]["yp"] for b in range(B)])
    y_sample = np.concatenate([R[c]["ys"].reshape(NSS, LS, D) for c in range(ncores)], axis=0)
    kpo = np.stack([R[b]["kp"] for b in range(B)], axis=1).reshape(L, B, SEQ, 8, 64)
    vpo = np.stack([R[b]["vp"] for b in range(B)], axis=1).reshape(L, B, SEQ, 8, 64)
    lfpo = np.stack([R[b]["lfp"] for b in range(B)], axis=1)
    cpo = np.stack([R[b]["cp"] for b in range(B)], axis=1)
    hpo = np.stack([R[b]["hp"] for b in range(B)], axis=1)
    kso = np.concatenate([R[c]["ks"] for c in range(ncores)], axis=1).reshape(L, 4 * ncores, LS, 8, 64)
    vso = np.concatenate([R[c]["vs"] for c in range(ncores)], axis=1).reshape(L, 4 * ncores, LS, 8, 64)
    lfso = np.concatenate([R[c]["lfs"] for c in range(ncores)], axis=1)
    cso = np.concatenate([R[c]["cs"] for c in range(ncores)], axis=1)
    hso = np.concatenate([R[c]["hs"] for c in range(ncores)], axis=1)
    return (y_prompt, y_sample, kpo, vpo, lfpo, cpo, hpo, kso, vso, lfso, cso, hso)
```

```python
import contextlib
import numpy as np
import concourse.bass as bass
import concourse.mybir as mybir
from concourse.bass_utils import run_bass_kernel_spmd

F32 = mybir.dt.float32
BF16 = mybir.dt.bfloat16
ALU = mybir.AluOpType
AF = mybir.ActivationFunctionType
AX = mybir.AxisListType

ENGS = ("pe", "act", "dve", "pool", "sp")
SAME_ENGINE_SYNC = {"pool", "dve", "act"}
EPS = 1e-6


class Buf:
    __slots__ = ("w", "r", "name", "dw", "dr")

    def __init__(self, name=""):
        self.w = None
        self.r = {}
        self.name = name
        self.dw = None
        self.dr = None


class DmaSem:
    def __init__(self, handle):
        self.h = handle
        self.n = 0


class _Rec:
    def __init__(self):
        self.call = None

    def __getattr__(self, name):
        def f(*a, **k):
            self.call = (name, a, k)
            return self
        return f


class Sched:
    def __init__(self, nc, stack):
        self.nc = nc
        self.stack = stack
        self.ops = {e: [] for e in ENGS}
        self.count = {e: 0 for e in ENGS}
        self.seen = {e: {} for e in ENGS}
        self.esem = {e: stack.enter_context(nc.semaphore("es_" + e)) for e in ENGS}
        self.dsems = []
        self.nwaits = 0

    def dma_sem(self, name):
        d = DmaSem(self.stack.enter_context(self.nc.semaphore(name)))
        self.dsems.append(d)
        return d

    def op(self, eng, fn, reads=(), writes=(), dsem=None, raw=()):
        waits = {}
        seen = self.seen[eng]

        def need(k, v):
            if dsem is None and k == eng and eng not in SAME_ENGINE_SYNC:
                return
            if seen.get(k, 0) >= v:
                return
            if waits.get(k, 0) < v:
                waits[k] = v

        for b in reads:
            if b.w is not None:
                need(*b.w)
        for b in raw:
            if b.w is not None:
                need(*b.w)
        for b in writes:
            if b.w is not None:
                need(*b.w)
            for k, v in b.r.items():
                need(k, v)
        for k, v in waits.items():
            seen[k] = v
        if dsem is None:
            self.count[eng] += 1
            ev = (eng, self.count[eng])
        else:
            dsem.n += 16
            ev = (dsem, dsem.n)
        for b in reads:
            if b.r.get(ev[0], 0) < ev[1]:
                b.r[ev[0]] = ev[1]
        for b in writes:
            b.w = ev
            b.r = {}
        self.nwaits += len(waits)
        rec = _Rec()
        fn(rec)
        self.ops[eng].append((list(waits.items()), rec.call, ev))

    def pe(self, fn, reads=(), writes=()):
        self.op("pe", fn, reads, writes)

    def act(self, fn, reads=(), writes=()):
        self.op("act", fn, reads, writes)

    def dve(self, fn, reads=(), writes=()):
        self.op("dve", fn, reads, writes)

    def pool(self, fn, reads=(), writes=()):
        self.op("pool", fn, reads, writes)

    def auto_sem(self, reads, writes):
        if writes:
            b = writes[0]
            if b.dw is None:
                b.dw = self.dma_sem("dw%d" % len(self.dsems))
            return b.dw
        b = reads[0]
        if b.dr is None:
            b.dr = self.dma_sem("dr%d" % len(self.dsems))
        return b.dr

    def dma(self, q, dsem, out, in_, reads=(), writes=(), raw=(), **kw):
        if dsem is None:
            dsem = self.auto_sem(reads, writes)
        self.op(q, lambda e: e.dma_start(out=out, in_=in_, **kw), reads, writes, dsem=dsem, raw=raw)

    def dma_group(self, q, dsem, pairs, reads=(), writes=(), raw=(), **kw):
        for i, (out, in_) in enumerate(pairs):
            if i == 0:
                self.op(q, lambda e, out=out, in_=in_: e.dma_start(out=out, in_=in_, **kw), reads, writes, dsem=dsem,
                        raw=raw)
            else:
                self.op(q, lambda e, out=out, in_=in_: e.dma_start(out=out, in_=in_, **kw), (), (), dsem=dsem)
        ev = (dsem, dsem.n)
        for b in reads:
            b.r[dsem] = dsem.n
        for b in writes:
            b.w = ev
            b.r = {}

    def emit(self):
        nc = self.nc
        fin = []
        for e in ENGS:
            if self.count[e]:
                fin.append((e, self.count[e]))
        for d in self.dsems:
            if d.n:
                fin.append((d, d.n))

        def semh(k):
            return self.esem[k] if isinstance(k, str) else k.h

        def run(engname, engobj):
            for waits, fn, ev in self.ops[engname]:
                for k, v in waits:
                    engobj.wait_ge(semh(k), v)
                ins = getattr(engobj, fn[0])(*fn[1], **fn[2])
                if isinstance(ev[0], str):
                    ins.then_inc(self.esem[ev[0]], 1)
                else:
                    ins.then_inc(ev[0].h, 16)
            if engname == "sp":
                for k, v in fin:
                    engobj.wait_ge(semh(k), v)

        with nc.Block() as block:
            @block.tensor
            def _(e):
                run("pe", e)

            @block.scalar
            def _(e):
                run("act", e)

            @block.vector
            def _(e):
                run("dve", e)

            @block.gpsimd
            def _(e):
                run("pool", e)

            @block.sync
            def _(e):
                run("sp", e)


class Rot:
    def __init__(self, tiles):
        self.tiles = tiles
        self.bufs = [Buf() for _ in tiles]
        self.i = 0

    def get(self):
        k = self.i % len(self.tiles)
        self.i += 1
        return self.tiles[k], self.bufs[k]


D = 1024
INW = 7688
NPAST = 2048
NSS = 4
LS = 16
VW = 68
C_AV, C_AG, C_Q, C_K, C_V, C_F = 0, 512, 1024, 1536, 2048, 2560
C_HQ, C_HF, C_HI, C_HO = 2568, 3080, 3592, 4104
C_GA, C_GB, C_GC = 4616, 5640, 6664

WNAMES = ["norm1_g", "w_in", "conv_w", "conv_b", "conv_ln_g", "conv_ln_b", "w_a_out", "fox_bf", "w_b_out",
          "hgrn_lb_param", "hgrn_norm_g", "w_c_out", "w_o", "norm2_g", "w_up", "w_down", "final_g"]


def build_program(SEQ=8192, L=4, T=512, SAMPLE=True, PROMPT=True):
    nc = bass.Bass("TRN2", target_bir_lowering=False)
    NT = SEQ // T
    NSUB = T // 128
    NCH = T // 64

    def din(name, shape):
        return nc.dram_tensor(name, shape, F32, kind="ExternalInput").ap()

    def dout(name, shape):
        return nc.dram_tensor(name, shape, F32, kind="ExternalOutput").ap()

    xp = din("xp", [SEQ, D])
    xs = din("xs", [NSS * LS, D])
    ck = din("ck", [L, NSS, NPAST, 512])
    cv = din("cv", [L, NSS, NPAST, 512])
    clf = din("clf", [L, NSS, NPAST, 8])
    sconv = din("sconv", [L, NSS, 30, 512])
    shg = din("shg", [L, NSS, 4, 128, 128])
    Wd = {
        "norm1_g": din("norm1_g", [L, D]), "w_in": din("w_in", [L, D, INW]), "conv_w": din("conv_w", [L, 31, 512]),
        "conv_b": din("conv_b", [L, 512]), "conv_ln_g": din("conv_ln_g", [L, 512]),
        "conv_ln_b": din("conv_ln_b", [L, 512]), "w_a_out": din("w_a_out", [L, 512, D]),
        "fox_bf": din("fox_bf", [L, 8]), "w_b_out": din("w_b_out", [L, 512, D]),
        "hgrn_lb_param": din("hgrn_lb_param", [L, 512]), "hgrn_norm_g": din("hgrn_norm_g", [L, 128]),
        "w_c_out": din("w_c_out", [L, 512, D]), "w_o": din("w_o", [L, D, D]), "norm2_g": din("norm2_g", [L, D]),
        "w_up": din("w_up", [L, D, 4096]), "w_down": din("w_down", [L, 4096, D]), "final_g": din("final_g", [1, D]),
    }
    yp = dout("yp", [SEQ, D])
    ys = dout("ys", [NSS * LS, D])
    kp = dout("kp", [L, SEQ, 512])
    vp = dout("vp", [L, SEQ, 512])
    lfp = dout("lfp", [L, SEQ, 8])
    cp = dout("cp", [L, 30, 512])
    hp = dout("hp", [L, 4, 128, 128])
    ks = dout("ks", [L, NSS, LS, 512])
    vs = dout("vs", [L, NSS, LS, 512])
    lfs = dout("lfs", [L, NSS, LS, 8])
    cs = dout("cs", [L, NSS, 30, 512])
    hs = dout("hs", [L, NSS, 4, 128, 128])
    kscr = nc.dram_tensor("kscr", [L, 68, 8, SEQ], BF16, kind="Internal").ap()
    vscr = nc.dram_tensor("vscr", [L, 128, 8, SEQ // 128, VW], BF16, kind="Internal").ap()

    with contextlib.ExitStack() as st:
        S = Sched(nc, st)
        _n = [0]

        def sb(shape, dt, name=None):
            _n[0] += 1
            return st.enter_context(nc.sbuf_tensor(name or ("t%d" % _n[0]), shape, dt))

        def pst(shape, dt, name=None):
            _n[0] += 1
            return st.enter_context(nc.psum_tensor(name or ("p%d" % _n[0]), shape, dt))

        xT = sb([128, 8, T], F32, "xT")
        Bx = [Buf() for _ in range(8)]
        hT = sb([128, 8, T], BF16, "hT")
        Bh = [Buf() for _ in range(8)]
        sq = sb([128, 8, T], BF16, "sq")
        Bsq = [Buf() for _ in range(8)]
        mT = sb([128, 8, T], F32, "mT")
        Bm = [Buf() for _ in range(8)]
        brT = sb([128, 4, T], BF16, "brT")
        Bbr = [Buf() for _ in range(4)]
        stg = Rot([sb([128, 1024], F32, "stage%d" % i) for i in range(2)])
        NW = 4
        WSZ = 8 * 520
        wrot = Rot([sb([128, WSZ], BF16, "w%d" % i) for i in range(NW)])
        wsem = [S.dma_sem("wsem%d" % i) for i in range(NW)]
        ft = Rot([sb([128, T], F32, "ft%d" % i) for i in range(8)])
        psg = Rot([pst([128, 512], F32, "psg%d" % i) for i in range(4)])
        psS = Rot([pst([128, 512], F32, "psS%d" % i) for i in range(2)])
        psO = [pst([128, 4, 128], F32, "psO%d" % i) for i in range(2)]
        BpsO = [Buf(), Buf()]
        ident = sb([128, 128], F32, "ident")
        triu = sb([128, 128], F32, "triu")
        triub = sb([128, 128], BF16, "triub")
        onesD = sb([128, 128], BF16, "onesD")
        onesC = sb([128, 128], BF16, "onesC")
        onesV = sb([128, 128], BF16, "onesV")
        ones = sb([128, T], F32, "ones")
        g1 = sb([128, L, 8], F32, "g1")
        g2 = sb([128, L, 8], F32, "g2")
        gf = sb([128, 8], F32, "gf")
        cw = sb([128, L, 4, 31], F32, "cw")
        cb = sb([128, L, 4], F32, "cb")
        lng = sb([128, L, 4], F32, "lng")
        lnb = sb([128, L, 4], F32, "lnb")
        nbf = sb([8, L], F32, "nbf")
        bfb = sb([128, L, 8], F32, "bfb")
        lbp = sb([128, 4, L], F32, "lbp")
        lb = sb([128, 4, L], F32, "lb")
        oml = sb([128, 4, L], F32, "oml")
        noml = sb([128, 4, L], F32, "noml")
        lbt = sb([128, 4, L], F32, "lbt")
        lbm = sb([128, 4], F32, "lbm")
        hng = sb([128, L], F32, "hng")
        epsb = sb([128, 1], F32, "epsb")
        Bc = Buf()
        csem = S.dma_sem("csem")
        hist = sb([128, L, 4, 30], F32, "hist")
        Bhist = [Buf() for _ in range(L)]
        Sst = sb([128, L, 4, 128], F32, "Sst")
        BS = [[Buf() for _ in range(4)] for _ in range(L)]
        Sbf = Rot([sb([128, 128], BF16, "Sbf%d" % i) for i in range(2)])
        ccar = sb([8, L], F32, "ccar")
        Bccar = [Buf() for _ in range(L)]
        ubuf = sb([128, 4, 30 + T], F32, "ubuf")
        Bub = [Buf() for _ in range(4)]
        yc = sb([128, 4, T], F32, "yc")
        Byc = [Buf() for _ in range(4)]
        ybf = sq[:, 0:4, :]
        ysq = sq[:, 4:8, :]
        Bybf = Bsq[0:4]
        Bysq = Bsq[4:8]
        Qa = sb([68, 8, T], BF16, "Qa")
        Ka = sb([68, 8, T], BF16, "Ka")
        BQ = [Buf() for _ in range(8)]
        BK = [Buf() for _ in range(8)]
        Va = sb([128, 8, NSUB, VW], BF16, "Va")
        BV = Buf()
        ksrc = Rot([sb([68, 2, 512], BF16, "ksrc%d" % i) for i in range(2)])
        vsrc = Rot([sb([128, 2, 4, VW], BF16, "vsrc%d" % i) for i in range(2)])
        kvsem = [S.dma_sem("kvsem%d" % i) for i in range(2)]
        Prot = Rot([sb([128, 512], BF16, "P%d" % i) for i in range(3)])
        otok = yc
        Botok = Byc
        rc = Rot([sb([128, 4], F32, "rc%d" % i) for i in range(2)])
        fT = sb([8, T], F32, "fT")
        csp = sb([8, T], F32, "csp")
        chi = sb([8, T], BF16, "chi")
        clo = sb([8, T], BF16, "clo")
        nhi = sb([8, T], BF16, "nhi")
        nlo = sb([8, T], BF16, "nlo")
        Bf = Buf()
        lftok = sb([128, NSUB, 8], F32, "lftok")
        Blftok = Buf()
        scrsem = S.dma_sem("scrsem")
        Bkscr = [Buf() for _ in range(L)]
        Bvscr = [Buf() for _ in range(L)]
        osem = S.dma_sem("osem")
        stsem = S.dma_sem("stsem")
        lfsem = S.dma_sem("lfsem")
        ksem_ = S.dma_sem("kscrsem")
        vsem_ = S.dma_sem("vscrsem")
        rowsem = S.dma_sem("rowsem")
        xsem = S.dma_sem("xsem")
        qs = sb([128, T], F32, "qs")
        kk = sb([128, T], F32, "kk")
        bg = sb([128, T], F32, "bg")
        Bqs, Bkk, Bbg = Buf(), Buf(), Buf()
        qt = sb([128, T], BF16, "qt")
        kt_ = sb([128, T], BF16, "kt_")
        qh = sb([128, T], BF16, "qh")
        kh = sb([128, T], F32, "kh")
        Bqt, Bkt, Bqh, Bkh = Buf(), Buf(), Buf(), Buf()
        vtok = sb([64, NCH, 512], BF16, "vtok")
        Bvtok = [Buf() for _ in range(NCH)]
        khtok = sb([64, NCH, 128], BF16, "khtok")
        Bkhtok = [Buf() for _ in range(NCH)]
        Abf = sb([64, NCH, 64], BF16, "Abf")
        BAbf = Buf()
        sqo = sb([128, T], BF16, "sqo")
        Bsqo = Buf()

        slm = sb([128, 128], F32, "slm")
        bmask = sb([64, 64], BF16, "bmask")
        lfc = sb([128, 16, 8], F32, "lfc")
        rk = sb([128, 16, 8], F32, "rk")
        rkt = sb([128, 16, 8], F32, "rkt")
        Brk = Buf()
        Ssm = Rot([sb([128, 128], F32, "Ssm%d" % i) for i in range(2)])

        PQ = "sp"
        WQ = "pool"

        def init_consts():
            S.pool(lambda e: e.memset(ident[:], 1.0), writes=[Bc])
            S.pool(lambda e: e.affine_select(ident[:], ident[:], [[-1, 128]], ALU.is_equal, 0.0, base=0,
                                             channel_multiplier=1), writes=[Bc])
            S.pool(lambda e: e.memset(triu[:], 1.0), writes=[Bc])
            S.pool(lambda e: e.affine_select(triu[:], triu[:], [[1, 128]], ALU.is_ge, 0.0, base=0,
                                             channel_multiplier=-1), writes=[Bc])
            S.pool(lambda e: e.tensor_copy(triub[:], triu[:]), writes=[Bc])
            S.pool(lambda e: e.memset(slm[:], 1.0), writes=[Bc])
            S.pool(lambda e: e.affine_select(slm[:], slm[:], [[-1, 128]], ALU.is_gt, 0.0, base=0,
                                             channel_multiplier=1), writes=[Bc])
            S.pool(lambda e: e.tensor_copy(bmask[:], triu[0:64, 0:64]), writes=[Bc])
            for b_ in range(1, 4):
                S.pool(lambda e, b_=b_: e.memset(bmask[0:16 * b_, 16 * b_:16 * b_ + 16], 0.0), writes=[Bc])
            S.pool(lambda e: e.memset(onesD[:], 1.0 / 1024), writes=[Bc])
            S.pool(lambda e: e.memset(onesC[:], 1.0 / 512), writes=[Bc])
            S.pool(lambda e: e.memset(onesV[:], 1.0 / 128), writes=[Bc])
            S.pool(lambda e: e.memset(ones[:], 1.0), writes=[Bc])
            S.pool(lambda e: e.memset(epsb[:], EPS), writes=[Bc])
            S.pool(lambda e: e.memset(Qa[64:68, :, :], 1.0), writes=BQ)
            S.pool(lambda e: e.memset(Ka[64:68, :, :], 1.0), writes=BK)
            S.pool(lambda e: e.memset(Va[:, :, :, 64:VW], 1.0), writes=[BV])
            S.pool(lambda e: e.memset(hist[:], 0.0), writes=Bhist)
            S.pool(lambda e: e.memset(Sst[:], 0.0), writes=[b for bl in BS for b in bl])
            S.pool(lambda e: e.memset(ccar[:], 0.0), writes=Bccar)
            W = Wd
            S.dma(PQ, csem, g1[:], W["norm1_g"].rearrange("l (c p) -> p l c", p=128), writes=[Bc])
            S.dma(PQ, csem, g2[:], W["norm2_g"].rearrange("l (c p) -> p l c", p=128), writes=[Bc])
            S.dma(PQ, csem, gf[:], W["final_g"][0].rearrange("(c p) -> p c", p=128), writes=[Bc])
            for l in range(L):
                for c in range(4):
                    S.dma(PQ, csem, cw[:, l, c, :], W["conv_w"][l][:, c * 128:(c + 1) * 128].rearrange("j p -> p j"),
                          writes=[Bc])
            S.dma(PQ, csem, cb[:], W["conv_b"].rearrange("l (c p) -> p l c", p=128), writes=[Bc])
            S.dma(PQ, csem, lng[:], W["conv_ln_g"].rearrange("l (c p) -> p l c", p=128), writes=[Bc])
            S.dma(PQ, csem, lnb[:], W["conv_ln_b"].rearrange("l (c p) -> p l c", p=128), writes=[Bc])
            S.dma(PQ, csem, nbf[:], W["fox_bf"].rearrange("l h -> h l"), writes=[Bc])
            for l in range(L):
                S.dma(PQ, csem, bfb[:, l:l + 1, :], W["fox_bf"][l:l + 1, :].partition_broadcast(128), writes=[Bc])
            for hd_ in range(4):
                S.dma(PQ, csem, lbp[:, hd_, :], W["hgrn_lb_param"][:, hd_ * 128:(hd_ + 1) * 128].rearrange("l p -> p l"),
                      writes=[Bc])
            S.dma(PQ, csem, hng[:], W["hgrn_norm_g"].rearrange("l v -> v l"), writes=[Bc])
            S.dve(lambda e: e.tensor_scalar(nbf[:], nbf[:], -1.0, None, ALU.mult), reads=[Bc], writes=[Bc])
            S.dve(lambda e: e.tensor_reduce(lbm[:], lbp[:], AX.X, ALU.max), reads=[Bc], writes=[Bc])
            S.dve(lambda e: e.tensor_tensor(lbt[:], lbp[:], lbm[:].unsqueeze(2).to_broadcast([128, 4, L]),
                                            ALU.subtract), writes=[Bc])
            S.act(lambda e: e.activation(lbt[:], lbt[:], AF.Exp), reads=[Bc], writes=[Bc])
            S.dve(lambda e: e.tensor_reduce(lbm[:], lbt[:], AX.X, ALU.add), reads=[Bc], writes=[Bc])
            S.dve(lambda e: e.reciprocal(lbm[:], lbm[:]), writes=[Bc])
            S.dve(lambda e: e.tensor_tensor(lbt[:], lbt[:], lbm[:].unsqueeze(2).to_broadcast([128, 4, L]),
                                            ALU.mult), writes=[Bc])
            S.dve(lambda e: e.memset(lb[:], 0.0), writes=[Bc])
            for l in range(1, L):
                S.dve(lambda e, l=l: e.tensor_tensor(lb[:, :, l:l + 1], lb[:, :, l - 1:l], lbt[:, :, l:l + 1],
                                                     ALU.add), writes=[Bc])
            S.dve(lambda e: e.tensor_scalar(oml[:], lb[:], -1.0, 1.0, ALU.mult, ALU.add), writes=[Bc])
            S.dve(lambda e: e.tensor_scalar(noml[:], oml[:], -1.0, None, ALU.mult), writes=[Bc])

        def rsqrt_eps(out_ap, in_ap, reads, wbuf):
            S.act(lambda e: e.activation(out_ap, in_ap, AF.Sqrt, bias=epsb[:, 0:1]), reads=list(reads) + [Bc],
                  writes=[wbuf])
            S.dve(lambda e: e.reciprocal(out_ap, out_ap), writes=[wbuf])

        def wload(src2d, kc, cols):
            best, bi_ = None, None
            for i_ in range(NW):
                b_ = wrot.bufs[i_]
                if b_.w is not None and not b_.r:
                    continue
                key = b_.r.get("pe", 0)
                if best is None or key < best:
                    best, bi_ = key, i_
            assert bi_ is not None, "no free weight slot"
            t, b = wrot.tiles[bi_], wrot.bufs[bi_]
            sem = wsem[bi_]
            view = t[:, 0:kc * cols].rearrange("p (k c) -> p k c", c=cols)
            S.dma(WQ, sem, view, src2d.rearrange("(k p) c -> p k c", p=128), writes=[b])
            return view, b

        def mm_fm(ps, M, N, wt, wb, col0, nk, act, Bact, extra_w=()):
            for k in range(nk):
                S.pe(lambda e, k=k: e.matmul(ps[0:M, 0:N], lhsT=wt[:, k, col0:col0 + M], rhs=act[:, k, 0:N],
                                             start=(k == 0), stop=(k == nk - 1)),
                     reads=[wb, Bact[k]], writes=list(extra_w))

        def rmsnorm(gt, l, N):
            for c in range(8):
                S.act(lambda e, c=c: e.activation(sq[:, c, 0:N], xT[:, c, 0:N], AF.Square),
                      reads=[Bx[c]], writes=[Bsq[c]])
            ps, bp = psg.get()
            for c in range(8):
                S.pe(lambda e, c=c: e.matmul(ps[:, 0:N], lhsT=onesD[:], rhs=sq[:, c, 0:N], start=(c == 0),
                                             stop=(c == 7)), reads=[Bc, Bsq[c]], writes=[bp])
            rs, brs = ft.get()
            rsqrt_eps(rs[:, 0:N], ps[:, 0:N], [bp], brs)
            for c in range(8):
                gap = gt[:, l, c:c + 1] if l is not None else gt[:, c:c + 1]
                S.dve(lambda e, c=c, gap=gap: e.scalar_tensor_tensor(hT[:, c, 0:N], xT[:, c, 0:N], gap, rs[:, 0:N],
                                                                     ALU.mult, ALU.mult),
                      reads=[Bx[c], brs, Bc], writes=[Bh[c]])

        def branch_out(l, wname, gcol, N, first, last):
            wo_, bwo = wload(Wd[wname][l], 4, 1024)
            wg = [wload(Wd["w_in"][l][:, gcol + i * 512: gcol + (i + 1) * 512], 8, 512) for i in range(2)]
            for oc in range(8):
                psy, bpy = psg.get()
                for k in range(4):
                    S.pe(lambda e, k=k, oc=oc: e.matmul(psy[:, 0:N], lhsT=wo_[:, k, oc * 128:(oc + 1) * 128],
                                                        rhs=brT[:, k, 0:N], start=(k == 0), stop=(k == 3)),
                         reads=[bwo, Bbr[k]], writes=[bpy])
                psgt, bpg = psg.get()
                wgt, bwg = wg[oc // 4]
                mm_fm(psgt, 128, N, wgt, bwg, (oc % 4) * 128, 8, hT, Bh, extra_w=[bpg])
                sg, bsg = ft.get()
                S.act(lambda e: e.activation(sg[:, 0:N], psgt[:, 0:N], AF.Sigmoid), reads=[bpg], writes=[bsg])
                if first:
                    S.dve(lambda e, oc=oc: e.tensor_tensor(mT[:, oc, 0:N], sg[:, 0:N], psy[:, 0:N], ALU.mult),
                          reads=[bsg, bpy], writes=[Bm[oc]])
                else:
                    S.dve(lambda e: e.tensor_tensor(sg[:, 0:N], sg[:, 0:N], psy[:, 0:N], ALU.mult),
                          reads=[bpy], writes=[bsg])
                    if last:
                        S.dve(lambda e, oc=oc: e.tensor_tensor(sq[:, oc, 0:N], mT[:, oc, 0:N], sg[:, 0:N], ALU.add),
                              reads=[bsg, Bm[oc]], writes=[Bsq[oc]])
                    else:
                        S.dve(lambda e, oc=oc: e.tensor_tensor(mT[:, oc, 0:N], mT[:, oc, 0:N], sg[:, 0:N], ALU.add),
                              reads=[bsg], writes=[Bm[oc]])

        def conv_branch(l, N, segs):
            wA, bA = wload(Wd["w_in"][l][:, C_AV:C_AV + 512], 8, 512)
            wG, bG = wload(Wd["w_in"][l][:, C_AG:C_AG + 512], 8, 512)
            for c in range(4):
                psv, bpv = psg.get()
                mm_fm(psv, 128, N, wA, bA, c * 128, 8, hT, Bh, extra_w=[bpv])
                psg_, bpg = psg.get()
                mm_fm(psg_, 128, N, wG, bG, c * 128, 8, hT, Bh, extra_w=[bpg])
                sg, bsg = ft.get()
                S.act(lambda e: e.activation(sg[:, 0:N], psg_[:, 0:N], AF.Sigmoid), reads=[bpg], writes=[bsg])
                av, bav = ft.get()
                S.act(lambda e: e.activation(av[:, 0:N], psv[:, 0:N], AF.Copy), reads=[bpv], writes=[bav])
                segs(l, c, av, bav, sg, bsg)

        def conv_post(l, N):
            for c in range(4):
                S.act(lambda e, c=c: e.activation(ybf[:, c, 0:N], yc[:, c, 0:N], AF.Copy), reads=[Byc[c]],
                      writes=[Bybf[c]])
                S.act(lambda e, c=c: e.activation(ysq[:, c, 0:N], yc[:, c, 0:N], AF.Square), reads=[Byc[c]],
                      writes=[Bysq[c]])
            psm, bpm = psg.get()
            for c in range(4):
                S.pe(lambda e, c=c: e.matmul(psm[:, 0:N], lhsT=onesC[:], rhs=ybf[:, c, 0:N], start=(c == 0),
                                             stop=(c == 3)), reads=[Bc, Bybf[c]], writes=[bpm])
            pss, bps = psg.get()
            for c in range(4):
                S.pe(lambda e, c=c: e.matmul(pss[:, 0:N], lhsT=onesC[:], rhs=ysq[:, c, 0:N], start=(c == 0),
                                             stop=(c == 3)), reads=[Bc, Bysq[c]], writes=[bps])
            mean, bmean = ft.get()
            S.dve(lambda e: e.tensor_copy(mean[:, 0:N], psm[:, 0:N]), reads=[bpm], writes=[bmean])
            var, bvar = ft.get()
            S.dve(lambda e: e.tensor_tensor(var[:, 0:N], mean[:, 0:N], mean[:, 0:N], ALU.mult), reads=[bmean],
                  writes=[bvar])
            S.dve(lambda e: e.tensor_tensor(var[:, 0:N], pss[:, 0:N], var[:, 0:N], ALU.subtract), reads=[bps],
                  writes=[bvar])
            rsqrt_eps(var[:, 0:N], var[:, 0:N], [], bvar)
            S.dve(lambda e: e.scalar_tensor_tensor(mean[:, 0:N], mean[:, 0:N], -1.0, var[:, 0:N], ALU.mult, ALU.mult),
                  reads=[bvar], writes=[bmean])
            for c in range(4):
                t, bt = ft.get()
                S.dve(lambda e, c=c: e.tensor_tensor(t[:, 0:N], yc[:, c, 0:N], var[:, 0:N], ALU.mult),
                      reads=[Byc[c], bvar], writes=[bt])
                S.dve(lambda e: e.tensor_tensor(t[:, 0:N], t[:, 0:N], mean[:, 0:N], ALU.add), reads=[bmean],
                      writes=[bt])
                S.act(lambda e, c=c: e.activation(brT[:, c, 0:N], t[:, 0:N], AF.Silu, bias=lnb[:, l, c:c + 1],
                                                  scale=lng[:, l, c:c + 1]), reads=[bt, Bc], writes=[Bbr[c]])

        def conv_taps(l, c, u0, y0, n):
            S.dve(lambda e: e.tensor_scalar(yc[:, c, y0:y0 + n], ubuf[:, c, u0:u0 + n], cw[:, l, c, 0:1],
                                            cb[:, l, c:c + 1], ALU.mult, ALU.add),
                  reads=[Bub[c], Bc], writes=[Byc[c]])
            for j in range(1, 31):
                S.dve(lambda e, j=j: e.scalar_tensor_tensor(yc[:, c, y0:y0 + n], ubuf[:, c, u0 + j:u0 + j + n],
                                                            cw[:, l, c, j:j + 1], yc[:, c, y0:y0 + n],
                                                            ALU.mult, ALU.add), reads=[Bub[c]], writes=[Byc[c]])

        def load_x_tile(src, N):
            nsub = (N + 127) // 128
            for i in range(nsub):
                n = min(128, N - i * 128)
                stt, bst_ = stg.get()
                S.dma(PQ, None, stt[0:n, :], src[i * 128:i * 128 + n, :], writes=[bst_])
                for half in range(2):
                    ps, bp = psg.get()
                    for cc in range(4):
                        c = half * 4 + cc
                        S.pe(lambda e, cc=cc, c=c, n=n, ps=ps, stt=stt: e.transpose(
                            ps[:, cc * 128:cc * 128 + n], stt[0:n, c * 128:(c + 1) * 128], ident[0:n, 0:n]),
                             reads=[bst_, Bc], writes=[bp])
                    S.act(lambda e, i=i, n=n, half=half, ps=ps: e.activation(
                        xT[:, half * 4:half * 4 + 4, i * 128:i * 128 + n],
                        ps[:, :].rearrange("p (c t) -> p c t", t=128)[:, :, 0:n], AF.Copy),
                          reads=[bp], writes=Bx[half * 4:half * 4 + 4])

        def store_y_tile(dst, N):
            for c in range(8):
                S.act(lambda e, c=c: e.activation(sq[:, c, 0:N], xT[:, c, 0:N], AF.Square),
                      reads=[Bx[c]], writes=[Bsq[c]])
            ps, bp = psg.get()
            for c in range(8):
                S.pe(lambda e, c=c: e.matmul(ps[:, 0:N], lhsT=onesD[:], rhs=sq[:, c, 0:N], start=(c == 0),
                                             stop=(c == 7)), reads=[Bc, Bsq[c]], writes=[bp])
            rs, brs = ft.get()
            rsqrt_eps(rs[:, 0:N], ps[:, 0:N], [bp], brs)
            for c in range(8):
                S.dve(lambda e, c=c: e.scalar_tensor_tensor(mT[:, c, 0:N], xT[:, c, 0:N], gf[:, c:c + 1], rs[:, 0:N],
                                                            ALU.mult, ALU.mult),
                      reads=[Bx[c], brs, Bc], writes=[Bm[c]])
            nsub = (N + 127) // 128
            for i in range(nsub):
                n = min(128, N - i * 128)
                stt, bst_ = stg.get()
                for half in range(2):
                    ps, bp = psg.get()
                    for cc in range(4):
                        c = half * 4 + cc
                        S.pe(lambda e, c=c, cc=cc, i=i, n=n, ps=ps: e.transpose(ps[0:n, cc * 128:(cc + 1) * 128],
                                                                         mT[:, c, i * 128:i * 128 + n], ident[:]),
                             reads=[Bm[c], Bc], writes=[bp])
                    S.act(lambda e, n=n, half=half, ps=ps, stt=stt: e.activation(
                        stt[0:n, half * 512:(half + 1) * 512], ps[0:n, :], AF.Copy), reads=[bp], writes=[bst_])
                S.dma(PQ, None, dst[i * 128:i * 128 + n, :], stt[0:n, :], reads=[bst_])

        def ffn_and_wo(l, N):
            wO = [wload(Wd["w_o"][l][:, i * 512:(i + 1) * 512], 8, 512) for i in range(2)]
            for oc in range(8):
                ps, bp = psg.get()
                wt, bw = wO[oc // 4]
                mm_fm(ps, 128, N, wt, bw, (oc % 4) * 128, 8, sq, Bsq, extra_w=[bp])
                S.dve(lambda e, oc=oc: e.tensor_tensor(xT[:, oc, 0:N], xT[:, oc, 0:N], ps[:, 0:N], ALU.add),
                      reads=[bp], writes=[Bx[oc]])
            rmsnorm(g2, l, N)
            for q in range(4):
                for j in range(2):
                    wU, bU = wload(Wd["w_up"][l][:, (q * 2 + j) * 512:(q * 2 + j + 1) * 512], 8, 512)
                    for cc in range(4):
                        ps, bp = psg.get()
                        mm_fm(ps, 128, N, wU, bU, cc * 128, 8, hT, Bh, extra_w=[bp])
                        r1, br1 = ft.get()
                        S.act(lambda e: e.activation(r1[:, 0:N], ps[:, 0:N], AF.Relu), reads=[bp], writes=[br1])
                        S.dve(lambda e, j=j, cc=cc: e.tensor_tensor(sq[:, j * 4 + cc, 0:N], r1[:, 0:N], r1[:, 0:N],
                                                                    ALU.mult), reads=[br1], writes=[Bsq[j * 4 + cc]])
                wDn = [wload(Wd["w_down"][l][q * 1024:(q + 1) * 1024, i * 512:(i + 1) * 512], 8, 512) for i in range(2)]
                for oc in range(8):
                    ps, bp = psg.get()
                    wt, bw = wDn[oc // 4]
                    mm_fm(ps, 128, N, wt, bw, (oc % 4) * 128, 8, sq, Bsq, extra_w=[bp])
                    S.dve(lambda e, oc=oc: e.tensor_tensor(xT[:, oc, 0:N], xT[:, oc, 0:N], ps[:, 0:N], ALU.add),
                          reads=[bp], writes=[Bx[oc]])

        def fox_qkv(l, N, tok0, kdst, vdst, lfdst, carry=True):
            wQ, bQ = wload(Wd["w_in"][l][:, C_Q:C_Q + 512], 8, 512)
            wK, bK = wload(Wd["w_in"][l][:, C_K:C_K + 512], 8, 512)
            wV, bV = wload(Wd["w_in"][l][:, C_V:C_V + 520], 8, 520)
            for hh in range(8):
                ps, bp = psg.get()
                mm_fm(ps, 64, N, wQ, bQ, hh * 64, 8, hT, Bh, extra_w=[bp])
                S.act(lambda e, hh=hh: e.activation(Qa[0:64, hh, 0:N], ps[0:64, 0:N], AF.Copy, scale=0.125),
                      reads=[bp], writes=[BQ[hh]])
                ps2, bp2 = psg.get()
                mm_fm(ps2, 64, N, wK, bK, hh * 64, 8, hT, Bh, extra_w=[bp2])
                S.act(lambda e, hh=hh: e.activation(Ka[0:64, hh, 0:N], ps2[0:64, 0:N], AF.Copy), reads=[bp2],
                      writes=[BK[hh]])
            ps, bp = psg.get()
            mm_fm(ps, 8, N, wV, bV, 512, 8, hT, Bh, extra_w=[bp])
            S.act(lambda e: e.activation(fT[:, 0:N], ps[0:8, 0:N], AF.Exp, bias=nbf[:, l:l + 1], scale=-1.0),
                  reads=[bp, Bc], writes=[Bf])
            S.act(lambda e: e.activation(fT[:, 0:N], fT[:, 0:N], AF.Ln, bias=1.0), writes=[Bf])
            return wK, bK, wV, bV

        def fox_scan_rows(l, N, c0, n, init_ap, Binit):
            S.dve(lambda e: e.tensor_tensor_scan(csp[:, c0:c0 + n], ones[0:8, 0:n], fT[:, c0:c0 + n], init_ap,
                                                 ALU.mult, ALU.add), reads=[Bf, Bc] + Binit, writes=[Bf])

        def fox_split_rows(N):
            S.dve(lambda e: e.tensor_copy(chi[:, 0:N], csp[:, 0:N]), writes=[Bf])
            S.dve(lambda e: e.tensor_tensor(clo[:, 0:N], csp[:, 0:N], chi[:, 0:N], ALU.subtract), writes=[Bf])
            S.dve(lambda e: e.tensor_scalar(nhi[:, 0:N], chi[:, 0:N], -1.0, None, ALU.mult), writes=[Bf])
            S.dve(lambda e: e.tensor_scalar(nlo[:, 0:N], clo[:, 0:N], -1.0, None, ALU.mult), writes=[Bf])
            pairs = []
            for hh in range(8):
                pairs.append((Qa[64:65, hh, 0:N], nhi[hh:hh + 1, 0:N]))
                pairs.append((Qa[65:66, hh, 0:N], nlo[hh:hh + 1, 0:N]))
                pairs.append((Ka[66:67, hh, 0:N], chi[hh:hh + 1, 0:N]))
                pairs.append((Ka[67:68, hh, 0:N], clo[hh:hh + 1, 0:N]))
            S.dma_group(PQ, rowsem, pairs, reads=[Bf], writes=BQ + BK)

        def fox_tok_outputs(l, N, wK, bK, wV, bV, kdst_fn, vdst_fn, lfdst_fn, va_fn):
            nsub = (N + 127) // 128
            for i in range(nsub):
                n = min(128, N - i * 128)
                stt, bst_ = stg.get()
                psk, bpk = psg.get()
                for k in range(8):
                    S.pe(lambda e, k=k, i=i, n=n, psk=psk: e.matmul(psk[0:n, :], lhsT=hT[:, k, i * 128:i * 128 + n],
                                                           rhs=wK[:, k, 0:512], start=(k == 0), stop=(k == 7)),
                         reads=[bK, Bh[k]], writes=[bpk])
                S.act(lambda e, n=n, psk=psk, stt=stt: e.activation(stt[0:n, 0:512], psk[0:n, :], AF.Copy),
                      reads=[bpk], writes=[bst_])
                psv, bpv = psg.get()
                for k in range(8):
                    S.pe(lambda e, k=k, i=i, n=n, psv=psv: e.matmul(psv[0:n, :], lhsT=hT[:, k, i * 128:i * 128 + n],
                                                           rhs=wV[:, k, 0:512], start=(k == 0), stop=(k == 7)),
                         reads=[bV, Bh[k]], writes=[bpv])
                S.act(lambda e, n=n, psv=psv, stt=stt: e.activation(stt[0:n, 512:1024], psv[0:n, :], AF.Copy),
                      reads=[bpv], writes=[bst_])
                if "nova" not in DBG2:
                    va_fn(i, n, stt, bst_)
                if "nolf" in DBG2:
                    kdst_fn(i, n, stt, bst_)
                    vdst_fn(i, n, stt, bst_)
                    continue
                psf, bpf = psg.get()
                S.pe(lambda e, i=i, n=n, psf=psf: e.transpose(psf[0:n, 0:8], fT[0:8, i * 128:i * 128 + n],
                                                              ident[0:8, 0:8]), reads=[Bf, Bc], writes=[bpf])
                S.act(lambda e, i=i, n=n, psf=psf: e.activation(lftok[0:n, i, :], psf[0:n, 0:8], AF.Copy, scale=-1.0),
                      reads=[bpf], writes=[Blftok])
                if "nokv" not in DBG2:
                    kdst_fn(i, n, stt, bst_)
                    vdst_fn(i, n, stt, bst_)
                if "nolfdma" not in DBG2:
                    lfdst_fn(i, n)

        def fox_attend_prompt(l, j, groups=(0, 1, 2, 3)):
            for g in groups:
                pairs = []
                for jj in range(j + 1):
                    for hl in range(2):
                        for kt in range(4):
                            pairs.append((jj, hl, kt))
                src = {}
                state = {}

                def get_src(jj):
                    if jj not in src:
                        kt_t, bkt = ksrc.get()
                        ksm = kvsem[(ksrc.i - 1) % 2]
                        vt_t, _unused = vsrc.get()
                        S.dma_group(PQ, ksm, [(kt_t[:], kscr[l, :, 2 * g:2 * g + 2, jj * T:(jj + 1) * T]),
                                              (vt_t[:], vscr[l, :, 2 * g:2 * g + 2, jj * 4:(jj + 1) * 4, :])],
                                    raw=[Bkscr[l], Bvscr[l]], writes=[bkt])
                        src[jj] = (kt_t, vt_t, bkt)
                    return src[jj]

                def emit_qk(n):
                    jj, hl, kt = pairs[n]
                    kt_t, vt_t, bkt = get_src(jj)
                    hh = 2 * g + hl
                    q0 = kt * 128 if jj == j else 0
                    ps, bp = psS.get()
                    S.pe(lambda e: e.matmul(ps[:, q0:T], lhsT=kt_t[:, hl, kt * 128:(kt + 1) * 128],
                                            rhs=Qa[:, hh, q0:T], start=True, stop=True),
                         reads=[bkt, BQ[hh]], writes=[bp])
                    state[n] = (ps, bp, q0)

                def emit_rest(n):
                    jj, hl, kt = pairs[n]
                    kt_t, vt_t, bkt = get_src(jj)
                    ps, bp, q0 = state.pop(n)
                    P, bP = Prot.get()
                    S.act(lambda e: e.activation(P[:, q0:T], ps[:, q0:T], AF.Exp), reads=[bp], writes=[bP])
                    if jj == j:
                        S.dve(lambda e: e.tensor_tensor(P[:, q0:q0 + 128], P[:, q0:q0 + 128], triub[:], ALU.mult),
                              reads=[Bc], writes=[bP])
                    for i in range(q0 // 128, 4):
                        S.pe(lambda e, i=i: e.matmul(psO[hl][:, i, 0:65], lhsT=P[:, i * 128:(i + 1) * 128],
                                                     rhs=vt_t[:, hl, kt, 0:65], start=first[hl], stop=False,
                                                     skip_group_check=True),
                             reads=[bP, bkt], writes=[BpsO[hl]])
                        first[hl] = False

                first = [True, True]
                emit_qk(0)
                for n in range(len(pairs)):
                    if n + 1 < len(pairs):
                        emit_qk(n + 1)
                    emit_rest(n)
                for hl in range(2):
                    hh = 2 * g + hl
                    r, br = rc.get()
                    S.dve(lambda e, hl=hl, r=r: e.reciprocal(r[:], psO[hl][:, :, 64]), reads=[BpsO[hl]], writes=[br])
                    for i in range(4):
                        S.dve(lambda e, hl=hl, hh=hh, i=i, r=r: e.tensor_scalar(
                            otok[:, i, hh * 64:(hh + 1) * 64], psO[hl][:, i, 0:64], r[:, i:i + 1], None, ALU.mult),
                              reads=[BpsO[hl], br], writes=[Botok[i]])

        def fox_o_transpose(N):
            nsub = (N + 127) // 128
            for fc in range(4):
                ps, bp = psg.get()
                for i in range(nsub):
                    n = min(128, N - i * 128)
                    S.pe(lambda e, i=i, n=n, fc=fc: e.transpose(ps[:, i * 128:i * 128 + n],
                                                                otok[0:n, i, fc * 128:(fc + 1) * 128],
                                                                ident[0:n, 0:n]),
                         reads=[Botok[i], Bc], writes=[bp])
                S.act(lambda e, fc=fc: e.activation(brT[:, fc, 0:N], ps[:, 0:N], AF.Copy), reads=[bp],
                      writes=[Bbr[fc]])

        def hgrn_branch(l, N, C, segs, S_in_fn, S_out_fn, between=None):
            nch = N // C
            wi, bi = wload(Wd["w_in"][l][:, C_HI:C_HI + 512], 8, 512)
            wq, bq = wload(Wd["w_in"][l][:, C_HQ:C_HQ + 512], 8, 512)
            wf, bf_ = wload(Wd["w_in"][l][:, C_HF:C_HF + 512], 8, 512)
            for ci in range(nch):
                ps, bp = psg.get()
                for k in range(8):
                    S.pe(lambda e, k=k, ci=ci: e.matmul(ps[0:C, :], lhsT=hT[:, k, ci * C:(ci + 1) * C],
                                                        rhs=wi[:, k, :], start=(k == 0), stop=(k == 7)),
                         reads=[bi, Bh[k]], writes=[bp])
                S.act(lambda e, ci=ci: e.activation(vtok[0:C, ci, :], ps[0:C, :], AF.Copy), reads=[bp],
                      writes=[Bvtok[ci]])
            wo_, bo = wload(Wd["w_in"][l][:, C_HO:C_HO + 512], 8, 512)
            mid = C // 2 - 1
            for hd in range(4):
                ps, bp = psg.get()
                mm_fm(ps, 128, N, wq, bq, hd * 128, 8, hT, Bh, extra_w=[bp])
                S.act(lambda e: e.activation(qs[:, 0:N], ps[:, 0:N], AF.Silu), reads=[bp], writes=[Bqs])
                ps2, bp2 = psg.get()
                mm_fm(ps2, 128, N, wf, bf_, hd * 128, 8, hT, Bh, extra_w=[bp2])
                sg, bsg = ft.get()
                S.act(lambda e: e.activation(sg[:, 0:N], ps2[:, 0:N], AF.Sigmoid), reads=[bp2], writes=[bsg])
                lg, blg = ft.get()
                S.act(lambda e, hd=hd: e.activation(lg[:, 0:N], sg[:, 0:N], AF.Ln, bias=lb[:, hd, l:l + 1],
                                                    scale=oml[:, hd, l:l + 1]), reads=[bsg, Bc], writes=[blg])
                S.dve(lambda e, hd=hd: e.tensor_scalar(kk[:, 0:N], sg[:, 0:N], noml[:, hd, l:l + 1],
                                                       oml[:, hd, l:l + 1], ALU.mult, ALU.add),
                      reads=[bsg, Bc], writes=[Bkk])
                for (c0, ncs, si) in segs:
                    S.dve(lambda e, c0=c0, ncs=ncs: e.tensor_tensor_scan(
                        bg[:, c0 * C:(c0 + ncs) * C], ones[:, 0:ncs * C], lg[:, c0 * C:(c0 + ncs) * C], 0.0,
                        ALU.mult, ALU.add), reads=[blg, Bc], writes=[Bbg])
                bg3 = bg[:, 0:N].rearrange("p (c s) -> p c s", s=C)
                dm, bdm = ft.get()
                dm3 = dm[:, 0:N].rearrange("p (c s) -> p c s", s=C)
                S.dve(lambda e: e.tensor_tensor(dm3, bg3, bg[:, mid:N:C].unsqueeze(2).to_broadcast([128, nch, C]),
                                                ALU.subtract), reads=[Bbg], writes=[bdm])
                e1, be1 = ft.get()
                S.act(lambda e: e.activation(e1[:, 0:N], dm[:, 0:N], AF.Exp), reads=[bdm], writes=[be1])
                S.dve(lambda e: e.tensor_tensor(qt[:, 0:N], qs[:, 0:N], e1[:, 0:N], ALU.mult), reads=[Bqs, be1],
                      writes=[Bqt])
                e2, be2 = ft.get()
                S.act(lambda e: e.activation(e2[:, 0:N], dm[:, 0:N], AF.Exp, scale=-1.0), reads=[bdm], writes=[be2])
                S.dve(lambda e: e.tensor_tensor(kt_[:, 0:N], kk[:, 0:N], e2[:, 0:N], ALU.mult), reads=[Bkk, be2],
                      writes=[Bkt])
                bst, bbst = ft.get()
                S.dve(lambda e: e.tensor_tensor(bst[:, 0:nch], bg[:, 0:N:C], lg[:, 0:N:C], ALU.subtract),
                      reads=[Bbg, blg], writes=[bbst])
                ds, bds = ft.get()
                ds3 = ds[:, 0:N].rearrange("p (c s) -> p c s", s=C)
                S.dve(lambda e: e.tensor_tensor(ds3, bg3, bst[:, 0:nch].unsqueeze(2).to_broadcast([128, nch, C]),
                                                ALU.subtract), reads=[bbst, Bbg], writes=[bds])
                e3, be3 = ft.get()
                S.act(lambda e: e.activation(e3[:, 0:N], ds[:, 0:N], AF.Exp), reads=[bds], writes=[be3])
                S.dve(lambda e: e.tensor_tensor(qh[:, 0:N], qs[:, 0:N], e3[:, 0:N], ALU.mult), reads=[Bqs, be3],
                      writes=[Bqh])
                dl, bdl = ft.get()
                dl3 = dl[:, 0:N].rearrange("p (c s) -> p c s", s=C)
                S.dve(lambda e: e.tensor_tensor(dl3, bg3, bg[:, C - 1:N:C].unsqueeze(2).to_broadcast([128, nch, C]),
                                                ALU.subtract), reads=[Bbg], writes=[bdl])
                e4, be4 = ft.get()
                S.act(lambda e: e.activation(e4[:, 0:N], dl[:, 0:N], AF.Exp, scale=-1.0), reads=[bdl], writes=[be4])
                S.dve(lambda e: e.tensor_tensor(kh[:, 0:N], kk[:, 0:N], e4[:, 0:N], ALU.mult), reads=[Bkk, be4],
                      writes=[Bkh])
                if between is not None:
                    between(hd)
                for ci in range(nch):
                    pst_, bpt = psg.get()
                    S.pe(lambda e, ci=ci: e.transpose(pst_[0:C, 0:128], kh[:, ci * C:(ci + 1) * C], ident[:]),
                         reads=[Bkh, Bc], writes=[bpt])
                    S.act(lambda e, ci=ci: e.activation(khtok[0:C, ci, :], pst_[0:C, 0:128], AF.Copy), reads=[bpt],
                          writes=[Bkhtok[ci]])
                psA, bpA = psg.get()
                for ci in range(nch):
                    S.pe(lambda e, ci=ci: e.matmul(psA[0:C, ci * C:(ci + 1) * C], lhsT=kt_[:, ci * C:(ci + 1) * C],
                                                   rhs=qt[:, ci * C:(ci + 1) * C], start=True, stop=True),
                         reads=[Bkt, Bqt], writes=[bpA])
                S.dve(lambda e: e.tensor_tensor(Abf[0:C, 0:nch, 0:C],
                                                psA[0:C, 0:nch * C].rearrange("p (c s) -> p c s", s=C),
                                                triu[0:C, 0:C].unsqueeze(1).to_broadcast([C, nch, C]), ALU.mult),
                      reads=[bpA, Bc], writes=[BAbf])
                pso, bpo = psS.get()
                for (c0, ncs, si) in segs:
                    Sap, BSt = S_in_fn(hd, si)
                    sbf, bsbf = Sbf.get()
                    S.act(lambda e, Sap=Sap, sbf=sbf: e.activation(sbf[:], Sap, AF.Copy), reads=[BSt], writes=[bsbf])
                    for ci in range(c0, c0 + ncs):
                        S.pe(lambda e, ci=ci, hd=hd: e.matmul(pso[:, ci * C:(ci + 1) * C],
                                                              lhsT=vtok[0:C, ci, hd * 128:(hd + 1) * 128],
                                                              rhs=Abf[0:C, ci, 0:C], start=True, stop=False),
                             reads=[Bvtok[ci], BAbf], writes=[bpo])
                        S.pe(lambda e, ci=ci, sbf=sbf: e.matmul(pso[:, ci * C:(ci + 1) * C], lhsT=sbf[:],
                                                                rhs=qh[:, ci * C:(ci + 1) * C], start=False, stop=True),
                             reads=[bsbf, Bqh], writes=[bpo])
                        psd, bpd = psg.get()
                        S.pe(lambda e, ci=ci, hd=hd, psd=psd: e.matmul(psd[:, 0:128], lhsT=khtok[0:C, ci, :],
                                                                       rhs=vtok[0:C, ci, hd * 128:(hd + 1) * 128],
                                                                       start=True, stop=True),
                             reads=[Bkhtok[ci], Bvtok[ci]], writes=[bpd])
                        S.dve(lambda e, ci=ci, Sap=Sap, psd=psd, e3=e3: e.scalar_tensor_tensor(
                            Sap, Sap, e3[:, (ci + 1) * C - 1:(ci + 1) * C], psd[:, 0:128], ALU.mult, ALU.add),
                              reads=[bpd, be3], writes=[BSt])
                        if ci < c0 + ncs - 1:
                            sbf, bsbf = Sbf.get()
                            S.act(lambda e, Sap=Sap, sbf=sbf: e.activation(sbf[:], Sap, AF.Copy), reads=[BSt],
                                  writes=[bsbf])
                    S_out_fn(hd, si, Sap, BSt)
                S.act(lambda e: e.activation(sqo[:, 0:N], pso[:, 0:N], AF.Square), reads=[bpo], writes=[Bsqo])
                psm, bpm = psg.get()
                S.pe(lambda e: e.matmul(psm[:, 0:N], lhsT=onesV[:], rhs=sqo[:, 0:N], start=True, stop=True),
                     reads=[Bsqo, Bc], writes=[bpm])
                rs, brs = ft.get()
                rsqrt_eps(rs[:, 0:N], psm[:, 0:N], [bpm], brs)
                on, bon = ft.get()
                S.dve(lambda e: e.scalar_tensor_tensor(on[:, 0:N], pso[:, 0:N], hng[:, l:l + 1], rs[:, 0:N], ALU.mult,
                                                       ALU.mult), reads=[bpo, brs, Bc], writes=[bon])
                psq, bpq = psg.get()
                mm_fm(psq, 128, N, wo_, bo, hd * 128, 8, hT, Bh, extra_w=[bpq])
                go, bgo = ft.get()
                S.act(lambda e: e.activation(go[:, 0:N], psq[:, 0:N], AF.Silu), reads=[bpq], writes=[bgo])
                S.dve(lambda e, hd=hd: e.tensor_tensor(brT[:, hd, 0:N], on[:, 0:N], go[:, 0:N], ALU.mult),
                      reads=[bon, bgo], writes=[Bbr[hd]])

        init_consts()

        import os as _os
        DBG = _os.environ.get("KDBG", "full")
        DBG2 = _os.environ.get("KDBG2", "")

        def prompt_tile(j):
            N = T
            tok0 = j * T
            load_x_tile(xp[tok0:tok0 + T, :], N)
            for l in range(L if DBG != "load" else 0):
                rmsnorm(g1, l, N)
                def segs(l_, c, psv, bpv, sg, bsg):
                    S.act(lambda e: e.activation(ubuf[:, c, 0:30], hist[:, l_, c, :], AF.Copy), reads=[Bhist[l_]],
                          writes=[Bub[c]])
                    S.dve(lambda e: e.tensor_tensor(ubuf[:, c, 30:30 + N], psv[:, 0:N], sg[:, 0:N], ALU.mult),
                          reads=[bpv, bsg], writes=[Bub[c]])
                    conv_taps(l_, c, 0, 0, N)
                    S.act(lambda e: e.activation(hist[:, l_, c, :], ubuf[:, c, N:N + 30], AF.Copy), reads=[Bub[c]],
                          writes=[Bhist[l_]])
                if DBG == "norm":
                    continue
                conv_branch(l, N, segs)
                wK, bK, wV, bV = fox_qkv(l, N, tok0, None, None, None)
                def va_fn(i, n, stt, bst_):
                    S.act(lambda e: e.activation(Va[0:n, :, i, 0:64],
                                                 stt[0:n, 512:1024].rearrange("p (h d) -> p h d", d=64), AF.Copy),
                          reads=[bst_], writes=[BV])

                def kdst(i, n, stt, bst_):
                    S.dma(PQ, None, kp[l, tok0 + i * 128: tok0 + i * 128 + n, :], stt[0:n, 0:512],
                          reads=[bst_])

                def vdst(i, n, stt, bst_):
                    S.dma(PQ, None, vp[l, tok0 + i * 128: tok0 + i * 128 + n, :], stt[0:n, 512:1024],
                          reads=[bst_])

                def lfdst(i, n):
                    if i == NSUB - 1:
                        S.dma(PQ, lfsem, lfp[l, tok0:tok0 + N, :].rearrange("(s p) h -> p s h", p=128), lftok[:],
                              reads=[Blftok])
                fox_tok_outputs(l, N, wK, bK, wV, bV, kdst, vdst, lfdst, va_fn)
                conv_post(l, N)
                branch_out(l, "w_a_out", C_GA, N, True, False)
                if j == NT - 1:
                    for c in range(4):
                        S.dma(PQ, osem, cp[l][:, c * 128:(c + 1) * 128].rearrange("j p -> p j"), hist[:, l, c, :],
                              reads=[Bhist[l]])
                fox_scan_rows(l, N, 0, N, ccar[:, l:l + 1], [Bccar[l]])
                S.dve(lambda e: e.tensor_copy(ccar[:, l:l + 1], csp[:, N - 1:N]), reads=[Bf], writes=[Bccar[l]])
                fox_split_rows(N)
                S.dma(PQ, ksem_, kscr[l, :, :, tok0:tok0 + N], Ka[:], reads=BK, writes=[Bkscr[l]])
                S.dma(PQ, vsem_, vscr[l, :, :, j * NSUB:(j + 1) * NSUB, :], Va[:], reads=[BV], writes=[Bvscr[l]])
                def S_in(hd, si):
                    return Sst[:, l, hd, :], BS[l][hd]

                def S_out(hd, si, Sap, BSt):
                    if j == NT - 1:
                        S.dma(PQ, osem, hp[l, hd], Sap, reads=[BSt])
                hgrn_branch(l, N, 64, [(0, NCH, 0)], S_in, S_out,
                            between=lambda hd: fox_attend_prompt(l, j, [hd]))
                branch_out(l, "w_c_out", C_GC, N, False, False)
                fox_o_transpose(N)
                branch_out(l, "w_b_out", C_GB, N, False, True)
                if DBG == "hgrn":
                    continue
                ffn_and_wo(l, N)
            store_y_tile(yp[tok0:tok0 + T, :], N)


        def sample_tile():
            N = NSS * LS
            for i_ in range(2):
                kct = ksrc.tiles[i_][:].rearrange("p a (b k) -> p (a b) k", k=128)
                vct = vsrc.tiles[i_][:].rearrange("p a b w -> p (a b) w")
                S.pool(lambda e, kct=kct: e.memset(kct[64:68, :, :], 0.0), writes=[ksrc.bufs[i_]])
                S.pool(lambda e, kct=kct: e.memset(kct[64:66, :, :], 1.0), writes=[ksrc.bufs[i_]])
                S.pool(lambda e, vct=vct: e.memset(vct[:, :, 64:VW], 1.0), writes=[ksrc.bufs[i_]])
            load_x_tile(xs, N)
            ub4 = ubuf[:, :, 0:4 * 46].rearrange("p c (s w) -> p c s w", w=46)
            for l in range(L):
                rmsnorm(g1, l, N)
                for s_ in range(NSS):
                    stt, bst_ = stg.get()
                    S.dma(PQ, None, stt[0:30, 0:512], sconv[l, s_], writes=[bst_])
                    ps, bp = psg.get()
                    for c in range(4):
                        S.pe(lambda e, c=c: e.transpose(ps[:, c * 32:c * 32 + 30], stt[0:30, c * 128:(c + 1) * 128],
                                                        ident[0:30, 0:30]), reads=[bst_, Bc], writes=[bp])
                    S.act(lambda e, s_=s_: e.activation(ub4[:, :, s_, 0:30],
                                                        ps[:, 0:128].rearrange("p (c w) -> p c w", w=32)[:, :, 0:30],
                                                        AF.Copy), reads=[bp], writes=Bub)

                def segs(l_, c, psv, bpv, sg, bsg):
                    S.dve(lambda e: e.tensor_tensor(ub4[:, c, :, 30:46],
                                                    psv[:, 0:N].rearrange("p (s t) -> p s t", t=LS),
                                                    sg[:, 0:N].rearrange("p (s t) -> p s t", t=LS), ALU.mult),
                          reads=[bpv, bsg], writes=[Bub[c]])
                    y4 = yc[:, c, 0:N].rearrange("p (s t) -> p s t", t=LS)
                    S.dve(lambda e: e.tensor_scalar(y4, ub4[:, c, :, 0:LS], cw[:, l_, c, 0:1], cb[:, l_, c:c + 1],
                                                    ALU.mult, ALU.add), reads=[Bub[c], Bc], writes=[Byc[c]])
                    for j in range(1, 31):
                        S.dve(lambda e, j=j: e.scalar_tensor_tensor(y4, ub4[:, c, :, j:j + LS], cw[:, l_, c, j:j + 1],
                                                                    y4, ALU.mult, ALU.add),
                              reads=[Bub[c]], writes=[Byc[c]])
                conv_branch(l, N, segs)
                for s_ in range(NSS):
                    ps, bp = psg.get()
                    for c in range(4):
                        S.pe(lambda e, c=c, s_=s_: e.transpose(ps[0:30, c * 128:(c + 1) * 128], ub4[:, c, s_, 16:46],
                                                               ident[:]), reads=[Bub[c], Bc], writes=[bp])
                    stt, bst_ = stg.get()
                    S.act(lambda e: e.activation(stt[0:30, 0:512], ps[0:30, :], AF.Copy), reads=[bp], writes=[bst_])
                    S.dma(PQ, None, cs[l, s_], stt[0:30, 0:512], reads=[bst_])
                conv_post(l, N)
                branch_out(l, "w_a_out", C_GA, N, True, False)
                wK, bK, wV, bV = fox_qkv(l, N, 0, None, None, None)
                for s_ in range(NSS):
                    fox_scan_rows(l, N, s_ * LS, LS, 0.0, [])
                fox_split_rows(N)

                def va_fn(i, n, stt, bst_):
                    S.act(lambda e: e.activation(Va[0:n, :, i, 0:64],
                                                 stt[0:n, 512:1024].rearrange("p (h d) -> p h d", d=64), AF.Copy),
                          reads=[bst_], writes=[BV])

                def kdst(i, n, stt, bst_):
                    S.dma(PQ, None, ks[l].rearrange("s t c -> (s t) c"), stt[0:n, 0:512], reads=[bst_])

                def vdst(i, n, stt, bst_):
                    S.dma(PQ, None, vs[l].rearrange("s t c -> (s t) c"), stt[0:n, 512:1024], reads=[bst_])

                def lfdst(i, n):
                    S.dma(PQ, lfsem, lfs[l].rearrange("s t h -> (s t) h"), lftok[0:n, 0, :], reads=[Blftok])
                fox_tok_outputs(l, N, wK, bK, wV, bV, kdst, vdst, lfdst, va_fn)
                fox_attend_sample(l)
                for fc in range(4):
                    ps, bp = psg.get()
                    for s_ in range(NSS):
                        S.pe(lambda e, s_=s_, fc=fc: e.transpose(ps[:, s_ * LS:(s_ + 1) * LS],
                                                                 otok[0:LS, s_, fc * 128:(fc + 1) * 128],
                                                                 ident[0:LS, 0:LS]),
                             reads=[Botok[s_], Bc], writes=[bp])
                    S.act(lambda e, fc=fc: e.activation(brT[:, fc, 0:N], ps[:, 0:N], AF.Copy), reads=[bp],
                          writes=[Bbr[fc]])
                branch_out(l, "w_b_out", C_GB, N, False, False)
                def S_in(hd, si):
                    t_, b_ = Ssm.get()
                    S.dma(PQ, None, t_[:], shg[l, si, hd], writes=[b_])
                    return t_[:], b_

                def S_out(hd, si, Sap, BSt):
                    S.dma(PQ, None, hs[l, si, hd], Sap, reads=[BSt])
                hgrn_branch(l, N, LS, [(s_, 1, s_) for s_ in range(NSS)], S_in, S_out)
                branch_out(l, "w_c_out", C_GC, N, False, True)
                ffn_and_wo(l, N)
            store_y_tile(ys, N)

        def fox_attend_sample(l):
            N = NSS * LS
            psn, bpn = psS.get()
            for hh in range(8):
                S.pe(lambda e, hh=hh: e.matmul(psn[0:N, hh * N:(hh + 1) * N], lhsT=Ka[:, hh, 0:N], rhs=Qa[:, hh, 0:N],
                                               start=True, stop=True), reads=[BK[hh], BQ[hh]], writes=[bpn])
            Pn, bPn = sqo, Bsqo
            S.act(lambda e: e.activation(Pn[0:N, :], psn[0:N, :], AF.Exp), reads=[bpn], writes=[bPn])
            S.dve(lambda e: e.tensor_tensor(Pn[0:N, :].rearrange("p (h q) -> p h q", q=N),
                                            Pn[0:N, :].rearrange("p (h q) -> p h q", q=N),
                                            bmask[:, :].unsqueeze(1).to_broadcast([N, 8, N]), ALU.mult),
                  reads=[Bc], writes=[bPn])
            for s_ in range(NSS):
                S.dma(PQ, None, lfc[:], clf[l, s_].rearrange("(kt p) h -> p kt h", p=128), writes=[Brk])
                lfc2 = lfc[:].rearrange("p k h -> p (k h)")
                ps1, bp1 = psg.get()
                S.pe(lambda e: e.matmul(ps1[:, 0:128], lhsT=slm[:], rhs=lfc2, start=True, stop=True),
                     reads=[Brk, Bc], writes=[bp1])
                ps2, bp2 = psg.get()
                S.pe(lambda e: e.matmul(ps2[:, 0:128], lhsT=ones[:, 0:128], rhs=lfc2, start=True, stop=True),
                     reads=[Brk, Bc], writes=[bp2])
                S.dve(lambda e: e.tensor_copy(rkt[:].rearrange("p k h -> p (k h)"), ps2[:, 0:128]), reads=[bp2],
                      writes=[Brk])
                S.dve(lambda e: e.tensor_copy(rk[:].rearrange("p k h -> p (k h)"), ps1[:, 0:128]), reads=[bp1],
                      writes=[Brk])
                S.dve(lambda e: e.tensor_copy(lfc[:, 15, :], rkt[:, 15, :]), writes=[Brk])
                S.dve(lambda e: e.memset(rkt[:, 15, :], 0.0), writes=[Brk])
                for kt in range(14, -1, -1):
                    S.dve(lambda e, kt=kt: e.tensor_copy(lfc[:, kt, :], rkt[:, kt, :]), writes=[Brk])
                    S.dve(lambda e, kt=kt: e.tensor_tensor(rkt[:, kt, :], rkt[:, kt + 1, :], lfc[:, kt + 1, :], ALU.add),
                          writes=[Brk])
                S.dve(lambda e: e.tensor_tensor(rk[:], rk[:], rkt[:], ALU.add), writes=[Brk])
                first = [True, True]
                for kt in range(NPAST // 128):
                    stt, bst_ = stg.get()
                    S.dma_group(PQ, None if False else S.auto_sem((), [bst_]),
                                [(stt[:, 0:512], ck[l, s_, kt * 128:(kt + 1) * 128, :]),
                                 (stt[:, 512:1024], cv[l, s_, kt * 128:(kt + 1) * 128, :])], writes=[bst_])
                    kct_t, bkc = ksrc.get()
                    vct_t, _u = vsrc.get()
                    kct = kct_t[:].rearrange("p a (b k) -> p (a b) k", k=128)
                    vct = vct_t[:].rearrange("p a b w -> p (a b) w")
                    for half in range(2):
                        pst_, bpt = psg.get()
                        for hq in range(4):
                            hh = half * 4 + hq
                            S.pe(lambda e, hh=hh, hq=hq, pst_=pst_, stt=stt: e.transpose(
                                pst_[0:64, hq * 128:(hq + 1) * 128], stt[:, hh * 64:(hh + 1) * 64], ident[:]),
                                 reads=[bst_, Bc], writes=[bpt])
                        S.act(lambda e, half=half, pst_=pst_, kct=kct: e.activation(
                            kct[0:64, half * 4:half * 4 + 4, :], pst_[0:64, :].rearrange("p (h k) -> p h k", k=128),
                            AF.Copy), reads=[bpt], writes=[bkc])
                    S.dve(lambda e, vct=vct, stt=stt: e.tensor_copy(
                        vct[:, :, 0:64], stt[:, 512:1024].rearrange("p (h d) -> p h d", d=64)),
                          reads=[bst_], writes=[bkc])
                    pss, bps = psS.get()
                    for hh in range(8):
                        S.pe(lambda e, hh=hh, pss=pss, kct=kct: e.matmul(
                            pss[:, hh * LS:(hh + 1) * LS], lhsT=kct[:, hh, :], rhs=Qa[:, hh, s_ * LS:(s_ + 1) * LS],
                            start=True, stop=True), reads=[bkc, BQ[hh]], writes=[bps])
                    Pt, bPt = Prot.get()
                    Pf, bPf = ft.get()
                    S.dve(lambda e, kt=kt, pss=pss, Pf=Pf: e.tensor_tensor(
                        Pf[:, 0:128].rearrange("p (h q) -> p h q", q=LS),
                        pss[:, 0:128].rearrange("p (h q) -> p h q", q=LS),
                        rk[:, kt, :].unsqueeze(2).to_broadcast([128, 8, LS]), ALU.add),
                          reads=[bps, Brk], writes=[bPf])
                    S.act(lambda e, Pt=Pt, Pf=Pf: e.activation(Pt[:, 0:128], Pf[:, 0:128], AF.Exp), reads=[bPf],
                          writes=[bPt])
                    for hh in range(8):
                        b_ = hh // 4
                        S.pe(lambda e, hh=hh, b_=b_, Pt=Pt, vct=vct, f=first[b_]: e.matmul(
                            psO[b_][0:LS, hh % 4, 0:65], lhsT=Pt[:, hh * LS:(hh + 1) * LS], rhs=vct[:, hh, 0:65],
                            start=f, stop=False, skip_group_check=True), reads=[bPt, bkc], writes=[BpsO[b_]])
                        first[b_] = False
                for hh in range(8):
                    b_ = hh // 4
                    S.pe(lambda e, hh=hh, b_=b_: e.matmul(
                        psO[b_][0:LS, hh % 4, 0:65], lhsT=Pn[0:N, hh * N + s_ * LS: hh * N + (s_ + 1) * LS],
                        rhs=Va[0:N, hh, 0, 0:65], start=False, stop=True, skip_group_check=True),
                         reads=[bPn, BV], writes=[BpsO[b_]])
                for b_ in range(2):
                    r, br = rc.get()
                    S.dve(lambda e, b_=b_, r=r: e.reciprocal(r[0:LS, :], psO[b_][0:LS, :, 64]), reads=[BpsO[b_]],
                          writes=[br])
                    for hq in range(4):
                        hh = b_ * 4 + hq
                        S.dve(lambda e, b_=b_, hq=hq, hh=hh, r=r: e.tensor_scalar(
                            otok[0:LS, s_, hh * 64:(hh + 1) * 64], psO[b_][0:LS, hq, 0:64], r[0:LS, hq:hq + 1], None,
                            ALU.mult), reads=[BpsO[b_], br], writes=[Botok[s_]])

        if PROMPT and DBG != "init":
            for j in range(NT):
                prompt_tile(j)

        if SAMPLE:
            sample_tile()

        with nc.allow_non_contiguous_dma(reason="small strided param/state transfers"):
            S.emit()
    return nc, S


_CACHE = {}


def kernel(**inputs):
    inp = {k: np.ascontiguousarray(np.asarray(v, dtype=np.float32)) for k, v in inputs.items()}
    L = inp["w_in"].shape[0]
    B, SEQ, _ = inp["x_prompt"].shape
    key = (SEQ, L)
    if key not in _CACHE:
        _CACHE[key] = build_program(SEQ=SEQ, L=L)[0]
    nc = _CACHE[key]
    ncores = 8
    in_maps = []
    for c in range(ncores):
        m = {
            "xp": inp["x_prompt"][c % B],
            "xs": inp["x_sample"][4 * c:4 * c + 4].reshape(NSS * LS, D),
            "ck": inp["cache_fox_k"][:, 4 * c:4 * c + 4].reshape(L, NSS, NPAST, 512),
            "cv": inp["cache_fox_v"][:, 4 * c:4 * c + 4].reshape(L, NSS, NPAST, 512),
            "clf": inp["cache_fox_logf"][:, 4 * c:4 * c + 4],
            "sconv": inp["state_conv"][:, 4 * c:4 * c + 4],
            "shg": inp["state_hgrn"][:, 4 * c:4 * c + 4],
        }
        for n in WNAMES:
            m[n] = inp[n] if n != "final_g" else inp[n].reshape(1, D)
        in_maps.append({k: np.ascontiguousarray(v) for k, v in m.items()})
    res = run_bass_kernel_spmd(nc, in_maps, core_ids=list(range(ncores)))
    R = res.results
    y_prompt = np.stack([R[b]["yp"] for b in range(B)])
    y_sample = np.concatenate([R[c]["ys"].reshape(NSS, LS, D) for c in range(ncores)], axis=0)
    kpo = np.stack([R[b]["kp"] for b in range(B)], axis=1).reshape(L, B, SEQ, 8, 64)
    vpo = np.stack([R[b]["vp"] for b in range(B)], axis=1).reshape(L, B, SEQ, 8, 64)
    lfpo = np.stack([R[b]["lfp"] for b in range(B)], axis=1)
    cpo = np.stack([R[b]["cp"] for b in range(B)], axis=1)
    hpo = np.stack([R[b]["hp"] for b in range(B)], axis=1)
    kso = np.concatenate([R[c]["ks"] for c in range(ncores)], axis=1).reshape(L, 4 * ncores, LS, 8, 64)
    vso = np.concatenate([R[c]["vs"] for c in range(ncores)], axis=1).reshape(L, 4 * ncores, LS, 8, 64)
    lfso = np.concatenate([R[c]["lfs"] for c in range(ncores)], axis=1)
    cso = np.concatenate([R[c]["cs"] for c in range(ncores)], axis=1)
    hso = np.concatenate([R[c]["hs"] for c in range(ncores)], axis=1)
    return (y_prompt, y_sample, kpo, vpo, lfpo, cpo, hpo, kso, vso, lfso, cso, hso)
```

```python
import contextlib
import numpy as np
import concourse.bass as bass
import concourse.mybir as mybir
from concourse.bass_utils import run_bass_kernel_spmd

F32 = mybir.dt.float32
BF16 = mybir.dt.bfloat16
ALU = mybir.AluOpType
AF = mybir.ActivationFunctionType
AX = mybir.AxisListType

ENGS = ("pe", "act", "dve", "pool", "sp")
SAME_ENGINE_SYNC = {"pool", "dve", "act"}
EPS = 1e-6


class Buf:
    __slots__ = ("w", "r", "name", "dw", "dr")

    def __init__(self, name=""):
        self.w = None
        self.r = {}
        self.name = name
        self.dw = None
        self.dr = None


class DmaSem:
    def __init__(self, handle):
        self.h = handle
        self.n = 0


class _Rec:
    def __init__(self):
        self.call = None

    def __getattr__(self, name):
        def f(*a, **k):
            self.call = (name, a, k)
            return self
        return f


class Sched:
    def __init__(self, nc, stack):
        self.nc = nc
        self.stack = stack
        self.ops = {e: [] for e in ENGS}
        self.count = {e: 0 for e in ENGS}
        self.seen = {e: {} for e in ENGS}
        self.esem = {e: stack.enter_context(nc.semaphore("es_" + e)) for e in ENGS}
        self.dsems = []
        self.nwaits = 0

    def dma_sem(self, name):
        d = DmaSem(self.stack.enter_context(self.nc.semaphore(name)))
        self.dsems.append(d)
        return d

    def op(self, eng, fn, reads=(), writes=(), dsem=None, raw=()):
        waits = {}
        seen = self.seen[eng]

        def need(k, v):
            if dsem is None and k == eng and eng not in SAME_ENGINE_SYNC:
                return
            if seen.get(k, 0) >= v:
                return
            if waits.get(k, 0) < v:
                waits[k] = v

        for b in reads:
            if b.w is not None:
                need(*b.w)
        for b in raw:
            if b.w is not None:
                need(*b.w)
        for b in writes:
            if b.w is not None:
                need(*b.w)
            for k, v in b.r.items():
                need(k, v)
        for k, v in waits.items():
            seen[k] = v
        if dsem is None:
            self.count[eng] += 1
            ev = (eng, self.count[eng])
        else:
            dsem.n += 16
            ev = (dsem, dsem.n)
        for b in reads:
            if b.r.get(ev[0], 0) < ev[1]:
                b.r[ev[0]] = ev[1]
        for b in writes:
            b.w = ev
            b.r = {}
        self.nwaits += len(waits)
        rec = _Rec()
        fn(rec)
        self.ops[eng].append((list(waits.items()), rec.call, ev))

    def pe(self, fn, reads=(), writes=()):
        self.op("pe", fn, reads, writes)

    def act(self, fn, reads=(), writes=()):
        self.op("act", fn, reads, writes)

    def dve(self, fn, reads=(), writes=()):
        self.op("dve", fn, reads, writes)

    def pool(self, fn, reads=(), writes=()):
        self.op("pool", fn, reads, writes)

    def auto_sem(self, reads, writes):
        if writes:
            b = writes[0]
            if b.dw is None:
                b.dw = self.dma_sem("dw%d" % len(self.dsems))
            return b.dw
        b = reads[0]
        if b.dr is None:
            b.dr = self.dma_sem("dr%d" % len(self.dsems))
        return b.dr

    def dma(self, q, dsem, out, in_, reads=(), writes=(), raw=(), **kw):
        if dsem is None:
            dsem = self.auto_sem(reads, writes)
        self.op(q, lambda e: e.dma_start(out=out, in_=in_, **kw), reads, writes, dsem=dsem, raw=raw)

    def dma_group(self, q, dsem, pairs, reads=(), writes=(), raw=(), **kw):
        for i, (out, in_) in enumerate(pairs):
            if i == 0:
                self.op(q, lambda e, out=out, in_=in_: e.dma_start(out=out, in_=in_, **kw), reads, writes, dsem=dsem,
                        raw=raw)
            else:
                self.op(q, lambda e, out=out, in_=in_: e.dma_start(out=out, in_=in_, **kw), (), (), dsem=dsem)
        ev = (dsem, dsem.n)
        for b in reads:
            b.r[dsem] = dsem.n
        for b in writes:
            b.w = ev
            b.r = {}

    def emit(self):
        nc = self.nc
        fin = []
        for e in ENGS:
            if self.count[e]:
                fin.append((e, self.count[e]))
        for d in self.dsems:
            if d.n:
                fin.append((d, d.n))

        def semh(k):
            return self.esem[k] if isinstance(k, str) else k.h

        def run(engname, engobj):
            for waits, fn, ev in self.ops[engname]:
                for k, v in waits:
                    engobj.wait_ge(semh(k), v)
                ins = getattr(engobj, fn[0])(*fn[1], **fn[2])
                if isinstance(ev[0], str):
                    ins.then_inc(self.esem[ev[0]], 1)
                else:
                    ins.then_inc(ev[0].h, 16)
            if engname == "sp":
                for k, v in fin:
                    engobj.wait_ge(semh(k), v)

        with nc.Block() as block:
            @block.tensor
            def _(e):
                run("pe", e)

            @block.scalar
            def _(e):
                run("act", e)

            @block.vector
            def _(e):
                run("dve", e)

            @block.gpsimd
            def _(e):
                run("pool", e)

            @block.sync
            def _(e):
                run("sp", e)


class Rot:
    def __init__(self, tiles):
        self.tiles = tiles
        self.bufs = [Buf() for _ in tiles]
        self.i = 0

    def get(self):
        k = self.i % len(self.tiles)
        self.i += 1
        return self.tiles[k], self.bufs[k]


D = 1024
INW = 7688
NPAST = 2048
NSS = 4
LS = 16
VW = 68
C_AV, C_AG, C_Q, C_K, C_V, C_F = 0, 512, 1024, 1536, 2048, 2560
C_HQ, C_HF, C_HI, C_HO = 2568, 3080, 3592, 4104
C_GA, C_GB, C_GC = 4616, 5640, 6664

WNAMES = ["norm1_g", "w_in", "conv_w", "conv_b", "conv_ln_g", "conv_ln_b", "w_a_out", "fox_bf", "w_b_out",
          "hgrn_lb_param", "hgrn_norm_g", "w_c_out", "w_o", "norm2_g", "w_up", "w_down", "final_g"]


def build_program(SEQ=8192, L=4, T=512, SAMPLE=True, PROMPT=True):
    nc = bass.Bass("TRN2", target_bir_lowering=False)
    NT = SEQ // T
    NSUB = T // 128
    NCH = T // 64

    def din(name, shape):
        return nc.dram_tensor(name, shape, F32, kind="ExternalInput").ap()

    def dout(name, shape):
        return nc.dram_tensor(name, shape, F32, kind="ExternalOutput").ap()

    xp = din("xp", [SEQ, D])
    xs = din("xs", [NSS * LS, D])
    ck = din("ck", [L, NSS, NPAST, 512])
    cv = din("cv", [L, NSS, NPAST, 512])
    clf = din("clf", [L, NSS, NPAST, 8])
    sconv = din("sconv", [L, NSS, 30, 512])
    shg = din("shg", [L, NSS, 4, 128, 128])
    Wd = {
        "norm1_g": din("norm1_g", [L, D]), "w_in": din("w_in", [L, D, INW]), "conv_w": din("conv_w", [L, 31, 512]),
        "conv_b": din("conv_b", [L, 512]), "conv_ln_g": din("conv_ln_g", [L, 512]),
        "conv_ln_b": din("conv_ln_b", [L, 512]), "w_a_out": din("w_a_out", [L, 512, D]),
        "fox_bf": din("fox_bf", [L, 8]), "w_b_out": din("w_b_out", [L, 512, D]),
        "hgrn_lb_param": din("hgrn_lb_param", [L, 512]), "hgrn_norm_g": din("hgrn_norm_g", [L, 128]),
        "w_c_out": din("w_c_out", [L, 512, D]), "w_o": din("w_o", [L, D, D]), "norm2_g": din("norm2_g", [L, D]),
        "w_up": din("w_up", [L, D, 4096]), "w_down": din("w_down", [L, 4096, D]), "final_g": din("final_g", [1, D]),
    }
    yp = dout("yp", [SEQ, D])
    ys = dout("ys", [NSS * LS, D])
    kp = dout("kp", [L, SEQ, 512])
    vp = dout("vp", [L, SEQ, 512])
    lfp = dout("lfp", [L, SEQ, 8])
    cp = dout("cp", [L, 30, 512])
    hp = dout("hp", [L, 4, 128, 128])
    ks = dout("ks", [L, NSS, LS, 512])
    vs = dout("vs", [L, NSS, LS, 512])
    lfs = dout("lfs", [L, NSS, LS, 8])
    cs = dout("cs", [L, NSS, 30, 512])
    hs = dout("hs", [L, NSS, 4, 128, 128])
    kscr = nc.dram_tensor("kscr", [L, 68, 8, SEQ], BF16, kind="Internal").ap()
    vscr = nc.dram_tensor("vscr", [L, 128, 8, SEQ // 128, VW], BF16, kind="Internal").ap()

    with contextlib.ExitStack() as st:
        S = Sched(nc, st)
        _n = [0]

        def sb(shape, dt, name=None):
            _n[0] += 1
            return st.enter_context(nc.sbuf_tensor(name or ("t%d" % _n[0]), shape, dt))

        def pst(shape, dt, name=None):
            _n[0] += 1
            return st.enter_context(nc.psum_tensor(name or ("p%d" % _n[0]), shape, dt))

        xT = sb([128, 8, T], F32, "xT")
        Bx = [Buf() for _ in range(8)]
        hT = sb([128, 8, T], BF16, "hT")
        Bh = [Buf() for _ in range(8)]
        sq = sb([128, 8, T], BF16, "sq")
        Bsq = [Buf() for _ in range(8)]
        mT = sb([128, 8, T], F32, "mT")
        Bm = [Buf() for _ in range(8)]
        brT = sb([128, 4, T], BF16, "brT")
        Bbr = [Buf() for _ in range(4)]
        stg = Rot([sb([128, 1024], F32, "stage%d" % i) for i in range(2)])
        NW = 4
        WSZ = 8 * 520
        wrot = Rot([sb([128, WSZ], BF16, "w%d" % i) for i in range(NW)])
        wsem = [S.dma_sem("wsem%d" % i) for i in range(NW)]
        ft = Rot([sb([128, T], F32, "ft%d" % i) for i in range(8)])
        psg = Rot([pst([128, 512], F32, "psg%d" % i) for i in range(4)])
        psS = Rot([pst([128, 512], F32, "psS%d" % i) for i in range(2)])
        psO = [pst([128, 4, 128], F32, "psO%d" % i) for i in range(2)]
        BpsO = [Buf(), Buf()]
        ident = sb([128, 128], F32, "ident")
        triu = sb([128, 128], F32, "triu")
        triub = sb([128, 128], BF16, "triub")
        onesD = sb([128, 128], BF16, "onesD")
        onesC = sb([128, 128], BF16, "onesC")
        onesV = sb([128, 128], BF16, "onesV")
        ones = sb([128, T], F32, "ones")
        g1 = sb([128, L, 8], F32, "g1")
        g2 = sb([128, L, 8], F32, "g2")
        gf = sb([128, 8], F32, "gf")
        cw = sb([128, L, 4, 31], F32, "cw")
        cb = sb([128, L, 4], F32, "cb")
        lng = sb([128, L, 4], F32, "lng")
        lnb = sb([128, L, 4], F32, "lnb")
        nbf = sb([8, L], F32, "nbf")
        bfb = sb([128, L, 8], F32, "bfb")
        lbp = sb([128, 4, L], F32, "lbp")
        lb = sb([128, 4, L], F32, "lb")
        oml = sb([128, 4, L], F32, "oml")
        noml = sb([128, 4, L], F32, "noml")
        lbt = sb([128, 4, L], F32, "lbt")
        lbm = sb([128, 4], F32, "lbm")
        hng = sb([128, L], F32, "hng")
        epsb = sb([128, 1], F32, "epsb")
        Bc = Buf()
        csem = S.dma_sem("csem")
        hist = sb([128, L, 4, 30], F32, "hist")
        Bhist = [Buf() for _ in range(L)]
        Sst = sb([128, L, 4, 128], F32, "Sst")
        BS = [[Buf() for _ in range(4)] for _ in range(L)]
        Sbf = Rot([sb([128, 128], BF16, "Sbf%d" % i) for i in range(2)])
        ccar = sb([8, L], F32, "ccar")
        Bccar = [Buf() for _ in range(L)]
        ubuf = sb([128, 4, 30 + T], F32, "ubuf")
        Bub = [Buf() for _ in range(4)]
        yc = sb([128, 4, T], F32, "yc")
        Byc = [Buf() for _ in range(4)]
        ybf = sq[:, 0:4, :]
        ysq = sq[:, 4:8, :]
        Bybf = Bsq[0:4]
        Bysq = Bsq[4:8]
        Qa = sb([68, 8, T], BF16, "Qa")
        Ka = sb([68, 8, T], BF16, "Ka")
        BQ = [Buf() for _ in range(8)]
        BK = [Buf() for _ in range(8)]
        Va = sb([128, 8, NSUB, VW], BF16, "Va")
        BV = Buf()
        ksrc = Rot([sb([68, 2, 512], BF16, "ksrc%d" % i) for i in range(2)])
        vsrc = Rot([sb([128, 2, 4, VW], BF16, "vsrc%d" % i) for i in range(2)])
        kvsem = [S.dma_sem("kvsem%d" % i) for i in range(2)]
        Prot = Rot([sb([128, 512], BF16, "P%d" % i) for i in range(3)])
        otok = yc
        Botok = Byc
        rc = Rot([sb([128, 4], F32, "rc%d" % i) for i in range(2)])
        fT = sb([8, T], F32, "fT")
        csp = sb([8, T], F32, "csp")
        chi = sb([8, T], BF16, "chi")
        clo = sb([8, T], BF16, "clo")
        nhi = sb([8, T], BF16, "nhi")
        nlo = sb([8, T], BF16, "nlo")
        Bf = Buf()
        lftok = sb([128, NSUB, 8], F32, "lftok")
        Blftok = Buf()
        scrsem = S.dma_sem("scrsem")
        Bkscr = [[Buf() for _ in range(NT + 1)] for _ in range(L)]
        Bvscr = [[Buf() for _ in range(NT + 1)] for _ in range(L)]
        osem = S.dma_sem("osem")
        stsem = S.dma_sem("stsem")
        lfsem = S.dma_sem("lfsem")
        ksem_ = S.dma_sem("kscrsem")
        vsem_ = S.dma_sem("vscrsem")
        rowsem = S.dma_sem("rowsem")
        xsem = S.dma_sem("xsem")
        qs = sb([128, T], F32, "qs")
        kk = sb([128, T], F32, "kk")
        bg = sb([128, T], F32, "bg")
        Bqs, Bkk, Bbg = Buf(), Buf(), Buf()
        qt = sb([128, T], BF16, "qt")
        kt_ = sb([128, T], BF16, "kt_")
        qh = sb([128, T], BF16, "qh")
        kh = sb([128, T], F32, "kh")
        Bqt, Bkt, Bqh, Bkh = Buf(), Buf(), Buf(), Buf()
        vtok = sb([64, NCH, 512], BF16, "vtok")
        Bvtok = [Buf() for _ in range(NCH)]
        khtok = sb([64, NCH, 128], BF16, "khtok")
        Bkhtok = [Buf() for _ in range(NCH)]
        Abf = sb([64, NCH, 64], BF16, "Abf")
        BAbf = Buf()
        sqo = sb([128, T], BF16, "sqo")
        Bsqo = Buf()

        slm = sb([128, 128], F32, "slm")
        bmask = sb([64, 64], BF16, "bmask")
        lfc = sb([128, 16, 8], F32, "lfc")
        rk = sb([128, 16, 8], F32, "rk")
        rkt = sb([128, 16, 8], F32, "rkt")
        Brk = Buf()
        Ssm = Rot([sb([128, 128], F32, "Ssm%d" % i) for i in range(2)])

        PQ = "sp"
        WQ = "pool"

        def init_consts():
            S.pool(lambda e: e.memset(ident[:], 1.0), writes=[Bc])
            S.pool(lambda e: e.affine_select(ident[:], ident[:], [[-1, 128]], ALU.is_equal, 0.0, base=0,
                                             channel_multiplier=1), writes=[Bc])
            S.pool(lambda e: e.memset(triu[:], 1.0), writes=[Bc])
            S.pool(lambda e: e.affine_select(triu[:], triu[:], [[1, 128]], ALU.is_ge, 0.0, base=0,
                                             channel_multiplier=-1), writes=[Bc])
            S.pool(lambda e: e.tensor_copy(triub[:], triu[:]), writes=[Bc])
            S.pool(lambda e: e.memset(slm[:], 1.0), writes=[Bc])
            S.pool(lambda e: e.affine_select(slm[:], slm[:], [[-1, 128]], ALU.is_gt, 0.0, base=0,
                                             channel_multiplier=1), writes=[Bc])
            S.pool(lambda e: e.tensor_copy(bmask[:], triu[0:64, 0:64]), writes=[Bc])
            for b_ in range(1, 4):
                S.pool(lambda e, b_=b_: e.memset(bmask[0:16 * b_, 16 * b_:16 * b_ + 16], 0.0), writes=[Bc])
            S.pool(lambda e: e.memset(onesD[:], 1.0 / 1024), writes=[Bc])
            S.pool(lambda e: e.memset(onesC[:], 1.0 / 512), writes=[Bc])
            S.pool(lambda e: e.memset(onesV[:], 1.0 / 128), writes=[Bc])
            S.pool(lambda e: e.memset(ones[:], 1.0), writes=[Bc])
            S.pool(lambda e: e.memset(epsb[:], EPS), writes=[Bc])
            S.pool(lambda e: e.memset(Qa[64:68, :, :], 1.0), writes=BQ)
            S.pool(lambda e: e.memset(Ka[64:68, :, :], 1.0), writes=BK)
            S.pool(lambda e: e.memset(Va[:, :, :, 64:VW], 1.0), writes=[BV])
            S.pool(lambda e: e.memset(hist[:], 0.0), writes=Bhist)
            S.pool(lambda e: e.memset(Sst[:], 0.0), writes=[b for bl in BS for b in bl])
            S.pool(lambda e: e.memset(ccar[:], 0.0), writes=Bccar)
            W = Wd
            S.dma(PQ, csem, g1[:], W["norm1_g"].rearrange("l (c p) -> p l c", p=128), writes=[Bc])
            S.dma(PQ, csem, g2[:], W["norm2_g"].rearrange("l (c p) -> p l c", p=128), writes=[Bc])
            S.dma(PQ, csem, gf[:], W["final_g"][0].rearrange("(c p) -> p c", p=128), writes=[Bc])
            for l in range(L):
                for c in range(4):
                    S.dma(PQ, csem, cw[:, l, c, :], W["conv_w"][l][:, c * 128:(c + 1) * 128].rearrange("j p -> p j"),
                          writes=[Bc])
            S.dma(PQ, csem, cb[:], W["conv_b"].rearrange("l (c p) -> p l c", p=128), writes=[Bc])
            S.dma(PQ, csem, lng[:], W["conv_ln_g"].rearrange("l (c p) -> p l c", p=128), writes=[Bc])
            S.dma(PQ, csem, lnb[:], W["conv_ln_b"].rearrange("l (c p) -> p l c", p=128), writes=[Bc])
            S.dma(PQ, csem, nbf[:], W["fox_bf"].rearrange("l h -> h l"), writes=[Bc])
            for l in range(L):
                S.dma(PQ, csem, bfb[:, l:l + 1, :], W["fox_bf"][l:l + 1, :].partition_broadcast(128), writes=[Bc])
            for hd_ in range(4):
                S.dma(PQ, csem, lbp[:, hd_, :], W["hgrn_lb_param"][:, hd_ * 128:(hd_ + 1) * 128].rearrange("l p -> p l"),
                      writes=[Bc])
            S.dma(PQ, csem, hng[:], W["hgrn_norm_g"].rearrange("l v -> v l"), writes=[Bc])
            S.dve(lambda e: e.tensor_scalar(nbf[:], nbf[:], -1.0, None, ALU.mult), reads=[Bc], writes=[Bc])
            S.dve(lambda e: e.tensor_reduce(lbm[:], lbp[:], AX.X, ALU.max), reads=[Bc], writes=[Bc])
            S.dve(lambda e: e.tensor_tensor(lbt[:], lbp[:], lbm[:].unsqueeze(2).to_broadcast([128, 4, L]),
                                            ALU.subtract), writes=[Bc])
            S.act(lambda e: e.activation(lbt[:], lbt[:], AF.Exp), reads=[Bc], writes=[Bc])
            S.dve(lambda e: e.tensor_reduce(lbm[:], lbt[:], AX.X, ALU.add), reads=[Bc], writes=[Bc])
            S.dve(lambda e: e.reciprocal(lbm[:], lbm[:]), writes=[Bc])
            S.dve(lambda e: e.tensor_tensor(lbt[:], lbt[:], lbm[:].unsqueeze(2).to_broadcast([128, 4, L]),
                                            ALU.mult), writes=[Bc])
            S.dve(lambda e: e.memset(lb[:], 0.0), writes=[Bc])
            for l in range(1, L):
                S.dve(lambda e, l=l: e.tensor_tensor(lb[:, :, l:l + 1], lb[:, :, l - 1:l], lbt[:, :, l:l + 1],
                                                     ALU.add), writes=[Bc])
            S.dve(lambda e: e.tensor_scalar(oml[:], lb[:], -1.0, 1.0, ALU.mult, ALU.add), writes=[Bc])
            S.dve(lambda e: e.tensor_scalar(noml[:], oml[:], -1.0, None, ALU.mult), writes=[Bc])

        def rsqrt_eps(out_ap, in_ap, reads, wbuf):
            S.act(lambda e: e.activation(out_ap, in_ap, AF.Sqrt, bias=epsb[:, 0:1]), reads=list(reads) + [Bc],
                  writes=[wbuf])
            S.dve(lambda e: e.reciprocal(out_ap, out_ap), writes=[wbuf])

        def wload(src2d, kc, cols):
            best, bi_ = None, None
            for i_ in range(NW):
                b_ = wrot.bufs[i_]
                if b_.w is not None and not b_.r:
                    continue
                key = b_.r.get("pe", 0)
                if best is None or key < best:
                    best, bi_ = key, i_
            assert bi_ is not None, "no free weight slot"
            t, b = wrot.tiles[bi_], wrot.bufs[bi_]
            sem = wsem[bi_]
            view = t[:, 0:kc * cols].rearrange("p (k c) -> p k c", c=cols)
            S.dma(WQ, sem, view, src2d.rearrange("(k p) c -> p k c", p=128), writes=[b])
            return view, b

        def mm_fm(ps, M, N, wt, wb, col0, nk, act, Bact, extra_w=()):
            for k in range(nk):
                S.pe(lambda e, k=k: e.matmul(ps[0:M, 0:N], lhsT=wt[:, k, col0:col0 + M], rhs=act[:, k, 0:N],
                                             start=(k == 0), stop=(k == nk - 1)),
                     reads=[wb, Bact[k]], writes=list(extra_w))

        def rmsnorm(gt, l, N):
            for c in range(8):
                S.act(lambda e, c=c: e.activation(sq[:, c, 0:N], xT[:, c, 0:N], AF.Square),
                      reads=[Bx[c]], writes=[Bsq[c]])
            ps, bp = psg.get()
            for c in range(8):
                S.pe(lambda e, c=c: e.matmul(ps[:, 0:N], lhsT=onesD[:], rhs=sq[:, c, 0:N], start=(c == 0),
                                             stop=(c == 7)), reads=[Bc, Bsq[c]], writes=[bp])
            rs, brs = ft.get()
            rsqrt_eps(rs[:, 0:N], ps[:, 0:N], [bp], brs)
            for c in range(8):
                gap = gt[:, l, c:c + 1] if l is not None else gt[:, c:c + 1]
                S.dve(lambda e, c=c, gap=gap: e.scalar_tensor_tensor(hT[:, c, 0:N], xT[:, c, 0:N], gap, rs[:, 0:N],
                                                                     ALU.mult, ALU.mult),
                      reads=[Bx[c], brs, Bc], writes=[Bh[c]])

        def branch_out(l, wname, gcol, N, first, last):
            wo_, bwo = wload(Wd[wname][l], 4, 1024)
            wg = [wload(Wd["w_in"][l][:, gcol + i * 512: gcol + (i + 1) * 512], 8, 512) for i in range(2)]
            for oc in range(8):
                psy, bpy = psg.get()
                for k in range(4):
                    S.pe(lambda e, k=k, oc=oc: e.matmul(psy[:, 0:N], lhsT=wo_[:, k, oc * 128:(oc + 1) * 128],
                                                        rhs=brT[:, k, 0:N], start=(k == 0), stop=(k == 3)),
                         reads=[bwo, Bbr[k]], writes=[bpy])
                psgt, bpg = psg.get()
                wgt, bwg = wg[oc // 4]
                mm_fm(psgt, 128, N, wgt, bwg, (oc % 4) * 128, 8, hT, Bh, extra_w=[bpg])
                sg, bsg = ft.get()
                S.act(lambda e: e.activation(sg[:, 0:N], psgt[:, 0:N], AF.Sigmoid), reads=[bpg], writes=[bsg])
                if first:
                    S.dve(lambda e, oc=oc: e.tensor_tensor(mT[:, oc, 0:N], sg[:, 0:N], psy[:, 0:N], ALU.mult),
                          reads=[bsg, bpy], writes=[Bm[oc]])
                else:
                    S.dve(lambda e: e.tensor_tensor(sg[:, 0:N], sg[:, 0:N], psy[:, 0:N], ALU.mult),
                          reads=[bpy], writes=[bsg])
                    if last:
                        S.dve(lambda e, oc=oc: e.tensor_tensor(sq[:, oc, 0:N], mT[:, oc, 0:N], sg[:, 0:N], ALU.add),
                              reads=[bsg, Bm[oc]], writes=[Bsq[oc]])
                    else:
                        S.dve(lambda e, oc=oc: e.tensor_tensor(mT[:, oc, 0:N], mT[:, oc, 0:N], sg[:, 0:N], ALU.add),
                              reads=[bsg], writes=[Bm[oc]])

        def conv_branch(l, N, segs):
            wA, bA = wload(Wd["w_in"][l][:, C_AV:C_AV + 512], 8, 512)
            wG, bG = wload(Wd["w_in"][l][:, C_AG:C_AG + 512], 8, 512)
            for c in range(4):
                psv, bpv = psg.get()
                mm_fm(psv, 128, N, wA, bA, c * 128, 8, hT, Bh, extra_w=[bpv])
                psg_, bpg = psg.get()
                mm_fm(psg_, 128, N, wG, bG, c * 128, 8, hT, Bh, extra_w=[bpg])
                sg, bsg = ft.get()
                S.act(lambda e: e.activation(sg[:, 0:N], psg_[:, 0:N], AF.Sigmoid), reads=[bpg], writes=[bsg])
                av, bav = ft.get()
                S.act(lambda e: e.activation(av[:, 0:N], psv[:, 0:N], AF.Copy), reads=[bpv], writes=[bav])
                segs(l, c, av, bav, sg, bsg)

        def conv_post(l, N):
            for c in range(4):
                S.act(lambda e, c=c: e.activation(ybf[:, c, 0:N], yc[:, c, 0:N], AF.Copy), reads=[Byc[c]],
                      writes=[Bybf[c]])
                S.act(lambda e, c=c: e.activation(ysq[:, c, 0:N], yc[:, c, 0:N], AF.Square), reads=[Byc[c]],
                      writes=[Bysq[c]])
            psm, bpm = psg.get()
            for c in range(4):
                S.pe(lambda e, c=c: e.matmul(psm[:, 0:N], lhsT=onesC[:], rhs=ybf[:, c, 0:N], start=(c == 0),
                                             stop=(c == 3)), reads=[Bc, Bybf[c]], writes=[bpm])
            pss, bps = psg.get()
            for c in range(4):
                S.pe(lambda e, c=c: e.matmul(pss[:, 0:N], lhsT=onesC[:], rhs=ysq[:, c, 0:N], start=(c == 0),
                                             stop=(c == 3)), reads=[Bc, Bysq[c]], writes=[bps])
            mean, bmean = ft.get()
            S.dve(lambda e: e.tensor_copy(mean[:, 0:N], psm[:, 0:N]), reads=[bpm], writes=[bmean])
            var, bvar = ft.get()
            S.dve(lambda e: e.tensor_tensor(var[:, 0:N], mean[:, 0:N], mean[:, 0:N], ALU.mult), reads=[bmean],
                  writes=[bvar])
            S.dve(lambda e: e.tensor_tensor(var[:, 0:N], pss[:, 0:N], var[:, 0:N], ALU.subtract), reads=[bps],
                  writes=[bvar])
            rsqrt_eps(var[:, 0:N], var[:, 0:N], [], bvar)
            S.dve(lambda e: e.scalar_tensor_tensor(mean[:, 0:N], mean[:, 0:N], -1.0, var[:, 0:N], ALU.mult, ALU.mult),
                  reads=[bvar], writes=[bmean])
            for c in range(4):
                t, bt = ft.get()
                S.dve(lambda e, c=c: e.tensor_tensor(t[:, 0:N], yc[:, c, 0:N], var[:, 0:N], ALU.mult),
                      reads=[Byc[c], bvar], writes=[bt])
                S.dve(lambda e: e.tensor_tensor(t[:, 0:N], t[:, 0:N], mean[:, 0:N], ALU.add), reads=[bmean],
                      writes=[bt])
                S.act(lambda e, c=c: e.activation(brT[:, c, 0:N], t[:, 0:N], AF.Silu, bias=lnb[:, l, c:c + 1],
                                                  scale=lng[:, l, c:c + 1]), reads=[bt, Bc], writes=[Bbr[c]])

        def conv_taps(l, c, u0, y0, n):
            S.dve(lambda e: e.tensor_scalar(yc[:, c, y0:y0 + n], ubuf[:, c, u0:u0 + n], cw[:, l, c, 0:1],
                                            cb[:, l, c:c + 1], ALU.mult, ALU.add),
                  reads=[Bub[c], Bc], writes=[Byc[c]])
            for j in range(1, 31):
                S.dve(lambda e, j=j: e.scalar_tensor_tensor(yc[:, c, y0:y0 + n], ubuf[:, c, u0 + j:u0 + j + n],
                                                            cw[:, l, c, j:j + 1], yc[:, c, y0:y0 + n],
                                                            ALU.mult, ALU.add), reads=[Bub[c]], writes=[Byc[c]])

        def load_x_tile(src, N):
            nsub = (N + 127) // 128
            for i in range(nsub):
                n = min(128, N - i * 128)
                stt, bst_ = stg.get()
                S.dma(PQ, None, stt[0:n, :], src[i * 128:i * 128 + n, :], writes=[bst_])
                for half in range(2):
                    ps, bp = psg.get()
                    for cc in range(4):
                        c = half * 4 + cc
                        S.pe(lambda e, cc=cc, c=c, n=n, ps=ps, stt=stt: e.transpose(
                            ps[:, cc * 128:cc * 128 + n], stt[0:n, c * 128:(c + 1) * 128], ident[0:n, 0:n]),
                             reads=[bst_, Bc], writes=[bp])
                    S.act(lambda e, i=i, n=n, half=half, ps=ps: e.activation(
                        xT[:, half * 4:half * 4 + 4, i * 128:i * 128 + n],
                        ps[:, :].rearrange("p (c t) -> p c t", t=128)[:, :, 0:n], AF.Copy),
                          reads=[bp], writes=Bx[half * 4:half * 4 + 4])

        def store_y_tile(dst, N):
            for c in range(8):
                S.act(lambda e, c=c: e.activation(sq[:, c, 0:N], xT[:, c, 0:N], AF.Square),
                      reads=[Bx[c]], writes=[Bsq[c]])
            ps, bp = psg.get()
            for c in range(8):
                S.pe(lambda e, c=c: e.matmul(ps[:, 0:N], lhsT=onesD[:], rhs=sq[:, c, 0:N], start=(c == 0),
                                             stop=(c == 7)), reads=[Bc, Bsq[c]], writes=[bp])
            rs, brs = ft.get()
            rsqrt_eps(rs[:, 0:N], ps[:, 0:N], [bp], brs)
            for c in range(8):
                S.dve(lambda e, c=c: e.scalar_tensor_tensor(mT[:, c, 0:N], xT[:, c, 0:N], gf[:, c:c + 1], rs[:, 0:N],
                                                            ALU.mult, ALU.mult),
                      reads=[Bx[c], brs, Bc], writes=[Bm[c]])
            nsub = (N + 127) // 128
            for i in range(nsub):
                n = min(128, N - i * 128)
                stt, bst_ = stg.get()
                for half in range(2):
                    ps, bp = psg.get()
                    for cc in range(4):
                        c = half * 4 + cc
                        S.pe(lambda e, c=c, cc=cc, i=i, n=n, ps=ps: e.transpose(ps[0:n, cc * 128:(cc + 1) * 128],
                                                                         mT[:, c, i * 128:i * 128 + n], ident[:]),
                             reads=[Bm[c], Bc], writes=[bp])
                    S.act(lambda e, n=n, half=half, ps=ps, stt=stt: e.activation(
                        stt[0:n, half * 512:(half + 1) * 512], ps[0:n, :], AF.Copy), reads=[bp], writes=[bst_])
                S.dma(PQ, None, dst[i * 128:i * 128 + n, :], stt[0:n, :], reads=[bst_])

        def ffn_and_wo(l, N):
            wO = [wload(Wd["w_o"][l][:, i * 512:(i + 1) * 512], 8, 512) for i in range(2)]
            for oc in range(8):
                ps, bp = psg.get()
                wt, bw = wO[oc // 4]
                mm_fm(ps, 128, N, wt, bw, (oc % 4) * 128, 8, sq, Bsq, extra_w=[bp])
                S.dve(lambda e, oc=oc: e.tensor_tensor(xT[:, oc, 0:N], xT[:, oc, 0:N], ps[:, 0:N], ALU.add),
                      reads=[bp], writes=[Bx[oc]])
            rmsnorm(g2, l, N)
            for q in range(4):
                for j in range(2):
                    wU, bU = wload(Wd["w_up"][l][:, (q * 2 + j) * 512:(q * 2 + j + 1) * 512], 8, 512)
                    for cc in range(4):
                        ps, bp = psg.get()
                        mm_fm(ps, 128, N, wU, bU, cc * 128, 8, hT, Bh, extra_w=[bp])
                        r1, br1 = ft.get()
                        S.act(lambda e: e.activation(r1[:, 0:N], ps[:, 0:N], AF.Relu), reads=[bp], writes=[br1])
                        S.dve(lambda e, j=j, cc=cc: e.tensor_tensor(sq[:, j * 4 + cc, 0:N], r1[:, 0:N], r1[:, 0:N],
                                                                    ALU.mult), reads=[br1], writes=[Bsq[j * 4 + cc]])
                wDn = [wload(Wd["w_down"][l][q * 1024:(q + 1) * 1024, i * 512:(i + 1) * 512], 8, 512) for i in range(2)]
                for oc in range(8):
                    ps, bp = psg.get()
                    wt, bw = wDn[oc // 4]
                    mm_fm(ps, 128, N, wt, bw, (oc % 4) * 128, 8, sq, Bsq, extra_w=[bp])
                    S.dve(lambda e, oc=oc: e.tensor_tensor(xT[:, oc, 0:N], xT[:, oc, 0:N], ps[:, 0:N], ALU.add),
                          reads=[bp], writes=[Bx[oc]])

        def fox_qkv(l, N, tok0, kdst, vdst, lfdst, carry=True):
            wQ, bQ = wload(Wd["w_in"][l][:, C_Q:C_Q + 512], 8, 512)
            wK, bK = wload(Wd["w_in"][l][:, C_K:C_K + 512], 8, 512)
            wV, bV = wload(Wd["w_in"][l][:, C_V:C_V + 520], 8, 520)
            for hh in range(8):
                ps, bp = psg.get()
                mm_fm(ps, 64, N, wQ, bQ, hh * 64, 8, hT, Bh, extra_w=[bp])
                S.act(lambda e, hh=hh: e.activation(Qa[0:64, hh, 0:N], ps[0:64, 0:N], AF.Copy, scale=0.125),
                      reads=[bp], writes=[BQ[hh]])
                ps2, bp2 = psg.get()
                mm_fm(ps2, 64, N, wK, bK, hh * 64, 8, hT, Bh, extra_w=[bp2])
                S.act(lambda e, hh=hh: e.activation(Ka[0:64, hh, 0:N], ps2[0:64, 0:N], AF.Copy), reads=[bp2],
                      writes=[BK[hh]])
            ps, bp = psg.get()
            mm_fm(ps, 8, N, wV, bV, 512, 8, hT, Bh, extra_w=[bp])
            S.act(lambda e: e.activation(fT[:, 0:N], ps[0:8, 0:N], AF.Exp, bias=nbf[:, l:l + 1], scale=-1.0),
                  reads=[bp, Bc], writes=[Bf])
            S.act(lambda e: e.activation(fT[:, 0:N], fT[:, 0:N], AF.Ln, bias=1.0), writes=[Bf])
            return wK, bK, wV, bV

        def fox_scan_rows(l, N, c0, n, init_ap, Binit):
            S.dve(lambda e: e.tensor_tensor_scan(csp[:, c0:c0 + n], ones[0:8, 0:n], fT[:, c0:c0 + n], init_ap,
                                                 ALU.mult, ALU.add), reads=[Bf, Bc] + Binit, writes=[Bf])

        def fox_split_rows(N):
            S.dve(lambda e: e.tensor_copy(chi[:, 0:N], csp[:, 0:N]), writes=[Bf])
            S.dve(lambda e: e.tensor_tensor(clo[:, 0:N], csp[:, 0:N], chi[:, 0:N], ALU.subtract), writes=[Bf])
            S.dve(lambda e: e.tensor_scalar(nhi[:, 0:N], chi[:, 0:N], -1.0, None, ALU.mult), writes=[Bf])
            S.dve(lambda e: e.tensor_scalar(nlo[:, 0:N], clo[:, 0:N], -1.0, None, ALU.mult), writes=[Bf])
            pairs = []
            for hh in range(8):
                pairs.append((Qa[64:65, hh, 0:N], nhi[hh:hh + 1, 0:N]))
                pairs.append((Qa[65:66, hh, 0:N], nlo[hh:hh + 1, 0:N]))
                pairs.append((Ka[66:67, hh, 0:N], chi[hh:hh + 1, 0:N]))
                pairs.append((Ka[67:68, hh, 0:N], clo[hh:hh + 1, 0:N]))
            S.dma_group(PQ, rowsem, pairs, reads=[Bf], writes=BQ + BK)

        def fox_tok_outputs(l, N, wK, bK, wV, bV, kdst_fn, vdst_fn, lfdst_fn, va_fn):
            nsub = (N + 127) // 128
            for i in range(nsub):
                n = min(128, N - i * 128)
                stt, bst_ = stg.get()
                psk, bpk = psg.get()
                for k in range(8):
                    S.pe(lambda e, k=k, i=i, n=n, psk=psk: e.matmul(psk[0:n, :], lhsT=hT[:, k, i * 128:i * 128 + n],
                                                           rhs=wK[:, k, 0:512], start=(k == 0), stop=(k == 7)),
                         reads=[bK, Bh[k]], writes=[bpk])
                S.act(lambda e, n=n, psk=psk, stt=stt: e.activation(stt[0:n, 0:512], psk[0:n, :], AF.Copy),
                      reads=[bpk], writes=[bst_])
                psv, bpv = psg.get()
                for k in range(8):
                    S.pe(lambda e, k=k, i=i, n=n, psv=psv: e.matmul(psv[0:n, :], lhsT=hT[:, k, i * 128:i * 128 + n],
                                                           rhs=wV[:, k, 0:512], start=(k == 0), stop=(k == 7)),
                         reads=[bV, Bh[k]], writes=[bpv])
                S.act(lambda e, n=n, psv=psv, stt=stt: e.activation(stt[0:n, 512:1024], psv[0:n, :], AF.Copy),
                      reads=[bpv], writes=[bst_])
                if "nova" not in DBG2:
                    va_fn(i, n, stt, bst_)
                if "nolf" in DBG2:
                    kdst_fn(i, n, stt, bst_)
                    vdst_fn(i, n, stt, bst_)
                    continue
                psf, bpf = psg.get()
                S.pe(lambda e, i=i, n=n, psf=psf: e.transpose(psf[0:n, 0:8], fT[0:8, i * 128:i * 128 + n],
                                                              ident[0:8, 0:8]), reads=[Bf, Bc], writes=[bpf])
                S.act(lambda e, i=i, n=n, psf=psf: e.activation(lftok[0:n, i, :], psf[0:n, 0:8], AF.Copy, scale=-1.0),
                      reads=[bpf], writes=[Blftok])
                if "nokv" not in DBG2:
                    kdst_fn(i, n, stt, bst_)
                    vdst_fn(i, n, stt, bst_)
                if "nolfdma" not in DBG2:
                    lfdst_fn(i, n)

        def fox_attend_prompt(l, j, groups=(0, 1, 2, 3)):
            for g in groups:
                pairs = []
                for jj in range(j + 1):
                    for hl in range(2):
                        for kt in range(4):
                            pairs.append((jj, hl, kt))
                src = {}
                state = {}

                def get_src(jj):
                    if jj not in src:
                        kt_t, bkt = ksrc.get()
                        ksm = kvsem[(ksrc.i - 1) % 2]
                        vt_t, _unused = vsrc.get()
                        S.dma_group(WQ, ksm, [(kt_t[:], kscr[l, :, 2 * g:2 * g + 2, jj * T:(jj + 1) * T]),
                                              (vt_t[:], vscr[l, :, 2 * g:2 * g + 2, jj * 4:(jj + 1) * 4, :])],
                                    raw=[Bkscr[l][jj], Bvscr[l][jj]], writes=[bkt])
                        src[jj] = (kt_t, vt_t, bkt)
                    return src[jj]

                def emit_qk(n):
                    jj, hl, kt = pairs[n]
                    kt_t, vt_t, bkt = get_src(jj)
                    hh = 2 * g + hl
                    q0 = kt * 128 if jj == j else 0
                    ps, bp = psS.get()
                    S.pe(lambda e: e.matmul(ps[:, q0:T], lhsT=kt_t[:, hl, kt * 128:(kt + 1) * 128],
                                            rhs=Qa[:, hh, q0:T], start=True, stop=True),
                         reads=[bkt, BQ[hh]], writes=[bp])
                    state[n] = (ps, bp, q0)

                def emit_rest(n):
                    jj, hl, kt = pairs[n]
                    kt_t, vt_t, bkt = get_src(jj)
                    ps, bp, q0 = state.pop(n)
                    P, bP = Prot.get()
                    S.act(lambda e: e.activation(P[:, q0:T], ps[:, q0:T], AF.Exp), reads=[bp], writes=[bP])
                    if jj == j:
                        S.dve(lambda e: e.tensor_tensor(P[:, q0:q0 + 128], P[:, q0:q0 + 128], triub[:], ALU.mult),
                              reads=[Bc], writes=[bP])
                    for i in range(q0 // 128, 4):
                        S.pe(lambda e, i=i: e.matmul(psO[hl][:, i, 0:65], lhsT=P[:, i * 128:(i + 1) * 128],
                                                     rhs=vt_t[:, hl, kt, 0:65], start=first[hl], stop=False,
                                                     skip_group_check=True),
                             reads=[bP, bkt], writes=[BpsO[hl]])
                        first[hl] = False

                first = [True, True]
                emit_qk(0)
                for n in range(len(pairs)):
                    if n + 1 < len(pairs):
                        emit_qk(n + 1)
                    emit_rest(n)
                for hl in range(2):
                    hh = 2 * g + hl
                    r, br = rc.get()
                    S.dve(lambda e, hl=hl, r=r: e.reciprocal(r[:], psO[hl][:, :, 64]), reads=[BpsO[hl]], writes=[br])
                    for i in range(4):
                        S.dve(lambda e, hl=hl, hh=hh, i=i, r=r: e.tensor_scalar(
                            otok[:, i, hh * 64:(hh + 1) * 64], psO[hl][:, i, 0:64], r[:, i:i + 1], None, ALU.mult),
                              reads=[BpsO[hl], br], writes=[Botok[i]])

        def fox_o_transpose(N):
            nsub = (N + 127) // 128
            for fc in range(4):
                ps, bp = psg.get()
                for i in range(nsub):
                    n = min(128, N - i * 128)
                    S.pe(lambda e, i=i, n=n, fc=fc: e.transpose(ps[:, i * 128:i * 128 + n],
                                                                otok[0:n, i, fc * 128:(fc + 1) * 128],
                                                                ident[0:n, 0:n]),
                         reads=[Botok[i], Bc], writes=[bp])
                S.act(lambda e, fc=fc: e.activation(brT[:, fc, 0:N], ps[:, 0:N], AF.Copy), reads=[bp],
                      writes=[Bbr[fc]])

        def hgrn_branch(l, N, C, segs, S_in_fn, S_out_fn, between=None):
            nch = N // C
            wi, bi = wload(Wd["w_in"][l][:, C_HI:C_HI + 512], 8, 512)
            wq, bq = wload(Wd["w_in"][l][:, C_HQ:C_HQ + 512], 8, 512)
            wf, bf_ = wload(Wd["w_in"][l][:, C_HF:C_HF + 512], 8, 512)
            for ci in range(nch):
                ps, bp = psg.get()
                for k in range(8):
                    S.pe(lambda e, k=k, ci=ci: e.matmul(ps[0:C, :], lhsT=hT[:, k, ci * C:(ci + 1) * C],
                                                        rhs=wi[:, k, :], start=(k == 0), stop=(k == 7)),
                         reads=[bi, Bh[k]], writes=[bp])
                S.act(lambda e, ci=ci: e.activation(vtok[0:C, ci, :], ps[0:C, :], AF.Copy), reads=[bp],
                      writes=[Bvtok[ci]])
            wo_, bo = wload(Wd["w_in"][l][:, C_HO:C_HO + 512], 8, 512)
            mid = C // 2 - 1
            for hd in range(4):
                ps, bp = psg.get()
                mm_fm(ps, 128, N, wq, bq, hd * 128, 8, hT, Bh, extra_w=[bp])
                S.act(lambda e: e.activation(qs[:, 0:N], ps[:, 0:N], AF.Silu), reads=[bp], writes=[Bqs])
                ps2, bp2 = psg.get()
                mm_fm(ps2, 128, N, wf, bf_, hd * 128, 8, hT, Bh, extra_w=[bp2])
                sg, bsg = ft.get()
                S.act(lambda e: e.activation(sg[:, 0:N], ps2[:, 0:N], AF.Sigmoid), reads=[bp2], writes=[bsg])
                lg, blg = ft.get()
                S.act(lambda e, hd=hd: e.activation(lg[:, 0:N], sg[:, 0:N], AF.Ln, bias=lb[:, hd, l:l + 1],
                                                    scale=oml[:, hd, l:l + 1]), reads=[bsg, Bc], writes=[blg])
                S.dve(lambda e, hd=hd: e.tensor_scalar(kk[:, 0:N], sg[:, 0:N], noml[:, hd, l:l + 1],
                                                       oml[:, hd, l:l + 1], ALU.mult, ALU.add),
                      reads=[bsg, Bc], writes=[Bkk])
                for (c0, ncs, si) in segs:
                    S.dve(lambda e, c0=c0, ncs=ncs: e.tensor_tensor_scan(
                        bg[:, c0 * C:(c0 + ncs) * C], ones[:, 0:ncs * C], lg[:, c0 * C:(c0 + ncs) * C], 0.0,
                        ALU.mult, ALU.add), reads=[blg, Bc], writes=[Bbg])
                bg3 = bg[:, 0:N].rearrange("p (c s) -> p c s", s=C)
                dm, bdm = ft.get()
                dm3 = dm[:, 0:N].rearrange("p (c s) -> p c s", s=C)
                S.dve(lambda e: e.tensor_tensor(dm3, bg3, bg[:, mid:N:C].unsqueeze(2).to_broadcast([128, nch, C]),
                                                ALU.subtract), reads=[Bbg], writes=[bdm])
                bst, bbst = ft.get()
                S.dve(lambda e: e.tensor_tensor(bst[:, 0:nch], bg[:, 0:N:C], lg[:, 0:N:C], ALU.subtract),
                      reads=[Bbg, blg], writes=[bbst])
                ds, bds = ft.get()
                ds3 = ds[:, 0:N].rearrange("p (c s) -> p c s", s=C)
                S.dve(lambda e: e.tensor_tensor(ds3, bg3, bst[:, 0:nch].unsqueeze(2).to_broadcast([128, nch, C]),
                                                ALU.subtract), reads=[bbst, Bbg], writes=[bds])
                dl, bdl = ft.get()
                dl3 = dl[:, 0:N].rearrange("p (c s) -> p c s", s=C)
                S.dve(lambda e: e.tensor_tensor(dl3, bg3, bg[:, C - 1:N:C].unsqueeze(2).to_broadcast([128, nch, C]),
                                                ALU.subtract), reads=[Bbg], writes=[bdl])
                e1, be1 = ft.get()
                S.act(lambda e: e.activation(e1[:, 0:N], dm[:, 0:N], AF.Exp), reads=[bdm], writes=[be1])
                e2, be2 = ft.get()
                S.act(lambda e: e.activation(e2[:, 0:N], dm[:, 0:N], AF.Exp, scale=-1.0), reads=[bdm], writes=[be2])
                e3, be3 = ft.get()
                S.act(lambda e: e.activation(e3[:, 0:N], ds[:, 0:N], AF.Exp), reads=[bds], writes=[be3])
                e4, be4 = ft.get()
                S.act(lambda e: e.activation(e4[:, 0:N], dl[:, 0:N], AF.Exp, scale=-1.0), reads=[bdl], writes=[be4])
                S.dve(lambda e: e.tensor_tensor(qt[:, 0:N], qs[:, 0:N], e1[:, 0:N], ALU.mult), reads=[Bqs, be1],
                      writes=[Bqt])
                S.dve(lambda e: e.tensor_tensor(kt_[:, 0:N], kk[:, 0:N], e2[:, 0:N], ALU.mult), reads=[Bkk, be2],
                      writes=[Bkt])
                S.dve(lambda e: e.tensor_tensor(qh[:, 0:N], qs[:, 0:N], e3[:, 0:N], ALU.mult), reads=[Bqs, be3],
                      writes=[Bqh])
                S.dve(lambda e: e.tensor_tensor(kh[:, 0:N], kk[:, 0:N], e4[:, 0:N], ALU.mult), reads=[Bkk, be4],
                      writes=[Bkh])
                if between is not None:
                    between(hd)
                for ci in range(nch):
                    pst_, bpt = psg.get()
                    S.pe(lambda e, ci=ci: e.transpose(pst_[0:C, 0:128], kh[:, ci * C:(ci + 1) * C], ident[:]),
                         reads=[Bkh, Bc], writes=[bpt])
                    S.act(lambda e, ci=ci: e.activation(khtok[0:C, ci, :], pst_[0:C, 0:128], AF.Copy), reads=[bpt],
                          writes=[Bkhtok[ci]])
                psA, bpA = psg.get()
                for ci in range(nch):
                    S.pe(lambda e, ci=ci: e.matmul(psA[0:C, ci * C:(ci + 1) * C], lhsT=kt_[:, ci * C:(ci + 1) * C],
                                                   rhs=qt[:, ci * C:(ci + 1) * C], start=True, stop=True),
                         reads=[Bkt, Bqt], writes=[bpA])
                S.dve(lambda e: e.tensor_tensor(Abf[0:C, 0:nch, 0:C],
                                                psA[0:C, 0:nch * C].rearrange("p (c s) -> p c s", s=C),
                                                triu[0:C, 0:C].unsqueeze(1).to_broadcast([C, nch, C]), ALU.mult),
                      reads=[bpA, Bc], writes=[BAbf])
                pso, bpo = psS.get()
                for (c0, ncs, si) in segs:
                    Sap, BSt = S_in_fn(hd, si)
                    sbf, bsbf = Sbf.get()
                    S.act(lambda e, Sap=Sap, sbf=sbf: e.activation(sbf[:], Sap, AF.Copy), reads=[BSt], writes=[bsbf])
                    for ci in range(c0, c0 + ncs):
                        S.pe(lambda e, ci=ci, hd=hd: e.matmul(pso[:, ci * C:(ci + 1) * C],
                                                              lhsT=vtok[0:C, ci, hd * 128:(hd + 1) * 128],
                                                              rhs=Abf[0:C, ci, 0:C], start=True, stop=False),
                             reads=[Bvtok[ci], BAbf], writes=[bpo])
                        S.pe(lambda e, ci=ci, sbf=sbf: e.matmul(pso[:, ci * C:(ci + 1) * C], lhsT=sbf[:],
                                                                rhs=qh[:, ci * C:(ci + 1) * C], start=False, stop=True),
                             reads=[bsbf, Bqh], writes=[bpo])
                        psd, bpd = psg.get()
                        S.pe(lambda e, ci=ci, hd=hd, psd=psd: e.matmul(psd[:, 0:128], lhsT=khtok[0:C, ci, :],
                                                                       rhs=vtok[0:C, ci, hd * 128:(hd + 1) * 128],
                                                                       start=True, stop=True),
                             reads=[Bkhtok[ci], Bvtok[ci]], writes=[bpd])
                        S.dve(lambda e, ci=ci, Sap=Sap, psd=psd, e3=e3: e.scalar_tensor_tensor(
                            Sap, Sap, e3[:, (ci + 1) * C - 1:(ci + 1) * C], psd[:, 0:128], ALU.mult, ALU.add),
                              reads=[bpd, be3], writes=[BSt])
                        if ci < c0 + ncs - 1:
                            sbf, bsbf = Sbf.get()
                            S.act(lambda e, Sap=Sap, sbf=sbf: e.activation(sbf[:], Sap, AF.Copy), reads=[BSt],
                                  writes=[bsbf])
                    S_out_fn(hd, si, Sap, BSt)
                S.act(lambda e: e.activation(sqo[:, 0:N], pso[:, 0:N], AF.Square), reads=[bpo], writes=[Bsqo])
                psm, bpm = psg.get()
                S.pe(lambda e: e.matmul(psm[:, 0:N], lhsT=onesV[:], rhs=sqo[:, 0:N], start=True, stop=True),
                     reads=[Bsqo, Bc], writes=[bpm])
                rs, brs = ft.get()
                rsqrt_eps(rs[:, 0:N], psm[:, 0:N], [bpm], brs)
                on, bon = ft.get()
                S.dve(lambda e: e.scalar_tensor_tensor(on[:, 0:N], pso[:, 0:N], hng[:, l:l + 1], rs[:, 0:N], ALU.mult,
                                                       ALU.mult), reads=[bpo, brs, Bc], writes=[bon])
                psq, bpq = psg.get()
                mm_fm(psq, 128, N, wo_, bo, hd * 128, 8, hT, Bh, extra_w=[bpq])
                go, bgo = ft.get()
                S.act(lambda e: e.activation(go[:, 0:N], psq[:, 0:N], AF.Silu), reads=[bpq], writes=[bgo])
                S.dve(lambda e, hd=hd: e.tensor_tensor(brT[:, hd, 0:N], on[:, 0:N], go[:, 0:N], ALU.mult),
                      reads=[bon, bgo], writes=[Bbr[hd]])

        init_consts()

        import os as _os
        DBG = _os.environ.get("KDBG", "full")
        DBG2 = _os.environ.get("KDBG2", "")

        def prompt_tile(j):
            N = T
            tok0 = j * T
            load_x_tile(xp[tok0:tok0 + T, :], N)
            for l in range(L if DBG != "load" else 0):
                rmsnorm(g1, l, N)
                def segs(l_, c, psv, bpv, sg, bsg):
                    S.act(lambda e: e.activation(ubuf[:, c, 0:30], hist[:, l_, c, :], AF.Copy), reads=[Bhist[l_]],
                          writes=[Bub[c]])
                    S.dve(lambda e: e.tensor_tensor(ubuf[:, c, 30:30 + N], psv[:, 0:N], sg[:, 0:N], ALU.mult),
                          reads=[bpv, bsg], writes=[Bub[c]])
                    S.dve(lambda e: e.tensor_copy(hist[:, l_, c, :], ubuf[:, c, N:N + 30]), reads=[Bub[c]],
                          writes=[Bhist[l_]])
                    conv_taps(l_, c, 0, 0, N)
                if DBG == "norm":
                    continue
                conv_branch(l, N, segs)
                wK, bK, wV, bV = fox_qkv(l, N, tok0, None, None, None)
                def va_fn(i, n, stt, bst_):
                    S.act(lambda e: e.activation(Va[0:n, :, i, 0:64],
                                                 stt[0:n, 512:1024].rearrange("p (h d) -> p h d", d=64), AF.Copy),
                          reads=[bst_], writes=[BV])

                def kdst(i, n, stt, bst_):
                    S.dma(PQ, None, kp[l, tok0 + i * 128: tok0 + i * 128 + n, :], stt[0:n, 0:512],
                          reads=[bst_])

                def vdst(i, n, stt, bst_):
                    S.dma(PQ, None, vp[l, tok0 + i * 128: tok0 + i * 128 + n, :], stt[0:n, 512:1024],
                          reads=[bst_])

                def lfdst(i, n):
                    if i == NSUB - 1:
                        S.dma(PQ, lfsem, lfp[l, tok0:tok0 + N, :].rearrange("(s p) h -> p s h", p=128), lftok[:],
                              reads=[Blftok])
                fox_tok_outputs(l, N, wK, bK, wV, bV, kdst, vdst, lfdst, va_fn)
                conv_post(l, N)
                branch_out(l, "w_a_out", C_GA, N, True, False)
                if j == NT - 1:
                    for c in range(4):
                        S.dma(PQ, osem, cp[l][:, c * 128:(c + 1) * 128].rearrange("j p -> p j"), hist[:, l, c, :],
                              reads=[Bhist[l]])
                fox_scan_rows(l, N, 0, N, ccar[:, l:l + 1], [Bccar[l]])
                S.dve(lambda e: e.tensor_copy(ccar[:, l:l + 1], csp[:, N - 1:N]), reads=[Bf], writes=[Bccar[l]])
                fox_split_rows(N)
                S.dma(PQ, ksem_, kscr[l, :, :, tok0:tok0 + N], Ka[:], reads=BK, writes=[Bkscr[l][j]])
                S.dma(PQ, vsem_, vscr[l, :, :, j * NSUB:(j + 1) * NSUB, :], Va[:], reads=[BV], writes=[Bvscr[l][j]])
                def S_in(hd, si):
                    return Sst[:, l, hd, :], BS[l][hd]

                def S_out(hd, si, Sap, BSt):
                    if j == NT - 1:
                        S.dma(PQ, osem, hp[l, hd], Sap, reads=[BSt])
                hgrn_branch(l, N, 64, [(0, NCH, 0)], S_in, S_out,
                            between=lambda hd: fox_attend_prompt(l, j, [hd]))
                branch_out(l, "w_c_out", C_GC, N, False, False)
                fox_o_transpose(N)
                branch_out(l, "w_b_out", C_GB, N, False, True)
                if DBG == "hgrn":
                    continue
                ffn_and_wo(l, N)
            store_y_tile(yp[tok0:tok0 + T, :], N)


        def sample_tile():
            N = NSS * LS
            for i_ in range(2):
                kct = ksrc.tiles[i_][:].rearrange("p a (b k) -> p (a b) k", k=128)
                vct = vsrc.tiles[i_][:].rearrange("p a b w -> p (a b) w")
                S.pool(lambda e, kct=kct: e.memset(kct[64:68, :, :], 0.0), writes=[ksrc.bufs[i_]])
                S.pool(lambda e, kct=kct: e.memset(kct[64:66, :, :], 1.0), writes=[ksrc.bufs[i_]])
                S.pool(lambda e, vct=vct: e.memset(vct[:, :, 64:VW], 1.0), writes=[ksrc.bufs[i_]])
            load_x_tile(xs, N)
            ub4 = ubuf[:, :, 0:4 * 46].rearrange("p c (s w) -> p c s w", w=46)
            for l in range(L):
                rmsnorm(g1, l, N)
                for s_ in range(NSS):
                    stt, bst_ = stg.get()
                    S.dma(PQ, None, stt[0:30, 0:512], sconv[l, s_], writes=[bst_])
                    ps, bp = psg.get()
                    for c in range(4):
                        S.pe(lambda e, c=c: e.transpose(ps[:, c * 32:c * 32 + 30], stt[0:30, c * 128:(c + 1) * 128],
                                                        ident[0:30, 0:30]), reads=[bst_, Bc], writes=[bp])
                    S.act(lambda e, s_=s_: e.activation(ub4[:, :, s_, 0:30],
                                                        ps[:, 0:128].rearrange("p (c w) -> p c w", w=32)[:, :, 0:30],
                                                        AF.Copy), reads=[bp], writes=Bub)

                def segs(l_, c, psv, bpv, sg, bsg):
                    S.dve(lambda e: e.tensor_tensor(ub4[:, c, :, 30:46],
                                                    psv[:, 0:N].rearrange("p (s t) -> p s t", t=LS),
                                                    sg[:, 0:N].rearrange("p (s t) -> p s t", t=LS), ALU.mult),
                          reads=[bpv, bsg], writes=[Bub[c]])
                    y4 = yc[:, c, 0:N].rearrange("p (s t) -> p s t", t=LS)
                    S.dve(lambda e: e.tensor_scalar(y4, ub4[:, c, :, 0:LS], cw[:, l_, c, 0:1], cb[:, l_, c:c + 1],
                                                    ALU.mult, ALU.add), reads=[Bub[c], Bc], writes=[Byc[c]])
                    for j in range(1, 31):
                        S.dve(lambda e, j=j: e.scalar_tensor_tensor(y4, ub4[:, c, :, j:j + LS], cw[:, l_, c, j:j + 1],
                                                                    y4, ALU.mult, ALU.add),
                              reads=[Bub[c]], writes=[Byc[c]])
                conv_branch(l, N, segs)
                for s_ in range(NSS):
                    ps, bp = psg.get()
                    for c in range(4):
                        S.pe(lambda e, c=c, s_=s_: e.transpose(ps[0:30, c * 128:(c + 1) * 128], ub4[:, c, s_, 16:46],
                                                               ident[:]), reads=[Bub[c], Bc], writes=[bp])
                    stt, bst_ = stg.get()
                    S.act(lambda e: e.activation(stt[0:30, 0:512], ps[0:30, :], AF.Copy), reads=[bp], writes=[bst_])
                    S.dma(PQ, None, cs[l, s_], stt[0:30, 0:512], reads=[bst_])
                conv_post(l, N)
                branch_out(l, "w_a_out", C_GA, N, True, False)
                wK, bK, wV, bV = fox_qkv(l, N, 0, None, None, None)
                for s_ in range(NSS):
                    fox_scan_rows(l, N, s_ * LS, LS, 0.0, [])
                fox_split_rows(N)

                def va_fn(i, n, stt, bst_):
                    S.act(lambda e: e.activation(Va[0:n, :, i, 0:64],
                                                 stt[0:n, 512:1024].rearrange("p (h d) -> p h d", d=64), AF.Copy),
                          reads=[bst_], writes=[BV])

                def kdst(i, n, stt, bst_):
                    S.dma(PQ, None, ks[l].rearrange("s t c -> (s t) c"), stt[0:n, 0:512], reads=[bst_])

                def vdst(i, n, stt, bst_):
                    S.dma(PQ, None, vs[l].rearrange("s t c -> (s t) c"), stt[0:n, 512:1024], reads=[bst_])

                def lfdst(i, n):
                    S.dma(PQ, lfsem, lfs[l].rearrange("s t h -> (s t) h"), lftok[0:n, 0, :], reads=[Blftok])
                fox_tok_outputs(l, N, wK, bK, wV, bV, kdst, vdst, lfdst, va_fn)
                fox_attend_sample(l)
                for fc in range(4):
                    ps, bp = psg.get()
                    for s_ in range(NSS):
                        S.pe(lambda e, s_=s_, fc=fc: e.transpose(ps[:, s_ * LS:(s_ + 1) * LS],
                                                                 otok[0:LS, s_, fc * 128:(fc + 1) * 128],
                                                                 ident[0:LS, 0:LS]),
                             reads=[Botok[s_], Bc], writes=[bp])
                    S.act(lambda e, fc=fc: e.activation(brT[:, fc, 0:N], ps[:, 0:N], AF.Copy), reads=[bp],
                          writes=[Bbr[fc]])
                branch_out(l, "w_b_out", C_GB, N, False, False)
                def S_in(hd, si):
                    t_, b_ = Ssm.get()
                    S.dma(PQ, None, t_[:], shg[l, si, hd], writes=[b_])
                    return t_[:], b_

                def S_out(hd, si, Sap, BSt):
                    S.dma(PQ, None, hs[l, si, hd], Sap, reads=[BSt])
                hgrn_branch(l, N, LS, [(s_, 1, s_) for s_ in range(NSS)], S_in, S_out)
                branch_out(l, "w_c_out", C_GC, N, False, True)
                ffn_and_wo(l, N)
            store_y_tile(ys, N)

        def fox_attend_sample(l):
            N = NSS * LS
            psn, bpn = psS.get()
            for hh in range(8):
                S.pe(lambda e, hh=hh: e.matmul(psn[0:N, hh * N:(hh + 1) * N], lhsT=Ka[:, hh, 0:N], rhs=Qa[:, hh, 0:N],
                                               start=True, stop=True), reads=[BK[hh], BQ[hh]], writes=[bpn])
            Pn, bPn = sqo, Bsqo
            S.act(lambda e: e.activation(Pn[0:N, :], psn[0:N, :], AF.Exp), reads=[bpn], writes=[bPn])
            S.dve(lambda e: e.tensor_tensor(Pn[0:N, :].rearrange("p (h q) -> p h q", q=N),
                                            Pn[0:N, :].rearrange("p (h q) -> p h q", q=N),
                                            bmask[:, :].unsqueeze(1).to_broadcast([N, 8, N]), ALU.mult),
                  reads=[Bc], writes=[bPn])
            for s_ in range(NSS):
                S.dma(PQ, None, lfc[:], clf[l, s_].rearrange("(kt p) h -> p kt h", p=128), writes=[Brk])
                lfc2 = lfc[:].rearrange("p k h -> p (k h)")
                ps1, bp1 = psg.get()
                S.pe(lambda e: e.matmul(ps1[:, 0:128], lhsT=slm[:], rhs=lfc2, start=True, stop=True),
                     reads=[Brk, Bc], writes=[bp1])
                ps2, bp2 = psg.get()
                S.pe(lambda e: e.matmul(ps2[:, 0:128], lhsT=ones[:, 0:128], rhs=lfc2, start=True, stop=True),
                     reads=[Brk, Bc], writes=[bp2])
                S.dve(lambda e: e.tensor_copy(rkt[:].rearrange("p k h -> p (k h)"), ps2[:, 0:128]), reads=[bp2],
                      writes=[Brk])
                S.dve(lambda e: e.tensor_copy(rk[:].rearrange("p k h -> p (k h)"), ps1[:, 0:128]), reads=[bp1],
                      writes=[Brk])
                S.dve(lambda e: e.tensor_copy(lfc[:, 15, :], rkt[:, 15, :]), writes=[Brk])
                S.dve(lambda e: e.memset(rkt[:, 15, :], 0.0), writes=[Brk])
                for kt in range(14, -1, -1):
                    S.dve(lambda e, kt=kt: e.tensor_copy(lfc[:, kt, :], rkt[:, kt, :]), writes=[Brk])
                    S.dve(lambda e, kt=kt: e.tensor_tensor(rkt[:, kt, :], rkt[:, kt + 1, :], lfc[:, kt + 1, :], ALU.add),
                          writes=[Brk])
                S.dve(lambda e: e.tensor_tensor(rk[:], rk[:], rkt[:], ALU.add), writes=[Brk])
                first = [True, True]
                for kt in range(NPAST // 128):
                    stt, bst_ = stg.get()
                    S.dma_group(PQ, None if False else S.auto_sem((), [bst_]),
                                [(stt[:, 0:512], ck[l, s_, kt * 128:(kt + 1) * 128, :]),
                                 (stt[:, 512:1024], cv[l, s_, kt * 128:(kt + 1) * 128, :])], writes=[bst_])
                    kct_t, bkc = ksrc.get()
                    vct_t, _u = vsrc.get()
                    kct = kct_t[:].rearrange("p a (b k) -> p (a b) k", k=128)
                    vct = vct_t[:].rearrange("p a b w -> p (a b) w")
                    for half in range(2):
                        pst_, bpt = psg.get()
                        for hq in range(4):
                            hh = half * 4 + hq
                            S.pe(lambda e, hh=hh, hq=hq, pst_=pst_, stt=stt: e.transpose(
                                pst_[0:64, hq * 128:(hq + 1) * 128], stt[:, hh * 64:(hh + 1) * 64], ident[:]),
                                 reads=[bst_, Bc], writes=[bpt])
                        S.act(lambda e, half=half, pst_=pst_, kct=kct: e.activation(
                            kct[0:64, half * 4:half * 4 + 4, :], pst_[0:64, :].rearrange("p (h k) -> p h k", k=128),
                            AF.Copy), reads=[bpt], writes=[bkc])
                    S.dve(lambda e, vct=vct, stt=stt: e.tensor_copy(
                        vct[:, :, 0:64], stt[:, 512:1024].rearrange("p (h d) -> p h d", d=64)),
                          reads=[bst_], writes=[bkc])
                    pss, bps = psS.get()
                    for hh in range(8):
                        S.pe(lambda e, hh=hh, pss=pss, kct=kct: e.matmul(
                            pss[:, hh * LS:(hh + 1) * LS], lhsT=kct[:, hh, :], rhs=Qa[:, hh, s_ * LS:(s_ + 1) * LS],
                            start=True, stop=True), reads=[bkc, BQ[hh]], writes=[bps])
                    Pt, bPt = Prot.get()
                    Pf, bPf = ft.get()
                    S.dve(lambda e, kt=kt, pss=pss, Pf=Pf: e.tensor_tensor(
                        Pf[:, 0:128].rearrange("p (h q) -> p h q", q=LS),
                        pss[:, 0:128].rearrange("p (h q) -> p h q", q=LS),
                        rk[:, kt, :].unsqueeze(2).to_broadcast([128, 8, LS]), ALU.add),
                          reads=[bps, Brk], writes=[bPf])
                    S.act(lambda e, Pt=Pt, Pf=Pf: e.activation(Pt[:, 0:128], Pf[:, 0:128], AF.Exp), reads=[bPf],
                          writes=[bPt])
                    for hh in range(8):
                        b_ = hh // 4
                        S.pe(lambda e, hh=hh, b_=b_, Pt=Pt, vct=vct, f=first[b_]: e.matmul(
                            psO[b_][0:LS, hh % 4, 0:65], lhsT=Pt[:, hh * LS:(hh + 1) * LS], rhs=vct[:, hh, 0:65],
                            start=f, stop=False, skip_group_check=True), reads=[bPt, bkc], writes=[BpsO[b_]])
                        first[b_] = False
                for hh in range(8):
                    b_ = hh // 4
                    S.pe(lambda e, hh=hh, b_=b_: e.matmul(
                        psO[b_][0:LS, hh % 4, 0:65], lhsT=Pn[0:N, hh * N + s_ * LS: hh * N + (s_ + 1) * LS],
                        rhs=Va[0:N, hh, 0, 0:65], start=False, stop=True, skip_group_check=True),
                         reads=[bPn, BV], writes=[BpsO[b_]])
                for b_ in range(2):
                    r, br = rc.get()
                    S.dve(lambda e, b_=b_, r=r: e.reciprocal(r[0:LS, :], psO[b_][0:LS, :, 64]), reads=[BpsO[b_]],
                          writes=[br])
                    for hq in range(4):
                        hh = b_ * 4 + hq
                        S.dve(lambda e, b_=b_, hq=hq, hh=hh, r=r: e.tensor_scalar(
                            otok[0:LS, s_, hh * 64:(hh + 1) * 64], psO[b_][0:LS, hq, 0:64], r[0:LS, hq:hq + 1], None,
                            ALU.mult), reads=[BpsO[b_], br], writes=[Botok[s_]])

        if PROMPT and DBG != "init":
            for j in range(NT):
                prompt_tile(j)

        if SAMPLE:
            sample_tile()

        with nc.allow_non_contiguous_dma(reason="small strided param/state transfers"):
            S.emit()
    return nc, S


_CACHE = {}


def kernel(**inputs):
    inp = {k: np.ascontiguousarray(np.asarray(v, dtype=np.float32)) for k, v in inputs.items()}
    L = inp["w_in"].shape[0]
    B, SEQ, _ = inp["x_prompt"].shape
    key = (SEQ, L)
    if key not in _CACHE:
        _CACHE[key] = build_program(SEQ=SEQ, L=L)[0]
    nc = _CACHE[key]
    ncores = 8
    in_maps = []
    for c in range(ncores):
        m = {
            "xp": inp["x_prompt"][c % B],
            "xs": inp["x_sample"][4 * c:4 * c + 4].reshape(NSS * LS, D),
            "ck": inp["cache_fox_k"][:, 4 * c:4 * c + 4].reshape(L, NSS, NPAST, 512),
            "cv": inp["cache_fox_v"][:, 4 * c:4 * c + 4].reshape(L, NSS, NPAST, 512),
            "clf": inp["cache_fox_logf"][:, 4 * c:4 * c + 4],
            "sconv": inp["state_conv"][:, 4 * c:4 * c + 4],
            "shg": inp["state_hgrn"][:, 4 * c:4 * c + 4],
        }
        for n in WNAMES:
            m[n] = inp[n] if n != "final_g" else inp[n].reshape(1, D)
        in_maps.append({k: np.ascontiguousarray(v) for k, v in m.items()})
    res = run_bass_kernel_spmd(nc, in_maps, core_ids=list(range(ncores)))
    R = res.results
    y_prompt = np.stack([R[b]["yp"] for b in range(B)])
    y_sample = np.concatenate([R[c]["ys"].reshape(NSS, LS, D) for c in range(ncores)], axis=0)
    kpo = np.stack([R[b]["kp"] for b in range(B)], axis=1).reshape(L, B, SEQ, 8, 64)
    vpo = np.stack([R[b]["vp"] for b in range(B)], axis=1).reshape(L, B, SEQ, 8, 64)
    lfpo = np.stack([R[b]["lfp"] for b in range(B)], axis=1)
    cpo = np.stack([R[b]["cp"] for b in range(B)], axis=1)
    hpo = np.stack([R[b]["hp"] for b in range(B)], axis=1)
    kso = np.concatenate([R[c]["ks"] for c in range(ncores)], axis=1).reshape(L, 4 * ncores, LS, 8, 64)
    vso = np.concatenate([R[c]["vs"] for c in range(ncores)], axis=1).reshape(L, 4 * ncores, LS, 8, 64)
    lfso = np.concatenate([R[c]["lfs"] for c in range(ncores)], axis=1)
    cso = np.concatenate([R[c]["cs"] for c in range(ncores)], axis=1)
    hso = np.concatenate([R[c]["hs"] for c in range(ncores)], axis=1)
    return (y_prompt, y_sample, kpo, vpo, lfpo, cpo, hpo, kso, vso, lfso, cso, hso)
```

```python
import contextlib
import numpy as np
import concourse.bass as bass
import concourse.mybir as mybir
from concourse.bass_utils import run_bass_kernel_spmd

F32 = mybir.dt.float32
BF16 = mybir.dt.bfloat16
ALU = mybir.AluOpType
AF = mybir.ActivationFunctionType
AX = mybir.AxisListType

ENGS = ("pe", "act", "dve", "pool", "sp")
SAME_ENGINE_SYNC = {"pool", "dve", "act"}
EPS = 1e-6


class Buf:
    __slots__ = ("w", "r", "name", "dw", "dr")

    def __init__(self, name=""):
        self.w = None
        self.r = {}
        self.name = name
        self.dw = None
        self.dr = None


class DmaSem:
    def __init__(self, handle):
        self.h = handle
        self.n = 0


class _Rec:
    def __init__(self):
        self.call = None

    def __getattr__(self, name):
        def f(*a, **k):
            self.call = (name, a, k)
            return self
        return f


class Sched:
    def __init__(self, nc, stack):
        self.nc = nc
        self.stack = stack
        self.ops = {e: [] for e in ENGS}
        self.count = {e: 0 for e in ENGS}
        self.seen = {e: {} for e in ENGS}
        self.esem = {e: stack.enter_context(nc.semaphore("es_" + e)) for e in ENGS}
        self.dsems = []
        self.nwaits = 0

    def dma_sem(self, name):
        d = DmaSem(self.stack.enter_context(self.nc.semaphore(name)))
        self.dsems.append(d)
        return d

    def op(self, eng, fn, reads=(), writes=(), dsem=None, raw=()):
        waits = {}
        seen = self.seen[eng]

        def need(k, v):
            if dsem is None and k == eng and eng not in SAME_ENGINE_SYNC:
                return
            if seen.get(k, 0) >= v:
                return
            if waits.get(k, 0) < v:
                waits[k] = v

        for b in reads:
            if b.w is not None:
                need(*b.w)
        for b in raw:
            if b.w is not None:
                need(*b.w)
        for b in writes:
            if b.w is not None:
                need(*b.w)
            for k, v in b.r.items():
                need(k, v)
        for k, v in waits.items():
            seen[k] = v
        if dsem is None:
            self.count[eng] += 1
            ev = (eng, self.count[eng])
        else:
            dsem.n += 16
            ev = (dsem, dsem.n)
        for b in reads:
            if b.r.get(ev[0], 0) < ev[1]:
                b.r[ev[0]] = ev[1]
        for b in writes:
            b.w = ev
            b.r = {}
        self.nwaits += len(waits)
        rec = _Rec()
        fn(rec)
        self.ops[eng].append((list(waits.items()), rec.call, ev))

    def pe(self, fn, reads=(), writes=()):
        self.op("pe", fn, reads, writes)

    def act(self, fn, reads=(), writes=()):
        self.op("act", fn, reads, writes)

    def dve(self, fn, reads=(), writes=()):
        self.op("dve", fn, reads, writes)

    def pool(self, fn, reads=(), writes=()):
        self.op("pool", fn, reads, writes)

    def auto_sem(self, reads, writes):
        if writes:
            b = writes[0]
            if b.dw is None:
                b.dw = self.dma_sem("dw%d" % len(self.dsems))
            return b.dw
        b = reads[0]
        if b.dr is None:
            b.dr = self.dma_sem("dr%d" % len(self.dsems))
        return b.dr

    def dma(self, q, dsem, out, in_, reads=(), writes=(), raw=(), **kw):
        if dsem is None:
            dsem = self.auto_sem(reads, writes)
        self.op(q, lambda e: e.dma_start(out=out, in_=in_, **kw), reads, writes, dsem=dsem, raw=raw)

    def dma_group(self, q, dsem, pairs, reads=(), writes=(), raw=(), **kw):
        for i, (out, in_) in enumerate(pairs):
            if i == 0:
                self.op(q, lambda e, out=out, in_=in_: e.dma_start(out=out, in_=in_, **kw), reads, writes, dsem=dsem,
                        raw=raw)
            else:
                self.op(q, lambda e, out=out, in_=in_: e.dma_start(out=out, in_=in_, **kw), (), (), dsem=dsem)
        ev = (dsem, dsem.n)
        for b in reads:
            b.r[dsem] = dsem.n
        for b in writes:
            b.w = ev
            b.r = {}

    def emit(self):
        nc = self.nc
        fin = []
        for e in ENGS:
            if self.count[e]:
                fin.append((e, self.count[e]))
        for d in self.dsems:
            if d.n:
                fin.append((d, d.n))

        def semh(k):
            return self.esem[k] if isinstance(k, str) else k.h

        def run(engname, engobj):
            for waits, fn, ev in self.ops[engname]:
                for k, v in waits:
                    engobj.wait_ge(semh(k), v)
                ins = getattr(engobj, fn[0])(*fn[1], **fn[2])
                if isinstance(ev[0], str):
                    ins.then_inc(self.esem[ev[0]], 1)
                else:
                    ins.then_inc(ev[0].h, 16)
            if engname == "sp":
                for k, v in fin:
                    engobj.wait_ge(semh(k), v)

        with nc.Block() as block:
            @block.tensor
            def _(e):
                run("pe", e)

            @block.scalar
            def _(e):
                run("act", e)

            @block.vector
            def _(e):
                run("dve", e)

            @block.gpsimd
            def _(e):
                run("pool", e)

            @block.sync
            def _(e):
                run("sp", e)


class Rot:
    def __init__(self, tiles):
        self.tiles = tiles
        self.bufs = [Buf() for _ in tiles]
        self.i = 0

    def get(self):
        k = self.i % len(self.tiles)
        self.i += 1
        return self.tiles[k], self.bufs[k]


D = 1024
INW = 7688
NPAST = 2048
NSS = 4
LS = 16
VW = 68
C_AV, C_AG, C_Q, C_K, C_V, C_F = 0, 512, 1024, 1536, 2048, 2560
C_HQ, C_HF, C_HI, C_HO = 2568, 3080, 3592, 4104
C_GA, C_GB, C_GC = 4616, 5640, 6664

WNAMES = ["norm1_g", "w_in", "conv_w", "conv_b", "conv_ln_g", "conv_ln_b", "w_a_out", "fox_bf", "w_b_out",
          "hgrn_lb_param", "hgrn_norm_g", "w_c_out", "w_o", "norm2_g", "w_up", "w_down", "final_g"]


def build_program(SEQ=8192, L=4, T=512, SAMPLE=True, PROMPT=True):
    nc = bass.Bass("TRN2", target_bir_lowering=False)
    NT = SEQ // T
    NSUB = T // 128
    NCH = T // 64

    def din(name, shape):
        return nc.dram_tensor(name, shape, F32, kind="ExternalInput").ap()

    def dout(name, shape):
        return nc.dram_tensor(name, shape, F32, kind="ExternalOutput").ap()

    xp = din("xp", [SEQ, D])
    xs = din("xs", [NSS * LS, D])
    ck = din("ck", [L, NSS, NPAST, 512])
    cv = din("cv", [L, NSS, NPAST, 512])
    clf = din("clf", [L, NSS, NPAST, 8])
    sconv = din("sconv", [L, NSS, 30, 512])
    shg = din("shg", [L, NSS, 4, 128, 128])
    Wd = {
        "norm1_g": din("norm1_g", [L, D]), "w_in": din("w_in", [L, D, INW]), "conv_w": din("conv_w", [L, 31, 512]),
        "conv_b": din("conv_b", [L, 512]), "conv_ln_g": din("conv_ln_g", [L, 512]),
        "conv_ln_b": din("conv_ln_b", [L, 512]), "w_a_out": din("w_a_out", [L, 512, D]),
        "fox_bf": din("fox_bf", [L, 8]), "w_b_out": din("w_b_out", [L, 512, D]),
        "hgrn_lb_param": din("hgrn_lb_param", [L, 512]), "hgrn_norm_g": din("hgrn_norm_g", [L, 128]),
        "w_c_out": din("w_c_out", [L, 512, D]), "w_o": din("w_o", [L, D, D]), "norm2_g": din("norm2_g", [L, D]),
        "w_up": din("w_up", [L, D, 4096]), "w_down": din("w_down", [L, 4096, D]), "final_g": din("final_g", [1, D]),
    }
    yp = dout("yp", [SEQ, D])
    ys = dout("ys", [NSS * LS, D])
    kp = dout("kp", [L, SEQ, 512])
    vp = dout("vp", [L, SEQ, 512])
    lfp = dout("lfp", [L, SEQ, 8])
    cp = dout("cp", [L, 30, 512])
    hp = dout("hp", [L, 4, 128, 128])
    ks = dout("ks", [L, NSS, LS, 512])
    vs = dout("vs", [L, NSS, LS, 512])
    lfs = dout("lfs", [L, NSS, LS, 8])
    cs = dout("cs", [L, NSS, 30, 512])
    hs = dout("hs", [L, NSS, 4, 128, 128])
    kscr = nc.dram_tensor("kscr", [L, 68, 8, SEQ], BF16, kind="Internal").ap()
    vscr = nc.dram_tensor("vscr", [L, 128, 8, SEQ // 128, VW], BF16, kind="Internal").ap()

    with contextlib.ExitStack() as st:
        S = Sched(nc, st)
        _n = [0]

        def sb(shape, dt, name=None):
            _n[0] += 1
            return st.enter_context(nc.sbuf_tensor(name or ("t%d" % _n[0]), shape, dt))

        def pst(shape, dt, name=None):
            _n[0] += 1
            return st.enter_context(nc.psum_tensor(name or ("p%d" % _n[0]), shape, dt))

        xT = sb([128, 8, T], F32, "xT")
        Bx = [Buf() for _ in range(8)]
        hT = sb([128, 8, T], BF16, "hT")
        Bh = [Buf() for _ in range(8)]
        sq = sb([128, 8, T], BF16, "sq")
        Bsq = [Buf() for _ in range(8)]
        mT = sb([128, 8, T], F32, "mT")
        Bm = [Buf() for _ in range(8)]
        brT = sb([128, 4, T], BF16, "brT")
        Bbr = [Buf() for _ in range(4)]
        stg = Rot([sb([128, 1024], F32, "stage%d" % i) for i in range(2)])
        NW = 4
        WSZ = 8 * 520
        wrot = Rot([sb([128, WSZ], BF16, "w%d" % i) for i in range(NW)])
        wsem = [S.dma_sem("wsem%d" % i) for i in range(NW)]
        ft = Rot([sb([128, T], F32, "ft%d" % i) for i in range(8)])
        psg = Rot([pst([128, 512], F32, "psg%d" % i) for i in range(4)])
        psS = Rot([pst([128, 512], F32, "psS%d" % i) for i in range(2)])
        psO = [pst([128, 4, 128], F32, "psO%d" % i) for i in range(2)]
        BpsO = [Buf(), Buf()]
        ident = sb([128, 128], F32, "ident")
        triu = sb([128, 128], F32, "triu")
        triub = sb([128, 128], BF16, "triub")
        onesD = sb([128, 128], BF16, "onesD")
        onesC = sb([128, 128], BF16, "onesC")
        onesV = sb([128, 128], BF16, "onesV")
        ones = sb([128, T], F32, "ones")
        g1 = sb([128, L, 8], F32, "g1")
        g2 = sb([128, L, 8], F32, "g2")
        gf = sb([128, 8], F32, "gf")
        cw = sb([128, L, 4, 31], F32, "cw")
        cb = sb([128, L, 4], F32, "cb")
        lng = sb([128, L, 4], F32, "lng")
        lnb = sb([128, L, 4], F32, "lnb")
        nbf = sb([8, L], F32, "nbf")
        bfb = sb([128, L, 8], F32, "bfb")
        lbp = sb([128, 4, L], F32, "lbp")
        lb = sb([128, 4, L], F32, "lb")
        oml = sb([128, 4, L], F32, "oml")
        noml = sb([128, 4, L], F32, "noml")
        lbt = sb([128, 4, L], F32, "lbt")
        lbm = sb([128, 4], F32, "lbm")
        hng = sb([128, L], F32, "hng")
        epsb = sb([128, 1], F32, "epsb")
        Bc = Buf()
        csem = S.dma_sem("csem")
        hist = sb([128, L, 4, 30], F32, "hist")
        Bhist = [[Buf() for _ in range(4)] for _ in range(L)]
        Sst = sb([128, L, 4, 128], F32, "Sst")
        BS = [[Buf() for _ in range(4)] for _ in range(L)]
        Sbf = Rot([sb([128, 128], BF16, "Sbf%d" % i) for i in range(2)])
        ccar = sb([8, L], F32, "ccar")
        Bccar = [Buf() for _ in range(L)]
        ubuf = sb([128, 4, 30 + T], F32, "ubuf")
        Bub = [Buf() for _ in range(4)]
        yc = sb([128, 4, T], F32, "yc")
        Byc = [Buf() for _ in range(4)]
        ybf = sq[:, 0:4, :]
        ysq = sq[:, 4:8, :]
        Bybf = Bsq[0:4]
        Bysq = Bsq[4:8]
        Qa = sb([68, 8, T], BF16, "Qa")
        Ka = sb([68, 8, T], BF16, "Ka")
        BQ = [Buf() for _ in range(8)]
        BK = [Buf() for _ in range(8)]
        Va = sb([128, 8, NSUB, VW], BF16, "Va")
        BV = Buf()
        ksrc = Rot([sb([68, 2, 512], BF16, "ksrc%d" % i) for i in range(2)])
        vsrc = Rot([sb([128, 2, 4, VW], BF16, "vsrc%d" % i) for i in range(2)])
        kvsem = [S.dma_sem("kvsem%d" % i) for i in range(2)]
        Prot = Rot([sb([128, 512], BF16, "P%d" % i) for i in range(3)])
        otok = yc
        Botok = Byc
        rc = Rot([sb([128, 4], F32, "rc%d" % i) for i in range(2)])
        fT = sb([8, T], F32, "fT")
        csp = sb([8, T], F32, "csp")
        chi = sb([8, T], BF16, "chi")
        clo = sb([8, T], BF16, "clo")
        nhi = sb([8, T], BF16, "nhi")
        nlo = sb([8, T], BF16, "nlo")
        Bf = Buf()
        lftok = sb([128, NSUB, 8], F32, "lftok")
        Blftok = Buf()
        scrsem = S.dma_sem("scrsem")
        Bkscr = [[Buf() for _ in range(NT + 1)] for _ in range(L)]
        Bvscr = [[Buf() for _ in range(NT + 1)] for _ in range(L)]
        osem = S.dma_sem("osem")
        stsem = S.dma_sem("stsem")
        lfsem = S.dma_sem("lfsem")
        ksem_ = S.dma_sem("kscrsem")
        vsem_ = S.dma_sem("vscrsem")
        rowsem = S.dma_sem("rowsem")
        xsem = S.dma_sem("xsem")
        qs = sb([128, T], F32, "qs")
        kk = sb([128, T], F32, "kk")
        bg = sb([128, T], F32, "bg")
        Bqs, Bkk, Bbg = Buf(), Buf(), Buf()
        qt = sb([128, T], BF16, "qt")
        kt_ = sb([128, T], BF16, "kt_")
        qh = sb([128, T], BF16, "qh")
        kh = sb([128, T], F32, "kh")
        Bqt, Bkt, Bqh, Bkh = Buf(), Buf(), Buf(), Buf()
        vtok = sb([64, NCH, 512], BF16, "vtok")
        Bvtok = [Buf() for _ in range(NCH)]
        khtok = sb([64, NCH, 128], BF16, "khtok")
        Bkhtok = [Buf() for _ in range(NCH)]
        Abf = sb([64, NCH, 64], BF16, "Abf")
        BAbf = Buf()
        sqo = sb([128, T], BF16, "sqo")
        Bsqo = Buf()

        slm = sb([128, 128], F32, "slm")
        bmask = sb([64, 64], BF16, "bmask")
        lfc = sb([128, 16, 8], F32, "lfc")
        rk = sb([128, 16, 8], F32, "rk")
        rkt = sb([128, 16, 8], F32, "rkt")
        Brk = Buf()
        Ssm = Rot([sb([128, 128], F32, "Ssm%d" % i) for i in range(2)])

        PQ = "sp"
        WQ = "pool"

        def init_consts():
            S.pool(lambda e: e.memset(ident[:], 1.0), writes=[Bc])
            S.pool(lambda e: e.affine_select(ident[:], ident[:], [[-1, 128]], ALU.is_equal, 0.0, base=0,
                                             channel_multiplier=1), writes=[Bc])
            S.pool(lambda e: e.memset(triu[:], 1.0), writes=[Bc])
            S.pool(lambda e: e.affine_select(triu[:], triu[:], [[1, 128]], ALU.is_ge, 0.0, base=0,
                                             channel_multiplier=-1), writes=[Bc])
            S.pool(lambda e: e.tensor_copy(triub[:], triu[:]), writes=[Bc])
            S.pool(lambda e: e.memset(slm[:], 1.0), writes=[Bc])
            S.pool(lambda e: e.affine_select(slm[:], slm[:], [[-1, 128]], ALU.is_gt, 0.0, base=0,
                                             channel_multiplier=1), writes=[Bc])
            S.pool(lambda e: e.tensor_copy(bmask[:], triu[0:64, 0:64]), writes=[Bc])
            for b_ in range(1, 4):
                S.pool(lambda e, b_=b_: e.memset(bmask[0:16 * b_, 16 * b_:16 * b_ + 16], 0.0), writes=[Bc])
            S.pool(lambda e: e.memset(onesD[:], 1.0 / 1024), writes=[Bc])
            S.pool(lambda e: e.memset(onesC[:], 1.0 / 512), writes=[Bc])
            S.pool(lambda e: e.memset(onesV[:], 1.0 / 128), writes=[Bc])
            S.pool(lambda e: e.memset(ones[:], 1.0), writes=[Bc])
            S.pool(lambda e: e.memset(epsb[:], EPS), writes=[Bc])
            S.pool(lambda e: e.memset(Qa[64:68, :, :], 1.0), writes=BQ)
            S.pool(lambda e: e.memset(Ka[64:68, :, :], 1.0), writes=BK)
            S.pool(lambda e: e.memset(Va[:, :, :, 64:VW], 1.0), writes=[BV])
            S.pool(lambda e: e.memset(hist[:], 0.0), writes=[b for bl in Bhist for b in bl])
            S.pool(lambda e: e.memset(Sst[:], 0.0), writes=[b for bl in BS for b in bl])
            S.pool(lambda e: e.memset(ccar[:], 0.0), writes=Bccar)
            W = Wd
            S.dma(PQ, csem, g1[:], W["norm1_g"].rearrange("l (c p) -> p l c", p=128), writes=[Bc])
            S.dma(PQ, csem, g2[:], W["norm2_g"].rearrange("l (c p) -> p l c", p=128), writes=[Bc])
            S.dma(PQ, csem, gf[:], W["final_g"][0].rearrange("(c p) -> p c", p=128), writes=[Bc])
            for l in range(L):
                for c in range(4):
                    S.dma(PQ, csem, cw[:, l, c, :], W["conv_w"][l][:, c * 128:(c + 1) * 128].rearrange("j p -> p j"),
                          writes=[Bc])
            S.dma(PQ, csem, cb[:], W["conv_b"].rearrange("l (c p) -> p l c", p=128), writes=[Bc])
            S.dma(PQ, csem, lng[:], W["conv_ln_g"].rearrange("l (c p) -> p l c", p=128), writes=[Bc])
            S.dma(PQ, csem, lnb[:], W["conv_ln_b"].rearrange("l (c p) -> p l c", p=128), writes=[Bc])
            S.dma(PQ, csem, nbf[:], W["fox_bf"].rearrange("l h -> h l"), writes=[Bc])
            for l in range(L):
                S.dma(PQ, csem, bfb[:, l:l + 1, :], W["fox_bf"][l:l + 1, :].partition_broadcast(128), writes=[Bc])
            for hd_ in range(4):
                S.dma(PQ, csem, lbp[:, hd_, :], W["hgrn_lb_param"][:, hd_ * 128:(hd_ + 1) * 128].rearrange("l p -> p l"),
                      writes=[Bc])
            S.dma(PQ, csem, hng[:], W["hgrn_norm_g"].rearrange("l v -> v l"), writes=[Bc])
            S.dve(lambda e: e.tensor_scalar(nbf[:], nbf[:], -1.0, None, ALU.mult), reads=[Bc], writes=[Bc])
            S.dve(lambda e: e.tensor_reduce(lbm[:], lbp[:], AX.X, ALU.max), reads=[Bc], writes=[Bc])
            S.dve(lambda e: e.tensor_tensor(lbt[:], lbp[:], lbm[:].unsqueeze(2).to_broadcast([128, 4, L]),
                                            ALU.subtract), writes=[Bc])
            S.act(lambda e: e.activation(lbt[:], lbt[:], AF.Exp), reads=[Bc], writes=[Bc])
            S.dve(lambda e: e.tensor_reduce(lbm[:], lbt[:], AX.X, ALU.add), reads=[Bc], writes=[Bc])
            S.dve(lambda e: e.reciprocal(lbm[:], lbm[:]), writes=[Bc])
            S.dve(lambda e: e.tensor_tensor(lbt[:], lbt[:], lbm[:].unsqueeze(2).to_broadcast([128, 4, L]),
                                            ALU.mult), writes=[Bc])
            S.dve(lambda e: e.memset(lb[:], 0.0), writes=[Bc])
            for l in range(1, L):
                S.dve(lambda e, l=l: e.tensor_tensor(lb[:, :, l:l + 1], lb[:, :, l - 1:l], lbt[:, :, l:l + 1],
                                                     ALU.add), writes=[Bc])
            S.dve(lambda e: e.tensor_scalar(oml[:], lb[:], -1.0, 1.0, ALU.mult, ALU.add), writes=[Bc])
            S.dve(lambda e: e.tensor_scalar(noml[:], oml[:], -1.0, None, ALU.mult), writes=[Bc])

        def rsqrt_eps(out_ap, in_ap, reads, wbuf):
            S.act(lambda e: e.activation(out_ap, in_ap, AF.Sqrt, bias=epsb[:, 0:1]), reads=list(reads) + [Bc],
                  writes=[wbuf])
            S.dve(lambda e: e.reciprocal(out_ap, out_ap), writes=[wbuf])

        def wload(src2d, kc, cols):
            best, bi_ = None, None
            for i_ in range(NW):
                b_ = wrot.bufs[i_]
                if b_.w is not None and not b_.r:
                    continue
                key = b_.r.get("pe", 0)
                if best is None or key < best:
                    best, bi_ = key, i_
            assert bi_ is not None, "no free weight slot"
            t, b = wrot.tiles[bi_], wrot.bufs[bi_]
            sem = wsem[bi_]
            view = t[:, 0:kc * cols].rearrange("p (k c) -> p k c", c=cols)
            S.dma(WQ, sem, view, src2d.rearrange("(k p) c -> p k c", p=128), writes=[b])
            return view, b

        def mm_fm(ps, M, N, wt, wb, col0, nk, act, Bact, extra_w=()):
            for k in range(nk):
                S.pe(lambda e, k=k: e.matmul(ps[0:M, 0:N], lhsT=wt[:, k, col0:col0 + M], rhs=act[:, k, 0:N],
                                             start=(k == 0), stop=(k == nk - 1)),
                     reads=[wb, Bact[k]], writes=list(extra_w))

        def rmsnorm(gt, l, N):
            for c in range(8):
                S.act(lambda e, c=c: e.activation(sq[:, c, 0:N], xT[:, c, 0:N], AF.Square),
                      reads=[Bx[c]], writes=[Bsq[c]])
            ps, bp = psg.get()
            for c in range(8):
                S.pe(lambda e, c=c: e.matmul(ps[:, 0:N], lhsT=onesD[:], rhs=sq[:, c, 0:N], start=(c == 0),
                                             stop=(c == 7)), reads=[Bc, Bsq[c]], writes=[bp])
            rs, brs = ft.get()
            rsqrt_eps(rs[:, 0:N], ps[:, 0:N], [bp], brs)
            for c in range(8):
                gap = gt[:, l, c:c + 1] if l is not None else gt[:, c:c + 1]
                S.dve(lambda e, c=c, gap=gap: e.scalar_tensor_tensor(hT[:, c, 0:N], xT[:, c, 0:N], gap, rs[:, 0:N],
                                                                     ALU.mult, ALU.mult),
                      reads=[Bx[c], brs, Bc], writes=[Bh[c]])

        def branch_out(l, wname, gcol, N, first, last):
            wo_, bwo = wload(Wd[wname][l], 4, 1024)
            wg = [wload(Wd["w_in"][l][:, gcol + i * 512: gcol + (i + 1) * 512], 8, 512) for i in range(2)]
            for oc in range(8):
                psy, bpy = psg.get()
                for k in range(4):
                    S.pe(lambda e, k=k, oc=oc: e.matmul(psy[:, 0:N], lhsT=wo_[:, k, oc * 128:(oc + 1) * 128],
                                                        rhs=brT[:, k, 0:N], start=(k == 0), stop=(k == 3)),
                         reads=[bwo, Bbr[k]], writes=[bpy])
                psgt, bpg = psg.get()
                wgt, bwg = wg[oc // 4]
                mm_fm(psgt, 128, N, wgt, bwg, (oc % 4) * 128, 8, hT, Bh, extra_w=[bpg])
                sg, bsg = ft.get()
                S.act(lambda e: e.activation(sg[:, 0:N], psgt[:, 0:N], AF.Sigmoid), reads=[bpg], writes=[bsg])
                if first:
                    S.dve(lambda e, oc=oc: e.tensor_tensor(mT[:, oc, 0:N], sg[:, 0:N], psy[:, 0:N], ALU.mult),
                          reads=[bsg, bpy], writes=[Bm[oc]])
                else:
                    S.dve(lambda e: e.tensor_tensor(sg[:, 0:N], sg[:, 0:N], psy[:, 0:N], ALU.mult),
                          reads=[bpy], writes=[bsg])
                    if last:
                        S.dve(lambda e, oc=oc: e.tensor_tensor(sq[:, oc, 0:N], mT[:, oc, 0:N], sg[:, 0:N], ALU.add),
                              reads=[bsg, Bm[oc]], writes=[Bsq[oc]])
                    else:
                        S.dve(lambda e, oc=oc: e.tensor_tensor(mT[:, oc, 0:N], mT[:, oc, 0:N], sg[:, 0:N], ALU.add),
                              reads=[bsg], writes=[Bm[oc]])

        def conv_branch(l, N, segs):
            wA, bA = wload(Wd["w_in"][l][:, C_AV:C_AV + 512], 8, 512)
            wG, bG = wload(Wd["w_in"][l][:, C_AG:C_AG + 512], 8, 512)
            for c in range(4):
                psv, bpv = psg.get()
                mm_fm(psv, 128, N, wA, bA, c * 128, 8, hT, Bh, extra_w=[bpv])
                psg_, bpg = psg.get()
                mm_fm(psg_, 128, N, wG, bG, c * 128, 8, hT, Bh, extra_w=[bpg])
                sg, bsg = ft.get()
                S.act(lambda e: e.activation(sg[:, 0:N], psg_[:, 0:N], AF.Sigmoid), reads=[bpg], writes=[bsg])
                av, bav = ft.get()
                S.act(lambda e: e.activation(av[:, 0:N], psv[:, 0:N], AF.Copy), reads=[bpv], writes=[bav])
                segs(l, c, av, bav, sg, bsg)

        def conv_post(l, N):
            for c in range(4):
                S.act(lambda e, c=c: e.activation(ybf[:, c, 0:N], yc[:, c, 0:N], AF.Copy), reads=[Byc[c]],
                      writes=[Bybf[c]])
                S.act(lambda e, c=c: e.activation(ysq[:, c, 0:N], yc[:, c, 0:N], AF.Square), reads=[Byc[c]],
                      writes=[Bysq[c]])
            psm, bpm = psg.get()
            for c in range(4):
                S.pe(lambda e, c=c: e.matmul(psm[:, 0:N], lhsT=onesC[:], rhs=ybf[:, c, 0:N], start=(c == 0),
                                             stop=(c == 3)), reads=[Bc, Bybf[c]], writes=[bpm])
            pss, bps = psg.get()
            for c in range(4):
                S.pe(lambda e, c=c: e.matmul(pss[:, 0:N], lhsT=onesC[:], rhs=ysq[:, c, 0:N], start=(c == 0),
                                             stop=(c == 3)), reads=[Bc, Bysq[c]], writes=[bps])
            mean, bmean = ft.get()
            S.dve(lambda e: e.tensor_copy(mean[:, 0:N], psm[:, 0:N]), reads=[bpm], writes=[bmean])
            var, bvar = ft.get()
            S.dve(lambda e: e.tensor_tensor(var[:, 0:N], mean[:, 0:N], mean[:, 0:N], ALU.mult), reads=[bmean],
                  writes=[bvar])
            S.dve(lambda e: e.tensor_tensor(var[:, 0:N], pss[:, 0:N], var[:, 0:N], ALU.subtract), reads=[bps],
                  writes=[bvar])
            rsqrt_eps(var[:, 0:N], var[:, 0:N], [], bvar)
            S.dve(lambda e: e.scalar_tensor_tensor(mean[:, 0:N], mean[:, 0:N], -1.0, var[:, 0:N], ALU.mult, ALU.mult),
                  reads=[bvar], writes=[bmean])
            for c in range(4):
                t, bt = ft.get()
                S.dve(lambda e, c=c: e.tensor_tensor(t[:, 0:N], yc[:, c, 0:N], var[:, 0:N], ALU.mult),
                      reads=[Byc[c], bvar], writes=[bt])
                S.dve(lambda e: e.tensor_tensor(t[:, 0:N], t[:, 0:N], mean[:, 0:N], ALU.add), reads=[bmean],
                      writes=[bt])
                S.act(lambda e, c=c: e.activation(brT[:, c, 0:N], t[:, 0:N], AF.Silu, bias=lnb[:, l, c:c + 1],
                                                  scale=lng[:, l, c:c + 1]), reads=[bt, Bc], writes=[Bbr[c]])

        def conv_taps(l, c, u0, y0, n):
            S.dve(lambda e: e.tensor_scalar(yc[:, c, y0:y0 + n], ubuf[:, c, u0:u0 + n], cw[:, l, c, 0:1],
                                            cb[:, l, c:c + 1], ALU.mult, ALU.add),
                  reads=[Bub[c], Bc], writes=[Byc[c]])
            for j in range(1, 31):
                S.dve(lambda e, j=j: e.scalar_tensor_tensor(yc[:, c, y0:y0 + n], ubuf[:, c, u0 + j:u0 + j + n],
                                                            cw[:, l, c, j:j + 1], yc[:, c, y0:y0 + n],
                                                            ALU.mult, ALU.add), reads=[Bub[c]], writes=[Byc[c]])

        def load_x_tile(src, N):
            nsub = (N + 127) // 128
            for i in range(nsub):
                n = min(128, N - i * 128)
                stt, bst_ = stg.get()
                S.dma(PQ, None, stt[0:n, :], src[i * 128:i * 128 + n, :], writes=[bst_])
                for half in range(2):
                    ps, bp = psg.get()
                    for cc in range(4):
                        c = half * 4 + cc
                        S.pe(lambda e, cc=cc, c=c, n=n, ps=ps, stt=stt: e.transpose(
                            ps[:, cc * 128:cc * 128 + n], stt[0:n, c * 128:(c + 1) * 128], ident[0:n, 0:n]),
                             reads=[bst_, Bc], writes=[bp])
                    S.act(lambda e, i=i, n=n, half=half, ps=ps: e.activation(
                        xT[:, half * 4:half * 4 + 4, i * 128:i * 128 + n],
                        ps[:, :].rearrange("p (c t) -> p c t", t=128)[:, :, 0:n], AF.Copy),
                          reads=[bp], writes=Bx[half * 4:half * 4 + 4])

        def store_y_tile(dst, N):
            for c in range(8):
                S.act(lambda e, c=c: e.activation(sq[:, c, 0:N], xT[:, c, 0:N], AF.Square),
                      reads=[Bx[c]], writes=[Bsq[c]])
            ps, bp = psg.get()
            for c in range(8):
                S.pe(lambda e, c=c: e.matmul(ps[:, 0:N], lhsT=onesD[:], rhs=sq[:, c, 0:N], start=(c == 0),
                                             stop=(c == 7)), reads=[Bc, Bsq[c]], writes=[bp])
            rs, brs = ft.get()
            rsqrt_eps(rs[:, 0:N], ps[:, 0:N], [bp], brs)
            for c in range(8):
                S.dve(lambda e, c=c: e.scalar_tensor_tensor(mT[:, c, 0:N], xT[:, c, 0:N], gf[:, c:c + 1], rs[:, 0:N],
                                                            ALU.mult, ALU.mult),
                      reads=[Bx[c], brs, Bc], writes=[Bm[c]])
            nsub = (N + 127) // 128
            for i in range(nsub):
                n = min(128, N - i * 128)
                stt, bst_ = stg.get()
                for half in range(2):
                    ps, bp = psg.get()
                    for cc in range(4):
                        c = half * 4 + cc
                        S.pe(lambda e, c=c, cc=cc, i=i, n=n, ps=ps: e.transpose(ps[0:n, cc * 128:(cc + 1) * 128],
                                                                         mT[:, c, i * 128:i * 128 + n], ident[:]),
                             reads=[Bm[c], Bc], writes=[bp])
                    S.act(lambda e, n=n, half=half, ps=ps, stt=stt: e.activation(
                        stt[0:n, half * 512:(half + 1) * 512], ps[0:n, :], AF.Copy), reads=[bp], writes=[bst_])
                S.dma(PQ, None, dst[i * 128:i * 128 + n, :], stt[0:n, :], reads=[bst_])

        def ffn_and_wo(l, N):
            wO = [wload(Wd["w_o"][l][:, i * 512:(i + 1) * 512], 8, 512) for i in range(2)]
            for oc in range(8):
                ps, bp = psg.get()
                wt, bw = wO[oc // 4]
                mm_fm(ps, 128, N, wt, bw, (oc % 4) * 128, 8, sq, Bsq, extra_w=[bp])
                S.dve(lambda e, oc=oc: e.tensor_tensor(xT[:, oc, 0:N], xT[:, oc, 0:N], ps[:, 0:N], ALU.add),
                      reads=[bp], writes=[Bx[oc]])
            rmsnorm(g2, l, N)
            for q in range(4):
                for j in range(2):
                    wU, bU = wload(Wd["w_up"][l][:, (q * 2 + j) * 512:(q * 2 + j + 1) * 512], 8, 512)
                    for cc in range(4):
                        ps, bp = psg.get()
                        mm_fm(ps, 128, N, wU, bU, cc * 128, 8, hT, Bh, extra_w=[bp])
                        r1, br1 = ft.get()
                        S.act(lambda e: e.activation(r1[:, 0:N], ps[:, 0:N], AF.Relu), reads=[bp], writes=[br1])
                        S.dve(lambda e, j=j, cc=cc: e.tensor_tensor(sq[:, j * 4 + cc, 0:N], r1[:, 0:N], r1[:, 0:N],
                                                                    ALU.mult), reads=[br1], writes=[Bsq[j * 4 + cc]])
                wDn = [wload(Wd["w_down"][l][q * 1024:(q + 1) * 1024, i * 512:(i + 1) * 512], 8, 512) for i in range(2)]
                for oc in range(8):
                    ps, bp = psg.get()
                    wt, bw = wDn[oc // 4]
                    mm_fm(ps, 128, N, wt, bw, (oc % 4) * 128, 8, sq, Bsq, extra_w=[bp])
                    S.dve(lambda e, oc=oc: e.tensor_tensor(xT[:, oc, 0:N], xT[:, oc, 0:N], ps[:, 0:N], ALU.add),
                          reads=[bp], writes=[Bx[oc]])

        def fox_qkv(l, N, tok0, kdst, vdst, lfdst, carry=True):
            wQ, bQ = wload(Wd["w_in"][l][:, C_Q:C_Q + 512], 8, 512)
            wK, bK = wload(Wd["w_in"][l][:, C_K:C_K + 512], 8, 512)
            wV, bV = wload(Wd["w_in"][l][:, C_V:C_V + 520], 8, 520)
            for hh in range(8):
                ps, bp = psg.get()
                mm_fm(ps, 64, N, wQ, bQ, hh * 64, 8, hT, Bh, extra_w=[bp])
                S.act(lambda e, hh=hh: e.activation(Qa[0:64, hh, 0:N], ps[0:64, 0:N], AF.Copy, scale=0.125),
                      reads=[bp], writes=[BQ[hh]])
                ps2, bp2 = psg.get()
                mm_fm(ps2, 64, N, wK, bK, hh * 64, 8, hT, Bh, extra_w=[bp2])
                S.act(lambda e, hh=hh: e.activation(Ka[0:64, hh, 0:N], ps2[0:64, 0:N], AF.Copy), reads=[bp2],
                      writes=[BK[hh]])
            ps, bp = psg.get()
            mm_fm(ps, 8, N, wV, bV, 512, 8, hT, Bh, extra_w=[bp])
            S.act(lambda e: e.activation(fT[:, 0:N], ps[0:8, 0:N], AF.Exp, bias=nbf[:, l:l + 1], scale=-1.0),
                  reads=[bp, Bc], writes=[Bf])
            S.act(lambda e: e.activation(fT[:, 0:N], fT[:, 0:N], AF.Ln, bias=1.0), writes=[Bf])
            return wK, bK, wV, bV

        def fox_scan_rows(l, N, c0, n, init_ap, Binit):
            S.dve(lambda e: e.tensor_tensor_scan(csp[:, c0:c0 + n], ones[0:8, 0:n], fT[:, c0:c0 + n], init_ap,
                                                 ALU.mult, ALU.add), reads=[Bf, Bc] + Binit, writes=[Bf])

        def fox_split_rows(N):
            S.dve(lambda e: e.tensor_copy(chi[:, 0:N], csp[:, 0:N]), writes=[Bf])
            S.dve(lambda e: e.tensor_tensor(clo[:, 0:N], csp[:, 0:N], chi[:, 0:N], ALU.subtract), writes=[Bf])
            S.dve(lambda e: e.tensor_scalar(nhi[:, 0:N], chi[:, 0:N], -1.0, None, ALU.mult), writes=[Bf])
            S.dve(lambda e: e.tensor_scalar(nlo[:, 0:N], clo[:, 0:N], -1.0, None, ALU.mult), writes=[Bf])
            pairs = []
            for hh in range(8):
                pairs.append((Qa[64:65, hh, 0:N], nhi[hh:hh + 1, 0:N]))
                pairs.append((Qa[65:66, hh, 0:N], nlo[hh:hh + 1, 0:N]))
                pairs.append((Ka[66:67, hh, 0:N], chi[hh:hh + 1, 0:N]))
                pairs.append((Ka[67:68, hh, 0:N], clo[hh:hh + 1, 0:N]))
            S.dma_group(PQ, rowsem, pairs, reads=[Bf], writes=BQ + BK)

        def fox_tok_outputs(l, N, wK, bK, wV, bV, kdst_fn, vdst_fn, lfdst_fn, va_fn):
            nsub = (N + 127) // 128
            for i in range(nsub):
                n = min(128, N - i * 128)
                stt, bst_ = stg.get()
                psk, bpk = psg.get()
                for k in range(8):
                    S.pe(lambda e, k=k, i=i, n=n, psk=psk: e.matmul(psk[0:n, :], lhsT=hT[:, k, i * 128:i * 128 + n],
                                                           rhs=wK[:, k, 0:512], start=(k == 0), stop=(k == 7)),
                         reads=[bK, Bh[k]], writes=[bpk])
                S.act(lambda e, n=n, psk=psk, stt=stt: e.activation(stt[0:n, 0:512], psk[0:n, :], AF.Copy),
                      reads=[bpk], writes=[bst_])
                psv, bpv = psg.get()
                for k in range(8):
                    S.pe(lambda e, k=k, i=i, n=n, psv=psv: e.matmul(psv[0:n, :], lhsT=hT[:, k, i * 128:i * 128 + n],
                                                           rhs=wV[:, k, 0:512], start=(k == 0), stop=(k == 7)),
                         reads=[bV, Bh[k]], writes=[bpv])
                S.act(lambda e, n=n, psv=psv, stt=stt: e.activation(stt[0:n, 512:1024], psv[0:n, :], AF.Copy),
                      reads=[bpv], writes=[bst_])
                if "nova" not in DBG2:
                    va_fn(i, n, stt, bst_)
                if "nolf" in DBG2:
                    kdst_fn(i, n, stt, bst_)
                    vdst_fn(i, n, stt, bst_)
                    continue
                psf, bpf = psg.get()
                S.pe(lambda e, i=i, n=n, psf=psf: e.transpose(psf[0:n, 0:8], fT[0:8, i * 128:i * 128 + n],
                                                              ident[0:8, 0:8]), reads=[Bf, Bc], writes=[bpf])
                S.act(lambda e, i=i, n=n, psf=psf: e.activation(lftok[0:n, i, :], psf[0:n, 0:8], AF.Copy, scale=-1.0),
                      reads=[bpf], writes=[Blftok])
                if "nokv" not in DBG2:
                    kdst_fn(i, n, stt, bst_)
                    vdst_fn(i, n, stt, bst_)
                if "nolfdma" not in DBG2:
                    lfdst_fn(i, n)

        def fox_attend_prompt(l, j, groups=(0, 1, 2, 3)):
            for g in groups:
                pairs = []
                for jj in range(j + 1):
                    for hl in range(2):
                        for kt in range(4):
                            pairs.append((jj, hl, kt))
                src = {}
                state = {}

                def get_src(jj):
                    if jj not in src:
                        kt_t, bkt = ksrc.get()
                        ksm = kvsem[(ksrc.i - 1) % 2]
                        vt_t, _unused = vsrc.get()
                        S.dma_group(WQ, ksm, [(kt_t[:], kscr[l, :, 2 * g:2 * g + 2, jj * T:(jj + 1) * T]),
                                              (vt_t[:], vscr[l, :, 2 * g:2 * g + 2, jj * 4:(jj + 1) * 4, :])],
                                    raw=[Bkscr[l][jj], Bvscr[l][jj]], writes=[bkt])
                        src[jj] = (kt_t, vt_t, bkt)
                    return src[jj]

                def emit_qk(n):
                    jj, hl, kt = pairs[n]
                    kt_t, vt_t, bkt = get_src(jj)
                    hh = 2 * g + hl
                    q0 = kt * 128 if jj == j else 0
                    ps, bp = psS.get()
                    S.pe(lambda e: e.matmul(ps[:, q0:T], lhsT=kt_t[:, hl, kt * 128:(kt + 1) * 128],
                                            rhs=Qa[:, hh, q0:T], start=True, stop=True),
                         reads=[bkt, BQ[hh]], writes=[bp])
                    state[n] = (ps, bp, q0)

                def emit_rest(n):
                    jj, hl, kt = pairs[n]
                    kt_t, vt_t, bkt = get_src(jj)
                    ps, bp, q0 = state.pop(n)
                    P, bP = Prot.get()
                    S.act(lambda e: e.activation(P[:, q0:T], ps[:, q0:T], AF.Exp), reads=[bp], writes=[bP])
                    if jj == j:
                        S.dve(lambda e: e.tensor_tensor(P[:, q0:q0 + 128], P[:, q0:q0 + 128], triub[:], ALU.mult),
                              reads=[Bc], writes=[bP])
                    for i in range(q0 // 128, 4):
                        S.pe(lambda e, i=i: e.matmul(psO[hl][:, i, 0:65], lhsT=P[:, i * 128:(i + 1) * 128],
                                                     rhs=vt_t[:, hl, kt, 0:65], start=first[hl], stop=False,
                                                     skip_group_check=True),
                             reads=[bP, bkt], writes=[BpsO[hl]])
                        first[hl] = False

                first = [True, True]
                emit_qk(0)
                for n in range(len(pairs)):
                    if n + 1 < len(pairs):
                        emit_qk(n + 1)
                    emit_rest(n)
                for hl in range(2):
                    hh = 2 * g + hl
                    r, br = rc.get()
                    S.dve(lambda e, hl=hl, r=r: e.reciprocal(r[:], psO[hl][:, :, 64]), reads=[BpsO[hl]], writes=[br])
                    for i in range(4):
                        S.dve(lambda e, hl=hl, hh=hh, i=i, r=r: e.tensor_scalar(
                            otok[:, i, hh * 64:(hh + 1) * 64], psO[hl][:, i, 0:64], r[:, i:i + 1], None, ALU.mult),
                              reads=[BpsO[hl], br], writes=[Botok[i]])

        def fox_o_transpose(N):
            nsub = (N + 127) // 128
            for fc in range(4):
                ps, bp = psg.get()
                for i in range(nsub):
                    n = min(128, N - i * 128)
                    S.pe(lambda e, i=i, n=n, fc=fc: e.transpose(ps[:, i * 128:i * 128 + n],
                                                                otok[0:n, i, fc * 128:(fc + 1) * 128],
                                                                ident[0:n, 0:n]),
                         reads=[Botok[i], Bc], writes=[bp])
                S.act(lambda e, fc=fc: e.activation(brT[:, fc, 0:N], ps[:, 0:N], AF.Copy), reads=[bp],
                      writes=[Bbr[fc]])

        def hgrn_branch(l, N, C, segs, S_in_fn, S_out_fn, between=None):
            nch = N // C
            wi, bi = wload(Wd["w_in"][l][:, C_HI:C_HI + 512], 8, 512)
            wq, bq = wload(Wd["w_in"][l][:, C_HQ:C_HQ + 512], 8, 512)
            wf, bf_ = wload(Wd["w_in"][l][:, C_HF:C_HF + 512], 8, 512)
            for ci in range(nch):
                ps, bp = psg.get()
                for k in range(8):
                    S.pe(lambda e, k=k, ci=ci: e.matmul(ps[0:C, :], lhsT=hT[:, k, ci * C:(ci + 1) * C],
                                                        rhs=wi[:, k, :], start=(k == 0), stop=(k == 7)),
                         reads=[bi, Bh[k]], writes=[bp])
                S.act(lambda e, ci=ci: e.activation(vtok[0:C, ci, :], ps[0:C, :], AF.Copy), reads=[bp],
                      writes=[Bvtok[ci]])
            wo_, bo = wload(Wd["w_in"][l][:, C_HO:C_HO + 512], 8, 512)
            mid = C // 2 - 1
            for hd in range(4):
                ps, bp = psg.get()
                mm_fm(ps, 128, N, wq, bq, hd * 128, 8, hT, Bh, extra_w=[bp])
                S.act(lambda e: e.activation(qs[:, 0:N], ps[:, 0:N], AF.Silu), reads=[bp], writes=[Bqs])
                ps2, bp2 = psg.get()
                mm_fm(ps2, 128, N, wf, bf_, hd * 128, 8, hT, Bh, extra_w=[bp2])
                sg, bsg = ft.get()
                S.act(lambda e: e.activation(sg[:, 0:N], ps2[:, 0:N], AF.Sigmoid), reads=[bp2], writes=[bsg])
                lg, blg = ft.get()
                S.act(lambda e, hd=hd: e.activation(lg[:, 0:N], sg[:, 0:N], AF.Ln, bias=lb[:, hd, l:l + 1],
                                                    scale=oml[:, hd, l:l + 1]), reads=[bsg, Bc], writes=[blg])
                S.dve(lambda e, hd=hd: e.tensor_scalar(kk[:, 0:N], sg[:, 0:N], noml[:, hd, l:l + 1],
                                                       oml[:, hd, l:l + 1], ALU.mult, ALU.add),
                      reads=[bsg, Bc], writes=[Bkk])
                for (c0, ncs, si) in segs:
                    S.dve(lambda e, c0=c0, ncs=ncs: e.tensor_tensor_scan(
                        bg[:, c0 * C:(c0 + ncs) * C], ones[:, 0:ncs * C], lg[:, c0 * C:(c0 + ncs) * C], 0.0,
                        ALU.mult, ALU.add), reads=[blg, Bc], writes=[Bbg])
                bg3 = bg[:, 0:N].rearrange("p (c s) -> p c s", s=C)
                dm, bdm = ft.get()
                dm3 = dm[:, 0:N].rearrange("p (c s) -> p c s", s=C)
                S.dve(lambda e: e.tensor_tensor(dm3, bg3, bg[:, mid:N:C].unsqueeze(2).to_broadcast([128, nch, C]),
                                                ALU.subtract), reads=[Bbg], writes=[bdm])
                bst, bbst = ft.get()
                S.dve(lambda e: e.tensor_tensor(bst[:, 0:nch], bg[:, 0:N:C], lg[:, 0:N:C], ALU.subtract),
                      reads=[Bbg, blg], writes=[bbst])
                ds, bds = ft.get()
                ds3 = ds[:, 0:N].rearrange("p (c s) -> p c s", s=C)
                S.dve(lambda e: e.tensor_tensor(ds3, bg3, bst[:, 0:nch].unsqueeze(2).to_broadcast([128, nch, C]),
                                                ALU.subtract), reads=[bbst, Bbg], writes=[bds])
                dl, bdl = ft.get()
                dl3 = dl[:, 0:N].rearrange("p (c s) -> p c s", s=C)
                S.dve(lambda e: e.tensor_tensor(dl3, bg3, bg[:, C - 1:N:C].unsqueeze(2).to_broadcast([128, nch, C]),
                                                ALU.subtract), reads=[Bbg], writes=[bdl])
                e1, be1 = ft.get()
                S.act(lambda e: e.activation(e1[:, 0:N], dm[:, 0:N], AF.Exp), reads=[bdm], writes=[be1])
                e2, be2 = ft.get()
                S.act(lambda e: e.activation(e2[:, 0:N], dm[:, 0:N], AF.Exp, scale=-1.0), reads=[bdm], writes=[be2])
                e3, be3 = ft.get()
                S.act(lambda e: e.activation(e3[:, 0:N], ds[:, 0:N], AF.Exp), reads=[bds], writes=[be3])
                e4, be4 = ft.get()
                S.act(lambda e: e.activation(e4[:, 0:N], dl[:, 0:N], AF.Exp, scale=-1.0), reads=[bdl], writes=[be4])
                S.dve(lambda e: e.tensor_tensor(qt[:, 0:N], qs[:, 0:N], e1[:, 0:N], ALU.mult), reads=[Bqs, be1],
                      writes=[Bqt])
                S.dve(lambda e: e.tensor_tensor(kt_[:, 0:N], kk[:, 0:N], e2[:, 0:N], ALU.mult), reads=[Bkk, be2],
                      writes=[Bkt])
                S.dve(lambda e: e.tensor_tensor(qh[:, 0:N], qs[:, 0:N], e3[:, 0:N], ALU.mult), reads=[Bqs, be3],
                      writes=[Bqh])
                S.dve(lambda e: e.tensor_tensor(kh[:, 0:N], kk[:, 0:N], e4[:, 0:N], ALU.mult), reads=[Bkk, be4],
                      writes=[Bkh])
                if between is not None:
                    between(hd)
                for ci in range(nch):
                    pst_, bpt = psg.get()
                    S.pe(lambda e, ci=ci: e.transpose(pst_[0:C, 0:128], kh[:, ci * C:(ci + 1) * C], ident[:]),
                         reads=[Bkh, Bc], writes=[bpt])
                    S.act(lambda e, ci=ci: e.activation(khtok[0:C, ci, :], pst_[0:C, 0:128], AF.Copy), reads=[bpt],
                          writes=[Bkhtok[ci]])
                psA, bpA = psg.get()
                for ci in range(nch):
                    S.pe(lambda e, ci=ci: e.matmul(psA[0:C, ci * C:(ci + 1) * C], lhsT=kt_[:, ci * C:(ci + 1) * C],
                                                   rhs=qt[:, ci * C:(ci + 1) * C], start=True, stop=True),
                         reads=[Bkt, Bqt], writes=[bpA])
                S.dve(lambda e: e.tensor_tensor(Abf[0:C, 0:nch, 0:C],
                                                psA[0:C, 0:nch * C].rearrange("p (c s) -> p c s", s=C),
                                                triu[0:C, 0:C].unsqueeze(1).to_broadcast([C, nch, C]), ALU.mult),
                      reads=[bpA, Bc], writes=[BAbf])
                pso, bpo = psS.get()
                for (c0, ncs, si) in segs:
                    Sap, BSt = S_in_fn(hd, si)
                    sbf, bsbf = Sbf.get()
                    S.dve(lambda e, Sap=Sap, sbf=sbf: e.tensor_copy(sbf[:], Sap), reads=[BSt], writes=[bsbf])
                    for ci in range(c0, c0 + ncs):
                        psd, bpd = psg.get()
                        S.pe(lambda e, ci=ci, hd=hd, psd=psd: e.matmul(psd[:, 0:128], lhsT=khtok[0:C, ci, :],
                                                                       rhs=vtok[0:C, ci, hd * 128:(hd + 1) * 128],
                                                                       start=True, stop=True),
                             reads=[Bkhtok[ci], Bvtok[ci]], writes=[bpd])
                        S.pe(lambda e, ci=ci, hd=hd: e.matmul(pso[:, ci * C:(ci + 1) * C],
                                                              lhsT=vtok[0:C, ci, hd * 128:(hd + 1) * 128],
                                                              rhs=Abf[0:C, ci, 0:C], start=True, stop=False),
                             reads=[Bvtok[ci], BAbf], writes=[bpo])
                        S.pe(lambda e, ci=ci, sbf=sbf: e.matmul(pso[:, ci * C:(ci + 1) * C], lhsT=sbf[:],
                                                                rhs=qh[:, ci * C:(ci + 1) * C], start=False, stop=True),
                             reads=[bsbf, Bqh], writes=[bpo])
                        S.dve(lambda e, ci=ci, Sap=Sap, psd=psd, e3=e3: e.scalar_tensor_tensor(
                            Sap, Sap, e3[:, (ci + 1) * C - 1:(ci + 1) * C], psd[:, 0:128], ALU.mult, ALU.add),
                              reads=[bpd, be3], writes=[BSt])
                        if ci < c0 + ncs - 1:
                            sbf, bsbf = Sbf.get()
                            S.dve(lambda e, Sap=Sap, sbf=sbf: e.tensor_copy(sbf[:], Sap), reads=[BSt],
                                  writes=[bsbf])
                    S_out_fn(hd, si, Sap, BSt)
                S.act(lambda e: e.activation(sqo[:, 0:N], pso[:, 0:N], AF.Square), reads=[bpo], writes=[Bsqo])
                psm, bpm = psg.get()
                S.pe(lambda e: e.matmul(psm[:, 0:N], lhsT=onesV[:], rhs=sqo[:, 0:N], start=True, stop=True),
                     reads=[Bsqo, Bc], writes=[bpm])
                rs, brs = ft.get()
                rsqrt_eps(rs[:, 0:N], psm[:, 0:N], [bpm], brs)
                on, bon = ft.get()
                S.dve(lambda e: e.scalar_tensor_tensor(on[:, 0:N], pso[:, 0:N], hng[:, l:l + 1], rs[:, 0:N], ALU.mult,
                                                       ALU.mult), reads=[bpo, brs, Bc], writes=[bon])
                psq, bpq = psg.get()
                mm_fm(psq, 128, N, wo_, bo, hd * 128, 8, hT, Bh, extra_w=[bpq])
                go, bgo = ft.get()
                S.act(lambda e: e.activation(go[:, 0:N], psq[:, 0:N], AF.Silu), reads=[bpq], writes=[bgo])
                S.dve(lambda e, hd=hd: e.tensor_tensor(brT[:, hd, 0:N], on[:, 0:N], go[:, 0:N], ALU.mult),
                      reads=[bon, bgo], writes=[Bbr[hd]])

        init_consts()

        import os as _os
        DBG = _os.environ.get("KDBG", "full")
        DBG2 = _os.environ.get("KDBG2", "")

        def prompt_tile(j):
            N = T
            tok0 = j * T
            load_x_tile(xp[tok0:tok0 + T, :], N)
            for l in range(L if DBG != "load" else 0):
                rmsnorm(g1, l, N)
                def segs(l_, c, psv, bpv, sg, bsg):
                    S.act(lambda e: e.activation(ubuf[:, c, 0:30], hist[:, l_, c, :], AF.Copy), reads=[Bhist[l_][c]],
                          writes=[Bub[c]])
                    S.dve(lambda e: e.tensor_tensor(ubuf[:, c, 30:30 + N], psv[:, 0:N], sg[:, 0:N], ALU.mult),
                          reads=[bpv, bsg], writes=[Bub[c]])
                    S.dve(lambda e: e.tensor_copy(hist[:, l_, c, :], ubuf[:, c, N:N + 30]), reads=[Bub[c]],
                          writes=[Bhist[l_][c]])
                    conv_taps(l_, c, 0, 0, N)
                if DBG == "norm":
                    continue
                conv_branch(l, N, segs)
                wK, bK, wV, bV = fox_qkv(l, N, tok0, None, None, None)
                def va_fn(i, n, stt, bst_):
                    S.act(lambda e: e.activation(Va[0:n, :, i, 0:64],
                                                 stt[0:n, 512:1024].rearrange("p (h d) -> p h d", d=64), AF.Copy),
                          reads=[bst_], writes=[BV])

                def kdst(i, n, stt, bst_):
                    S.dma(PQ, None, kp[l, tok0 + i * 128: tok0 + i * 128 + n, :], stt[0:n, 0:512],
                          reads=[bst_])

                def vdst(i, n, stt, bst_):
                    S.dma(PQ, None, vp[l, tok0 + i * 128: tok0 + i * 128 + n, :], stt[0:n, 512:1024],
                          reads=[bst_])

                def lfdst(i, n):
                    if i == NSUB - 1:
                        S.dma(PQ, lfsem, lfp[l, tok0:tok0 + N, :].rearrange("(s p) h -> p s h", p=128), lftok[:],
                              reads=[Blftok])
                fox_tok_outputs(l, N, wK, bK, wV, bV, kdst, vdst, lfdst, va_fn)
                conv_post(l, N)
                branch_out(l, "w_a_out", C_GA, N, True, False)
                if j == NT - 1:
                    for c in range(4):
                        S.dma(PQ, osem, cp[l][:, c * 128:(c + 1) * 128].rearrange("j p -> p j"), hist[:, l, c, :],
                              reads=[Bhist[l][c]])
                fox_scan_rows(l, N, 0, N, ccar[:, l:l + 1], [Bccar[l]])
                S.dve(lambda e: e.tensor_copy(ccar[:, l:l + 1], csp[:, N - 1:N]), reads=[Bf], writes=[Bccar[l]])
                fox_split_rows(N)
                S.dma(PQ, ksem_, kscr[l, :, :, tok0:tok0 + N], Ka[:], reads=BK, writes=[Bkscr[l][j]])
                S.dma(PQ, vsem_, vscr[l, :, :, j * NSUB:(j + 1) * NSUB, :], Va[:], reads=[BV], writes=[Bvscr[l][j]])
                def S_in(hd, si):
                    return Sst[:, l, hd, :], BS[l][hd]

                def S_out(hd, si, Sap, BSt):
                    if j == NT - 1:
                        S.dma(PQ, osem, hp[l, hd], Sap, reads=[BSt])
                hgrn_branch(l, N, 64, [(0, NCH, 0)], S_in, S_out,
                            between=lambda hd: fox_attend_prompt(l, j, [hd]))
                branch_out(l, "w_c_out", C_GC, N, False, False)
                fox_o_transpose(N)
                branch_out(l, "w_b_out", C_GB, N, False, True)
                if DBG == "hgrn":
                    continue
                ffn_and_wo(l, N)
            store_y_tile(yp[tok0:tok0 + T, :], N)


        def sample_tile():
            N = NSS * LS
            for i_ in range(2):
                kct = ksrc.tiles[i_][:].rearrange("p a (b k) -> p (a b) k", k=128)
                vct = vsrc.tiles[i_][:].rearrange("p a b w -> p (a b) w")
                S.pool(lambda e, kct=kct: e.memset(kct[64:68, :, :], 0.0), writes=[ksrc.bufs[i_]])
                S.pool(lambda e, kct=kct: e.memset(kct[64:66, :, :], 1.0), writes=[ksrc.bufs[i_]])
                S.pool(lambda e, vct=vct: e.memset(vct[:, :, 64:VW], 1.0), writes=[ksrc.bufs[i_]])
            load_x_tile(xs, N)
            ub4 = ubuf[:, :, 0:4 * 46].rearrange("p c (s w) -> p c s w", w=46)
            for l in range(L):
                rmsnorm(g1, l, N)
                for s_ in range(NSS):
                    stt, bst_ = stg.get()
                    S.dma(PQ, None, stt[0:30, 0:512], sconv[l, s_], writes=[bst_])
                    ps, bp = psg.get()
                    for c in range(4):
                        S.pe(lambda e, c=c: e.transpose(ps[:, c * 32:c * 32 + 30], stt[0:30, c * 128:(c + 1) * 128],
                                                        ident[0:30, 0:30]), reads=[bst_, Bc], writes=[bp])
                    S.act(lambda e, s_=s_: e.activation(ub4[:, :, s_, 0:30],
                                                        ps[:, 0:128].rearrange("p (c w) -> p c w", w=32)[:, :, 0:30],
                                                        AF.Copy), reads=[bp], writes=Bub)

                def segs(l_, c, psv, bpv, sg, bsg):
                    S.dve(lambda e: e.tensor_tensor(ub4[:, c, :, 30:46],
                                                    psv[:, 0:N].rearrange("p (s t) -> p s t", t=LS),
                                                    sg[:, 0:N].rearrange("p (s t) -> p s t", t=LS), ALU.mult),
                          reads=[bpv, bsg], writes=[Bub[c]])
                    y4 = yc[:, c, 0:N].rearrange("p (s t) -> p s t", t=LS)
                    S.dve(lambda e: e.tensor_scalar(y4, ub4[:, c, :, 0:LS], cw[:, l_, c, 0:1], cb[:, l_, c:c + 1],
                                                    ALU.mult, ALU.add), reads=[Bub[c], Bc], writes=[Byc[c]])
                    for j in range(1, 31):
                        S.dve(lambda e, j=j: e.scalar_tensor_tensor(y4, ub4[:, c, :, j:j + LS], cw[:, l_, c, j:j + 1],
                                                                    y4, ALU.mult, ALU.add),
                              reads=[Bub[c]], writes=[Byc[c]])
                conv_branch(l, N, segs)
                for s_ in range(NSS):
                    ps, bp = psg.get()
                    for c in range(4):
                        S.pe(lambda e, c=c, s_=s_: e.transpose(ps[0:30, c * 128:(c + 1) * 128], ub4[:, c, s_, 16:46],
                                                               ident[:]), reads=[Bub[c], Bc], writes=[bp])
                    stt, bst_ = stg.get()
                    S.act(lambda e: e.activation(stt[0:30, 0:512], ps[0:30, :], AF.Copy), reads=[bp], writes=[bst_])
                    S.dma(PQ, None, cs[l, s_], stt[0:30, 0:512], reads=[bst_])
                conv_post(l, N)
                branch_out(l, "w_a_out", C_GA, N, True, False)
                wK, bK, wV, bV = fox_qkv(l, N, 0, None, None, None)
                for s_ in range(NSS):
                    fox_scan_rows(l, N, s_ * LS, LS, 0.0, [])
                fox_split_rows(N)

                def va_fn(i, n, stt, bst_):
                    S.act(lambda e: e.activation(Va[0:n, :, i, 0:64],
                                                 stt[0:n, 512:1024].rearrange("p (h d) -> p h d", d=64), AF.Copy),
                          reads=[bst_], writes=[BV])

                def kdst(i, n, stt, bst_):
                    S.dma(PQ, None, ks[l].rearrange("s t c -> (s t) c"), stt[0:n, 0:512], reads=[bst_])

                def vdst(i, n, stt, bst_):
                    S.dma(PQ, None, vs[l].rearrange("s t c -> (s t) c"), stt[0:n, 512:1024], reads=[bst_])

                def lfdst(i, n):
                    S.dma(PQ, lfsem, lfs[l].rearrange("s t h -> (s t) h"), lftok[0:n, 0, :], reads=[Blftok])
                fox_tok_outputs(l, N, wK, bK, wV, bV, kdst, vdst, lfdst, va_fn)
                fox_attend_sample(l)
                for fc in range(4):
                    ps, bp = psg.get()
                    for s_ in range(NSS):
                        S.pe(lambda e, s_=s_, fc=fc: e.transpose(ps[:, s_ * LS:(s_ + 1) * LS],
                                                                 otok[0:LS, s_, fc * 128:(fc + 1) * 128],
                                                                 ident[0:LS, 0:LS]),
                             reads=[Botok[s_], Bc], writes=[bp])
                    S.act(lambda e, fc=fc: e.activation(brT[:, fc, 0:N], ps[:, 0:N], AF.Copy), reads=[bp],
                          writes=[Bbr[fc]])
                branch_out(l, "w_b_out", C_GB, N, False, False)
                def S_in(hd, si):
                    t_, b_ = Ssm.get()
                    S.dma(PQ, None, t_[:], shg[l, si, hd], writes=[b_])
                    return t_[:], b_

                def S_out(hd, si, Sap, BSt):
                    S.dma(PQ, None, hs[l, si, hd], Sap, reads=[BSt])
                hgrn_branch(l, N, LS, [(s_, 1, s_) for s_ in range(NSS)], S_in, S_out)
                branch_out(l, "w_c_out", C_GC, N, False, True)
                ffn_and_wo(l, N)
            store_y_tile(ys, N)

        def fox_attend_sample(l):
            N = NSS * LS
            psn, bpn = psS.get()
            for hh in range(8):
                S.pe(lambda e, hh=hh: e.matmul(psn[0:N, hh * N:(hh + 1) * N], lhsT=Ka[:, hh, 0:N], rhs=Qa[:, hh, 0:N],
                                               start=True, stop=True), reads=[BK[hh], BQ[hh]], writes=[bpn])
            Pn, bPn = sqo, Bsqo
            S.act(lambda e: e.activation(Pn[0:N, :], psn[0:N, :], AF.Exp), reads=[bpn], writes=[bPn])
            S.dve(lambda e: e.tensor_tensor(Pn[0:N, :].rearrange("p (h q) -> p h q", q=N),
                                            Pn[0:N, :].rearrange("p (h q) -> p h q", q=N),
                                            bmask[:, :].unsqueeze(1).to_broadcast([N, 8, N]), ALU.mult),
                  reads=[Bc], writes=[bPn])
            for s_ in range(NSS):
                S.dma(PQ, None, lfc[:], clf[l, s_].rearrange("(kt p) h -> p kt h", p=128), writes=[Brk])
                lfc2 = lfc[:].rearrange("p k h -> p (k h)")
                ps1, bp1 = psg.get()
                S.pe(lambda e: e.matmul(ps1[:, 0:128], lhsT=slm[:], rhs=lfc2, start=True, stop=True),
                     reads=[Brk, Bc], writes=[bp1])
                ps2, bp2 = psg.get()
                S.pe(lambda e: e.matmul(ps2[:, 0:128], lhsT=ones[:, 0:128], rhs=lfc2, start=True, stop=True),
                     reads=[Brk, Bc], writes=[bp2])
                S.dve(lambda e: e.tensor_copy(rkt[:].rearrange("p k h -> p (k h)"), ps2[:, 0:128]), reads=[bp2],
                      writes=[Brk])
                S.dve(lambda e: e.tensor_copy(rk[:].rearrange("p k h -> p (k h)"), ps1[:, 0:128]), reads=[bp1],
                      writes=[Brk])
                S.dve(lambda e: e.tensor_copy(lfc[:, 15, :], rkt[:, 15, :]), writes=[Brk])
                S.dve(lambda e: e.memset(rkt[:, 15, :], 0.0), writes=[Brk])
                for kt in range(14, -1, -1):
                    S.dve(lambda e, kt=kt: e.tensor_copy(lfc[:, kt, :], rkt[:, kt, :]), writes=[Brk])
                    S.dve(lambda e, kt=kt: e.tensor_tensor(rkt[:, kt, :], rkt[:, kt + 1, :], lfc[:, kt + 1, :], ALU.add),
                          writes=[Brk])
                S.dve(lambda e: e.tensor_tensor(rk[:], rk[:], rkt[:], ALU.add), writes=[Brk])
                first = [True, True]
                for kt in range(NPAST // 128):
                    stt, bst_ = stg.get()
                    S.dma_group(PQ, None if False else S.auto_sem((), [bst_]),
                                [(stt[:, 0:512], ck[l, s_, kt * 128:(kt + 1) * 128, :]),
                                 (stt[:, 512:1024], cv[l, s_, kt * 128:(kt + 1) * 128, :])], writes=[bst_])
                    kct_t, bkc = ksrc.get()
                    vct_t, _u = vsrc.get()
                    kct = kct_t[:].rearrange("p a (b k) -> p (a b) k", k=128)
                    vct = vct_t[:].rearrange("p a b w -> p (a b) w")
                    for half in range(2):
                        pst_, bpt = psg.get()
                        for hq in range(4):
                            hh = half * 4 + hq
                            S.pe(lambda e, hh=hh, hq=hq, pst_=pst_, stt=stt: e.transpose(
                                pst_[0:64, hq * 128:(hq + 1) * 128], stt[:, hh * 64:(hh + 1) * 64], ident[:]),
                                 reads=[bst_, Bc], writes=[bpt])
                        S.act(lambda e, half=half, pst_=pst_, kct=kct: e.activation(
                            kct[0:64, half * 4:half * 4 + 4, :], pst_[0:64, :].rearrange("p (h k) -> p h k", k=128),
                            AF.Copy), reads=[bpt], writes=[bkc])
                    S.dve(lambda e, vct=vct, stt=stt: e.tensor_copy(
                        vct[:, :, 0:64], stt[:, 512:1024].rearrange("p (h d) -> p h d", d=64)),
                          reads=[bst_], writes=[bkc])
                    pss, bps = psS.get()
                    for hh in range(8):
                        S.pe(lambda e, hh=hh, pss=pss, kct=kct: e.matmul(
                            pss[:, hh * LS:(hh + 1) * LS], lhsT=kct[:, hh, :], rhs=Qa[:, hh, s_ * LS:(s_ + 1) * LS],
                            start=True, stop=True), reads=[bkc, BQ[hh]], writes=[bps])
                    Pt, bPt = Prot.get()
                    Pf, bPf = ft.get()
                    S.dve(lambda e, kt=kt, pss=pss, Pf=Pf: e.tensor_tensor(
                        Pf[:, 0:128].rearrange("p (h q) -> p h q", q=LS),
                        pss[:, 0:128].rearrange("p (h q) -> p h q", q=LS),
                        rk[:, kt, :].unsqueeze(2).to_broadcast([128, 8, LS]), ALU.add),
                          reads=[bps, Brk], writes=[bPf])
                    S.act(lambda e, Pt=Pt, Pf=Pf: e.activation(Pt[:, 0:128], Pf[:, 0:128], AF.Exp), reads=[bPf],
                          writes=[bPt])
                    for hh in range(8):
                        b_ = hh // 4
                        S.pe(lambda e, hh=hh, b_=b_, Pt=Pt, vct=vct, f=first[b_]: e.matmul(
                            psO[b_][0:LS, hh % 4, 0:65], lhsT=Pt[:, hh * LS:(hh + 1) * LS], rhs=vct[:, hh, 0:65],
                            start=f, stop=False, skip_group_check=True), reads=[bPt, bkc], writes=[BpsO[b_]])
                        first[b_] = False
                for hh in range(8):
                    b_ = hh // 4
                    S.pe(lambda e, hh=hh, b_=b_: e.matmul(
                        psO[b_][0:LS, hh % 4, 0:65], lhsT=Pn[0:N, hh * N + s_ * LS: hh * N + (s_ + 1) * LS],
                        rhs=Va[0:N, hh, 0, 0:65], start=False, stop=True, skip_group_check=True),
                         reads=[bPn, BV], writes=[BpsO[b_]])
                for b_ in range(2):
                    r, br = rc.get()
                    S.dve(lambda e, b_=b_, r=r: e.reciprocal(r[0:LS, :], psO[b_][0:LS, :, 64]), reads=[BpsO[b_]],
                          writes=[br])
                    for hq in range(4):
                        hh = b_ * 4 + hq
                        S.dve(lambda e, b_=b_, hq=hq, hh=hh, r=r: e.tensor_scalar(
                            otok[0:LS, s_, hh * 64:(hh + 1) * 64], psO[b_][0:LS, hq, 0:64], r[0:LS, hq:hq + 1], None,
                            ALU.mult), reads=[BpsO[b_], br], writes=[Botok[s_]])

        if PROMPT and DBG != "init":
            for j in range(NT):
                prompt_tile(j)

        if SAMPLE:
            sample_tile()

        with nc.allow_non_contiguous_dma(reason="small strided param/state transfers"):
            S.emit()
    return nc, S


_CACHE = {}


def kernel(**inputs):
    inp = {k: np.ascontiguousarray(np.asarray(v, dtype=np.float32)) for k, v in inputs.items()}
    L = inp["w_in"].shape[0]
    B, SEQ, _ = inp["x_prompt"].shape
    key = (SEQ, L)
    if key not in _CACHE:
        _CACHE[key] = build_program(SEQ=SEQ, L=L)[0]
    nc = _CACHE[key]
    ncores = 8
    in_maps = []
    for c in range(ncores):
        m = {
            "xp": inp["x_prompt"][c % B],
            "xs": inp["x_sample"][4 * c:4 * c + 4].reshape(NSS * LS, D),
            "ck": inp["cache_fox_k"][:, 4 * c:4 * c + 4].reshape(L, NSS, NPAST, 512),
            "cv": inp["cache_fox_v"][:, 4 * c:4 * c + 4].reshape(L, NSS, NPAST, 512),
            "clf": inp["cache_fox_logf"][:, 4 * c:4 * c + 4],
            "sconv": inp["state_conv"][:, 4 * c:4 * c + 4],
            "shg": inp["state_hgrn"][:, 4 * c:4 * c + 4],
        }
        for n in WNAMES:
            m[n] = inp[n] if n != "final_g" else inp[n].reshape(1, D)
        in_maps.append({k: np.ascontiguousarray(v) for k, v in m.items()})
    res = run_bass_kernel_spmd(nc, in_maps, core_ids=list(range(ncores)))
    R = res.results
    y_prompt = np.stack([R[b]["yp"] for b in range(B)])
    y_sample = np.concatenate([R[c]["ys"].reshape(NSS, LS, D) for c in range(ncores)], axis=0)
    kpo = np.stack([R[b]["kp"] for b in range(B)], axis=1).reshape(L, B, SEQ, 8, 64)
    vpo = np.stack([R[b]["vp"] for b in range(B)], axis=1).reshape(L, B, SEQ, 8, 64)
    lfpo = np.stack([R[b]["lfp"] for b in range(B)], axis=1)
    cpo = np.stack([R[b]["cp"] for b in range(B)], axis=1)
    hpo = np.stack([R[b]["hp"] for b in range(B)], axis=1)
    kso = np.concatenate([R[c]["ks"] for c in range(ncores)], axis=1).reshape(L, 4 * ncores, LS, 8, 64)
    vso = np.concatenate([R[c]["vs"] for c in range(ncores)], axis=1).reshape(L, 4 * ncores, LS, 8, 64)
    lfso = np.concatenate([R[c]["lfs"] for c in range(ncores)], axis=1)
    cso = np.concatenate([R[c]["cs"] for c in range(ncores)], axis=1)
    hso = np.concatenate([R[c]["hs"] for c in range(ncores)], axis=1)
    return (y_prompt, y_sample, kpo, vpo, lfpo, cpo, hpo, kso, vso, lfso, cso, hso)
```

```python
import contextlib
import numpy as np
import concourse.bass as bass
import concourse.mybir as mybir
from concourse.bass_utils import run_bass_kernel_spmd

F32 = mybir.dt.float32
BF16 = mybir.dt.bfloat16
ALU = mybir.AluOpType
AF = mybir.ActivationFunctionType
AX = mybir.AxisListType

ENGS = ("pe", "act", "dve", "pool", "sp")
SAME_ENGINE_SYNC = {"pool", "dve", "act"}
EPS = 1e-6


class Buf:
    __slots__ = ("w", "r", "name", "dw", "dr")

    def __init__(self, name=""):
        self.w = None
        self.r = {}
        self.name = name
        self.dw = None
        self.dr = None


class DmaSem:
    def __init__(self, handle):
        self.h = handle
        self.n = 0


class _Rec:
    def __init__(self):
        self.call = None

    def __getattr__(self, name):
        def f(*a, **k):
            self.call = (name, a, k)
            return self
        return f


class Sched:
    def __init__(self, nc, stack):
        self.nc = nc
        self.stack = stack
        self.ops = {e: [] for e in ENGS}
        self.count = {e: 0 for e in ENGS}
        self.seen = {e: {} for e in ENGS}
        self.esem = {e: stack.enter_context(nc.semaphore("es_" + e)) for e in ENGS}
        self.dsems = []
        self.nwaits = 0

    def dma_sem(self, name):
        d = DmaSem(self.stack.enter_context(self.nc.semaphore(name)))
        self.dsems.append(d)
        return d

    def op(self, eng, fn, reads=(), writes=(), dsem=None, raw=()):
        waits = {}
        seen = self.seen[eng]

        def need(k, v):
            if dsem is None and k == eng and eng not in SAME_ENGINE_SYNC:
                return
            if seen.get(k, 0) >= v:
                return
            if waits.get(k, 0) < v:
                waits[k] = v

        for b in reads:
            if b.w is not None:
                need(*b.w)
        for b in raw:
            if b.w is not None:
                need(*b.w)
        for b in writes:
            if b.w is not None:
                need(*b.w)
            for k, v in b.r.items():
                need(k, v)
        for k, v in waits.items():
            seen[k] = v
        if dsem is None:
            self.count[eng] += 1
            ev = (eng, self.count[eng])
        else:
            dsem.n += 16
            ev = (dsem, dsem.n)
        for b in reads:
            if b.r.get(ev[0], 0) < ev[1]:
                b.r[ev[0]] = ev[1]
        for b in writes:
            b.w = ev
            b.r = {}
        self.nwaits += len(waits)
        rec = _Rec()
        fn(rec)
        self.ops[eng].append((list(waits.items()), rec.call, ev))

    def pe(self, fn, reads=(), writes=()):
        self.op("pe", fn, reads, writes)

    def act(self, fn, reads=(), writes=()):
        self.op("act", fn, reads, writes)

    def dve(self, fn, reads=(), writes=()):
        self.op("dve", fn, reads, writes)

    def pool(self, fn, reads=(), writes=()):
        self.op("pool", fn, reads, writes)

    def auto_sem(self, reads, writes):
        if writes:
            b = writes[0]
            if b.dw is None:
                b.dw = self.dma_sem("dw%d" % len(self.dsems))
            return b.dw
        b = reads[0]
        if b.dr is None:
            b.dr = self.dma_sem("dr%d" % len(self.dsems))
        return b.dr

    def dma(self, q, dsem, out, in_, reads=(), writes=(), raw=(), **kw):
        if dsem is None:
            dsem = self.auto_sem(reads, writes)
        self.op(q, lambda e: e.dma_start(out=out, in_=in_, **kw), reads, writes, dsem=dsem, raw=raw)

    def dma_group(self, q, dsem, pairs, reads=(), writes=(), raw=(), **kw):
        for i, (out, in_) in enumerate(pairs):
            if i == 0:
                self.op(q, lambda e, out=out, in_=in_: e.dma_start(out=out, in_=in_, **kw), reads, writes, dsem=dsem,
                        raw=raw)
            else:
                self.op(q, lambda e, out=out, in_=in_: e.dma_start(out=out, in_=in_, **kw), (), (), dsem=dsem)
        ev = (dsem, dsem.n)
        for b in reads:
            b.r[dsem] = dsem.n
        for b in writes:
            b.w = ev
            b.r = {}

    def emit(self):
        nc = self.nc
        fin = []
        for e in ENGS:
            if self.count[e]:
                fin.append((e, self.count[e]))
        for d in self.dsems:
            if d.n:
                fin.append((d, d.n))

        def semh(k):
            return self.esem[k] if isinstance(k, str) else k.h

        def run(engname, engobj):
            for waits, fn, ev in self.ops[engname]:
                for k, v in waits:
                    engobj.wait_ge(semh(k), v)
                ins = getattr(engobj, fn[0])(*fn[1], **fn[2])
                if isinstance(ev[0], str):
                    ins.then_inc(self.esem[ev[0]], 1)
                else:
                    ins.then_inc(ev[0].h, 16)
            if engname == "sp":
                for k, v in fin:
                    engobj.wait_ge(semh(k), v)

        with nc.Block() as block:
            @block.tensor
            def _(e):
                run("pe", e)

            @block.scalar
            def _(e):
                run("act", e)

            @block.vector
            def _(e):
                run("dve", e)

            @block.gpsimd
            def _(e):
                run("pool", e)

            @block.sync
            def _(e):
                run("sp", e)


class Rot:
    def __init__(self, tiles):
        self.tiles = tiles
        self.bufs = [Buf() for _ in tiles]
        self.i = 0

    def get(self):
        k = self.i % len(self.tiles)
        self.i += 1
        return self.tiles[k], self.bufs[k]


D = 1024
INW = 7688
NPAST = 2048
NSS = 4
LS = 16
VW = 68
C_AV, C_AG, C_Q, C_K, C_V, C_F = 0, 512, 1024, 1536, 2048, 2560
C_HQ, C_HF, C_HI, C_HO = 2568, 3080, 3592, 4104
C_GA, C_GB, C_GC = 4616, 5640, 6664

WNAMES = ["norm1_g", "w_in", "conv_w", "conv_b", "conv_ln_g", "conv_ln_b", "w_a_out", "fox_bf", "w_b_out",
          "hgrn_lb_param", "hgrn_norm_g", "w_c_out", "w_o", "norm2_g", "w_up", "w_down", "final_g"]


def build_program(SEQ=8192, L=4, T=512, SAMPLE=True, PROMPT=True):
    nc = bass.Bass("TRN2", target_bir_lowering=False)
    NT = SEQ // T
    NSUB = T // 128
    NCH = T // 64

    def din(name, shape):
        return nc.dram_tensor(name, shape, F32, kind="ExternalInput").ap()

    def dout(name, shape):
        return nc.dram_tensor(name, shape, F32, kind="ExternalOutput").ap()

    xp = din("xp", [SEQ, D])
    xs = din("xs", [NSS * LS, D])
    ck = din("ck", [L, NSS, NPAST, 512])
    cv = din("cv", [L, NSS, NPAST, 512])
    clf = din("clf", [L, NSS, NPAST, 8])
    sconv = din("sconv", [L, NSS, 30, 512])
    shg = din("shg", [L, NSS, 4, 128, 128])
    Wd = {
        "norm1_g": din("norm1_g", [L, D]), "w_in": din("w_in", [L, D, INW]), "conv_w": din("conv_w", [L, 31, 512]),
        "conv_b": din("conv_b", [L, 512]), "conv_ln_g": din("conv_ln_g", [L, 512]),
        "conv_ln_b": din("conv_ln_b", [L, 512]), "w_a_out": din("w_a_out", [L, 512, D]),
        "fox_bf": din("fox_bf", [L, 8]), "w_b_out": din("w_b_out", [L, 512, D]),
        "hgrn_lb_param": din("hgrn_lb_param", [L, 512]), "hgrn_norm_g": din("hgrn_norm_g", [L, 128]),
        "w_c_out": din("w_c_out", [L, 512, D]), "w_o": din("w_o", [L, D, D]), "norm2_g": din("norm2_g", [L, D]),
        "w_up": din("w_up", [L, D, 4096]), "w_down": din("w_down", [L, 4096, D]), "final_g": din("final_g", [1, D]),
    }
    yp = dout("yp", [SEQ, D])
    ys = dout("ys", [NSS * LS, D])
    kp = dout("kp", [L, SEQ, 512])
    vp = dout("vp", [L, SEQ, 512])
    lfp = dout("lfp", [L, SEQ, 8])
    cp = dout("cp", [L, 30, 512])
    hp = dout("hp", [L, 4, 128, 128])
    ks = dout("ks", [L, NSS, LS, 512])
    vs = dout("vs", [L, NSS, LS, 512])
    lfs = dout("lfs", [L, NSS, LS, 8])
    cs = dout("cs", [L, NSS, 30, 512])
    hs = dout("hs", [L, NSS, 4, 128, 128])
    kscr = nc.dram_tensor("kscr", [L, 68, 8, SEQ], BF16, kind="Internal").ap()
    vscr = nc.dram_tensor("vscr", [L, 128, 8, SEQ // 128, VW], BF16, kind="Internal").ap()

    with contextlib.ExitStack() as st:
        S = Sched(nc, st)
        _n = [0]

        def sb(shape, dt, name=None):
            _n[0] += 1
            return st.enter_context(nc.sbuf_tensor(name or ("t%d" % _n[0]), shape, dt))

        def pst(shape, dt, name=None):
            _n[0] += 1
            return st.enter_context(nc.psum_tensor(name or ("p%d" % _n[0]), shape, dt))

        xT = sb([128, 8, T], F32, "xT")
        Bx = [Buf() for _ in range(8)]
        hT = sb([128, 8, T], BF16, "hT")
        Bh = [Buf() for _ in range(8)]
        sq = sb([128, 8, T], BF16, "sq")
        Bsq = [Buf() for _ in range(8)]
        mT = sb([128, 8, T], F32, "mT")
        Bm = [Buf() for _ in range(8)]
        brT = sb([128, 4, T], BF16, "brT")
        Bbr = [Buf() for _ in range(4)]
        stg = Rot([sb([128, 1024], F32, "stage%d" % i) for i in range(2)])
        NW = 4
        WSZ = 8 * 520
        wrot = Rot([sb([128, WSZ], BF16, "w%d" % i) for i in range(NW)])
        wsem = [S.dma_sem("wsem%d" % i) for i in range(NW)]
        ft = Rot([sb([128, T], F32, "ft%d" % i) for i in range(8)])
        psg = Rot([pst([128, 512], F32, "psg%d" % i) for i in range(4)])
        psS = Rot([pst([128, 512], F32, "psS%d" % i) for i in range(2)])
        psO = [pst([128, 4, 128], F32, "psO%d" % i) for i in range(2)]
        BpsO = [Buf(), Buf()]
        ident = sb([128, 128], F32, "ident")
        triu = sb([128, 128], F32, "triu")
        triub = sb([128, 128], BF16, "triub")
        onesD = sb([128, 128], BF16, "onesD")
        onesC = sb([128, 128], BF16, "onesC")
        onesV = sb([128, 128], BF16, "onesV")
        ones = sb([128, T], F32, "ones")
        g1 = sb([128, L, 8], F32, "g1")
        g2 = sb([128, L, 8], F32, "g2")
        gf = sb([128, 8], F32, "gf")
        cw = sb([128, L, 4, 31], F32, "cw")
        cb = sb([128, L, 4], F32, "cb")
        lng = sb([128, L, 4], F32, "lng")
        lnb = sb([128, L, 4], F32, "lnb")
        nbf = sb([8, L], F32, "nbf")
        bfb = sb([128, L, 8], F32, "bfb")
        lbp = sb([128, 4, L], F32, "lbp")
        lb = sb([128, 4, L], F32, "lb")
        oml = sb([128, 4, L], F32, "oml")
        noml = sb([128, 4, L], F32, "noml")
        lbt = sb([128, 4, L], F32, "lbt")
        lbm = sb([128, 4], F32, "lbm")
        hng = sb([128, L], F32, "hng")
        epsb = sb([128, 1], F32, "epsb")
        Bc = Buf()
        csem = S.dma_sem("csem")
        hist = sb([128, L, 4, 30], F32, "hist")
        Bhist = [[Buf() for _ in range(4)] for _ in range(L)]
        Sst = sb([128, L, 4, 128], F32, "Sst")
        BS = [[Buf() for _ in range(4)] for _ in range(L)]
        Sbf = Rot([sb([128, 128], BF16, "Sbf%d" % i) for i in range(2)])
        ccar = sb([8, L], F32, "ccar")
        Bccar = [Buf() for _ in range(L)]
        ubuf = sb([128, 4, 30 + T], F32, "ubuf")
        Bub = [Buf() for _ in range(4)]
        yc = sb([128, 4, T], F32, "yc")
        Byc = [Buf() for _ in range(4)]
        ybf = sq[:, 0:4, :]
        ysq = sq[:, 4:8, :]
        Bybf = Bsq[0:4]
        Bysq = Bsq[4:8]
        Qa = sb([68, 8, T], BF16, "Qa")
        Ka = sb([68, 8, T], BF16, "Ka")
        BQ = [Buf() for _ in range(8)]
        BK = [Buf() for _ in range(8)]
        Va = sb([128, 8, NSUB, VW], BF16, "Va")
        BV = Buf()
        ksrc = Rot([sb([68, 2, 512], BF16, "ksrc%d" % i) for i in range(2)])
        vsrc = Rot([sb([128, 2, 4, VW], BF16, "vsrc%d" % i) for i in range(2)])
        kvsem = [S.dma_sem("kvsem%d" % i) for i in range(2)]
        Prot = Rot([sb([128, 512], BF16, "P%d" % i) for i in range(3)])
        otok = yc
        Botok = Byc
        rc = Rot([sb([128, 4], F32, "rc%d" % i) for i in range(2)])
        fT = sb([8, T], F32, "fT")
        csp = sb([8, T], F32, "csp")
        chi = sb([8, T], BF16, "chi")
        clo = sb([8, T], BF16, "clo")
        nhi = sb([8, T], BF16, "nhi")
        nlo = sb([8, T], BF16, "nlo")
        Bf = Buf()
        lftok = sb([128, NSUB, 8], F32, "lftok")
        Blftok = Buf()
        scrsem = S.dma_sem("scrsem")
        Bkscr = [[Buf() for _ in range(NT + 1)] for _ in range(L)]
        Bvscr = [[Buf() for _ in range(NT + 1)] for _ in range(L)]
        osem = S.dma_sem("osem")
        stsem = S.dma_sem("stsem")
        lfsem = S.dma_sem("lfsem")
        ksem_ = S.dma_sem("kscrsem")
        vsem_ = S.dma_sem("vscrsem")
        rowsem = S.dma_sem("rowsem")
        xsem = S.dma_sem("xsem")
        qs = sb([128, T], F32, "qs")
        kk = sb([128, T], F32, "kk")
        bg = sb([128, T], F32, "bg")
        Bqs, Bkk, Bbg = Buf(), Buf(), Buf()
        qt = sb([128, T], BF16, "qt")
        kt_ = sb([128, T], BF16, "kt_")
        qh = sb([128, T], BF16, "qh")
        kh = sb([128, T], F32, "kh")
        Bqt, Bkt, Bqh, Bkh = Buf(), Buf(), Buf(), Buf()
        vtok = sb([64, NCH, 512], BF16, "vtok")
        Bvtok = [Buf() for _ in range(NCH)]
        khtok = sb([64, NCH, 128], BF16, "khtok")
        Bkhtok = [Buf() for _ in range(NCH)]
        Abf = sb([64, NCH, 64], BF16, "Abf")
        BAbf = Buf()
        sqo = sb([128, T], BF16, "sqo")
        Bsqo = Buf()

        slm = sb([128, 128], F32, "slm")
        bmask = sb([64, 64], BF16, "bmask")
        lfc = sb([128, 16, 8], F32, "lfc")
        rk = sb([128, 16, 8], F32, "rk")
        rkt = sb([128, 16, 8], F32, "rkt")
        Brk = Buf()
        Ssm = Rot([sb([128, 128], F32, "Ssm%d" % i) for i in range(2)])

        PQ = "sp"
        WQ = "pool"

        def init_consts():
            S.pool(lambda e: e.memset(ident[:], 1.0), writes=[Bc])
            S.pool(lambda e: e.affine_select(ident[:], ident[:], [[-1, 128]], ALU.is_equal, 0.0, base=0,
                                             channel_multiplier=1), writes=[Bc])
            S.pool(lambda e: e.memset(triu[:], 1.0), writes=[Bc])
            S.pool(lambda e: e.affine_select(triu[:], triu[:], [[1, 128]], ALU.is_ge, 0.0, base=0,
                                             channel_multiplier=-1), writes=[Bc])
            S.pool(lambda e: e.tensor_copy(triub[:], triu[:]), writes=[Bc])
            S.pool(lambda e: e.memset(slm[:], 1.0), writes=[Bc])
            S.pool(lambda e: e.affine_select(slm[:], slm[:], [[-1, 128]], ALU.is_gt, 0.0, base=0,
                                             channel_multiplier=1), writes=[Bc])
            S.pool(lambda e: e.tensor_copy(bmask[:], triu[0:64, 0:64]), writes=[Bc])
            for b_ in range(1, 4):
                S.pool(lambda e, b_=b_: e.memset(bmask[0:16 * b_, 16 * b_:16 * b_ + 16], 0.0), writes=[Bc])
            S.pool(lambda e: e.memset(onesD[:], 1.0 / 1024), writes=[Bc])
            S.pool(lambda e: e.memset(onesC[:], 1.0 / 512), writes=[Bc])
            S.pool(lambda e: e.memset(onesV[:], 1.0 / 128), writes=[Bc])
            S.pool(lambda e: e.memset(ones[:], 1.0), writes=[Bc])
            S.pool(lambda e: e.memset(epsb[:], EPS), writes=[Bc])
            S.pool(lambda e: e.memset(Qa[64:68, :, :], 1.0), writes=BQ)
            S.pool(lambda e: e.memset(Ka[64:68, :, :], 1.0), writes=BK)
            S.pool(lambda e: e.memset(Va[:, :, :, 64:VW], 1.0), writes=[BV])
            S.pool(lambda e: e.memset(hist[:], 0.0), writes=[b for bl in Bhist for b in bl])
            S.pool(lambda e: e.memset(Sst[:], 0.0), writes=[b for bl in BS for b in bl])
            S.pool(lambda e: e.memset(ccar[:], 0.0), writes=Bccar)
            W = Wd
            S.dma(PQ, csem, g1[:], W["norm1_g"].rearrange("l (c p) -> p l c", p=128), writes=[Bc])
            S.dma(PQ, csem, g2[:], W["norm2_g"].rearrange("l (c p) -> p l c", p=128), writes=[Bc])
            S.dma(PQ, csem, gf[:], W["final_g"][0].rearrange("(c p) -> p c", p=128), writes=[Bc])
            for l in range(L):
                for c in range(4):
                    S.dma(PQ, csem, cw[:, l, c, :], W["conv_w"][l][:, c * 128:(c + 1) * 128].rearrange("j p -> p j"),
                          writes=[Bc])
            S.dma(PQ, csem, cb[:], W["conv_b"].rearrange("l (c p) -> p l c", p=128), writes=[Bc])
            S.dma(PQ, csem, lng[:], W["conv_ln_g"].rearrange("l (c p) -> p l c", p=128), writes=[Bc])
            S.dma(PQ, csem, lnb[:], W["conv_ln_b"].rearrange("l (c p) -> p l c", p=128), writes=[Bc])
            S.dma(PQ, csem, nbf[:], W["fox_bf"].rearrange("l h -> h l"), writes=[Bc])
            for l in range(L):
                S.dma(PQ, csem, bfb[:, l:l + 1, :], W["fox_bf"][l:l + 1, :].partition_broadcast(128), writes=[Bc])
            for hd_ in range(4):
                S.dma(PQ, csem, lbp[:, hd_, :], W["hgrn_lb_param"][:, hd_ * 128:(hd_ + 1) * 128].rearrange("l p -> p l"),
                      writes=[Bc])
            S.dma(PQ, csem, hng[:], W["hgrn_norm_g"].rearrange("l v -> v l"), writes=[Bc])
            S.dve(lambda e: e.tensor_scalar(nbf[:], nbf[:], -1.0, None, ALU.mult), reads=[Bc], writes=[Bc])
            S.dve(lambda e: e.tensor_reduce(lbm[:], lbp[:], AX.X, ALU.max), reads=[Bc], writes=[Bc])
            S.dve(lambda e: e.tensor_tensor(lbt[:], lbp[:], lbm[:].unsqueeze(2).to_broadcast([128, 4, L]),
                                            ALU.subtract), writes=[Bc])
            S.act(lambda e: e.activation(lbt[:], lbt[:], AF.Exp), reads=[Bc], writes=[Bc])
            S.dve(lambda e: e.tensor_reduce(lbm[:], lbt[:], AX.X, ALU.add), reads=[Bc], writes=[Bc])
            S.dve(lambda e: e.reciprocal(lbm[:], lbm[:]), writes=[Bc])
            S.dve(lambda e: e.tensor_tensor(lbt[:], lbt[:], lbm[:].unsqueeze(2).to_broadcast([128, 4, L]),
                                            ALU.mult), writes=[Bc])
            S.dve(lambda e: e.memset(lb[:], 0.0), writes=[Bc])
            for l in range(1, L):
                S.dve(lambda e, l=l: e.tensor_tensor(lb[:, :, l:l + 1], lb[:, :, l - 1:l], lbt[:, :, l:l + 1],
                                                     ALU.add), writes=[Bc])
            S.dve(lambda e: e.tensor_scalar(oml[:], lb[:], -1.0, 1.0, ALU.mult, ALU.add), writes=[Bc])
            S.dve(lambda e: e.tensor_scalar(noml[:], oml[:], -1.0, None, ALU.mult), writes=[Bc])

        def rsqrt_eps(out_ap, in_ap, reads, wbuf):
            S.act(lambda e: e.activation(out_ap, in_ap, AF.Sqrt, bias=epsb[:, 0:1]), reads=list(reads) + [Bc],
                  writes=[wbuf])
            S.dve(lambda e: e.reciprocal(out_ap, out_ap), writes=[wbuf])

        def wload(src2d, kc, cols):
            best, bi_ = None, None
            for i_ in range(NW):
                b_ = wrot.bufs[i_]
                if b_.w is not None and not b_.r:
                    continue
                key = b_.r.get("pe", 0)
                if best is None or key < best:
                    best, bi_ = key, i_
            assert bi_ is not None, "no free weight slot"
            t, b = wrot.tiles[bi_], wrot.bufs[bi_]
            sem = wsem[bi_]
            view = t[:, 0:kc * cols].rearrange("p (k c) -> p k c", c=cols)
            S.dma(WQ, sem, view, src2d.rearrange("(k p) c -> p k c", p=128), writes=[b])
            return view, b

        def mm_fm(ps, M, N, wt, wb, col0, nk, act, Bact, extra_w=()):
            for k in range(nk):
                S.pe(lambda e, k=k: e.matmul(ps[0:M, 0:N], lhsT=wt[:, k, col0:col0 + M], rhs=act[:, k, 0:N],
                                             start=(k == 0), stop=(k == nk - 1)),
                     reads=[wb, Bact[k]], writes=list(extra_w))

        def rmsnorm(gt, l, N):
            for c in range(8):
                S.act(lambda e, c=c: e.activation(sq[:, c, 0:N], xT[:, c, 0:N], AF.Square),
                      reads=[Bx[c]], writes=[Bsq[c]])
            ps, bp = psg.get()
            for c in range(8):
                S.pe(lambda e, c=c: e.matmul(ps[:, 0:N], lhsT=onesD[:], rhs=sq[:, c, 0:N], start=(c == 0),
                                             stop=(c == 7)), reads=[Bc, Bsq[c]], writes=[bp])
            rs, brs = ft.get()
            rsqrt_eps(rs[:, 0:N], ps[:, 0:N], [bp], brs)
            for c in range(8):
                gap = gt[:, l, c:c + 1] if l is not None else gt[:, c:c + 1]
                S.dve(lambda e, c=c, gap=gap: e.scalar_tensor_tensor(hT[:, c, 0:N], xT[:, c, 0:N], gap, rs[:, 0:N],
                                                                     ALU.mult, ALU.mult),
                      reads=[Bx[c], brs, Bc], writes=[Bh[c]])

        def branch_out(l, wname, gcol, N, first, last):
            wo_, bwo = wload(Wd[wname][l], 4, 1024)
            wg = [wload(Wd["w_in"][l][:, gcol + i * 512: gcol + (i + 1) * 512], 8, 512) for i in range(2)]
            for oc in range(8):
                psy, bpy = psg.get()
                for k in range(4):
                    S.pe(lambda e, k=k, oc=oc: e.matmul(psy[:, 0:N], lhsT=wo_[:, k, oc * 128:(oc + 1) * 128],
                                                        rhs=brT[:, k, 0:N], start=(k == 0), stop=(k == 3)),
                         reads=[bwo, Bbr[k]], writes=[bpy])
                psgt, bpg = psg.get()
                wgt, bwg = wg[oc // 4]
                mm_fm(psgt, 128, N, wgt, bwg, (oc % 4) * 128, 8, hT, Bh, extra_w=[bpg])
                sg, bsg = ft.get()
                S.act(lambda e: e.activation(sg[:, 0:N], psgt[:, 0:N], AF.Sigmoid), reads=[bpg], writes=[bsg])
                if first:
                    S.dve(lambda e, oc=oc: e.tensor_tensor(mT[:, oc, 0:N], sg[:, 0:N], psy[:, 0:N], ALU.mult),
                          reads=[bsg, bpy], writes=[Bm[oc]])
                else:
                    S.dve(lambda e: e.tensor_tensor(sg[:, 0:N], sg[:, 0:N], psy[:, 0:N], ALU.mult),
                          reads=[bpy], writes=[bsg])
                    if last:
                        S.dve(lambda e, oc=oc: e.tensor_tensor(sq[:, oc, 0:N], mT[:, oc, 0:N], sg[:, 0:N], ALU.add),
                              reads=[bsg, Bm[oc]], writes=[Bsq[oc]])
                    else:
                        S.dve(lambda e, oc=oc: e.tensor_tensor(mT[:, oc, 0:N], mT[:, oc, 0:N], sg[:, 0:N], ALU.add),
                              reads=[bsg], writes=[Bm[oc]])

        def conv_branch(l, N, segs):
            wA, bA = wload(Wd["w_in"][l][:, C_AV:C_AV + 512], 8, 512)
            wG, bG = wload(Wd["w_in"][l][:, C_AG:C_AG + 512], 8, 512)
            for c in range(4):
                psv, bpv = psg.get()
                mm_fm(psv, 128, N, wA, bA, c * 128, 8, hT, Bh, extra_w=[bpv])
                psg_, bpg = psg.get()
                mm_fm(psg_, 128, N, wG, bG, c * 128, 8, hT, Bh, extra_w=[bpg])
                sg, bsg = ft.get()
                S.act(lambda e: e.activation(sg[:, 0:N], psg_[:, 0:N], AF.Sigmoid), reads=[bpg], writes=[bsg])
                av, bav = ft.get()
                S.act(lambda e: e.activation(av[:, 0:N], psv[:, 0:N], AF.Copy), reads=[bpv], writes=[bav])
                segs(l, c, av, bav, sg, bsg)

        def conv_post(l, N):
            for c in range(4):
                S.act(lambda e, c=c: e.activation(ybf[:, c, 0:N], yc[:, c, 0:N], AF.Copy), reads=[Byc[c]],
                      writes=[Bybf[c]])
                S.act(lambda e, c=c: e.activation(ysq[:, c, 0:N], yc[:, c, 0:N], AF.Square), reads=[Byc[c]],
                      writes=[Bysq[c]])
            psm, bpm = psg.get()
            for c in range(4):
                S.pe(lambda e, c=c: e.matmul(psm[:, 0:N], lhsT=onesC[:], rhs=ybf[:, c, 0:N], start=(c == 0),
                                             stop=(c == 3)), reads=[Bc, Bybf[c]], writes=[bpm])
            pss, bps = psg.get()
            for c in range(4):
                S.pe(lambda e, c=c: e.matmul(pss[:, 0:N], lhsT=onesC[:], rhs=ysq[:, c, 0:N], start=(c == 0),
                                             stop=(c == 3)), reads=[Bc, Bysq[c]], writes=[bps])
            mean, bmean = ft.get()
            S.dve(lambda e: e.tensor_copy(mean[:, 0:N], psm[:, 0:N]), reads=[bpm], writes=[bmean])
            var, bvar = ft.get()
            S.dve(lambda e: e.tensor_tensor(var[:, 0:N], mean[:, 0:N], mean[:, 0:N], ALU.mult), reads=[bmean],
                  writes=[bvar])
            S.dve(lambda e: e.tensor_tensor(var[:, 0:N], pss[:, 0:N], var[:, 0:N], ALU.subtract), reads=[bps],
                  writes=[bvar])
            rsqrt_eps(var[:, 0:N], var[:, 0:N], [], bvar)
            S.dve(lambda e: e.scalar_tensor_tensor(mean[:, 0:N], mean[:, 0:N], -1.0, var[:, 0:N], ALU.mult, ALU.mult),
                  reads=[bvar], writes=[bmean])
            for c in range(4):
                t, bt = ft.get()
                S.dve(lambda e, c=c: e.tensor_tensor(t[:, 0:N], yc[:, c, 0:N], var[:, 0:N], ALU.mult),
                      reads=[Byc[c], bvar], writes=[bt])
                S.dve(lambda e: e.tensor_tensor(t[:, 0:N], t[:, 0:N], mean[:, 0:N], ALU.add), reads=[bmean],
                      writes=[bt])
                S.act(lambda e, c=c: e.activation(brT[:, c, 0:N], t[:, 0:N], AF.Silu, bias=lnb[:, l, c:c + 1],
                                                  scale=lng[:, l, c:c + 1]), reads=[bt, Bc], writes=[Bbr[c]])

        def conv_taps(l, c, u0, y0, n):
            S.dve(lambda e: e.tensor_scalar(yc[:, c, y0:y0 + n], ubuf[:, c, u0:u0 + n], cw[:, l, c, 0:1],
                                            cb[:, l, c:c + 1], ALU.mult, ALU.add),
                  reads=[Bub[c], Bc], writes=[Byc[c]])
            for j in range(1, 31):
                S.dve(lambda e, j=j: e.scalar_tensor_tensor(yc[:, c, y0:y0 + n], ubuf[:, c, u0 + j:u0 + j + n],
                                                            cw[:, l, c, j:j + 1], yc[:, c, y0:y0 + n],
                                                            ALU.mult, ALU.add), reads=[Bub[c]], writes=[Byc[c]])

        def load_x_tile(src, N):
            nsub = (N + 127) // 128
            for i in range(nsub):
                n = min(128, N - i * 128)
                stt, bst_ = stg.get()
                S.dma(PQ, None, stt[0:n, :], src[i * 128:i * 128 + n, :], writes=[bst_])
                for half in range(2):
                    ps, bp = psg.get()
                    for cc in range(4):
                        c = half * 4 + cc
                        S.pe(lambda e, cc=cc, c=c, n=n, ps=ps, stt=stt: e.transpose(
                            ps[:, cc * 128:cc * 128 + n], stt[0:n, c * 128:(c + 1) * 128], ident[0:n, 0:n]),
                             reads=[bst_, Bc], writes=[bp])
                    S.act(lambda e, i=i, n=n, half=half, ps=ps: e.activation(
                        xT[:, half * 4:half * 4 + 4, i * 128:i * 128 + n],
                        ps[:, :].rearrange("p (c t) -> p c t", t=128)[:, :, 0:n], AF.Copy),
                          reads=[bp], writes=Bx[half * 4:half * 4 + 4])

        def store_y_tile(dst, N):
            for c in range(8):
                S.act(lambda e, c=c: e.activation(sq[:, c, 0:N], xT[:, c, 0:N], AF.Square),
                      reads=[Bx[c]], writes=[Bsq[c]])
            ps, bp = psg.get()
            for c in range(8):
                S.pe(lambda e, c=c: e.matmul(ps[:, 0:N], lhsT=onesD[:], rhs=sq[:, c, 0:N], start=(c == 0),
                                             stop=(c == 7)), reads=[Bc, Bsq[c]], writes=[bp])
            rs, brs = ft.get()
            rsqrt_eps(rs[:, 0:N], ps[:, 0:N], [bp], brs)
            for c in range(8):
                S.dve(lambda e, c=c: e.scalar_tensor_tensor(mT[:, c, 0:N], xT[:, c, 0:N], gf[:, c:c + 1], rs[:, 0:N],
                                                            ALU.mult, ALU.mult),
                      reads=[Bx[c], brs, Bc], writes=[Bm[c]])
            nsub = (N + 127) // 128
            for i in range(nsub):
                n = min(128, N - i * 128)
                stt, bst_ = stg.get()
                for half in range(2):
                    ps, bp = psg.get()
                    for cc in range(4):
                        c = half * 4 + cc
                        S.pe(lambda e, c=c, cc=cc, i=i, n=n, ps=ps: e.transpose(ps[0:n, cc * 128:(cc + 1) * 128],
                                                                         mT[:, c, i * 128:i * 128 + n], ident[:]),
                             reads=[Bm[c], Bc], writes=[bp])
                    S.act(lambda e, n=n, half=half, ps=ps, stt=stt: e.activation(
                        stt[0:n, half * 512:(half + 1) * 512], ps[0:n, :], AF.Copy), reads=[bp], writes=[bst_])
                S.dma(PQ, None, dst[i * 128:i * 128 + n, :], stt[0:n, :], reads=[bst_])

        def ffn_and_wo(l, N):
            wO = [wload(Wd["w_o"][l][:, i * 512:(i + 1) * 512], 8, 512) for i in range(2)]
            for oc in range(8):
                ps, bp = psg.get()
                wt, bw = wO[oc // 4]
                mm_fm(ps, 128, N, wt, bw, (oc % 4) * 128, 8, sq, Bsq, extra_w=[bp])
                S.dve(lambda e, oc=oc: e.tensor_tensor(xT[:, oc, 0:N], xT[:, oc, 0:N], ps[:, 0:N], ALU.add),
                      reads=[bp], writes=[Bx[oc]])
            rmsnorm(g2, l, N)
            for q in range(4):
                for j in range(2):
                    wU, bU = wload(Wd["w_up"][l][:, (q * 2 + j) * 512:(q * 2 + j + 1) * 512], 8, 512)
                    for cc in range(4):
                        ps, bp = psg.get()
                        mm_fm(ps, 128, N, wU, bU, cc * 128, 8, hT, Bh, extra_w=[bp])
                        r1, br1 = ft.get()
                        S.act(lambda e: e.activation(r1[:, 0:N], ps[:, 0:N], AF.Relu), reads=[bp], writes=[br1])
                        S.dve(lambda e, j=j, cc=cc: e.tensor_tensor(sq[:, j * 4 + cc, 0:N], r1[:, 0:N], r1[:, 0:N],
                                                                    ALU.mult), reads=[br1], writes=[Bsq[j * 4 + cc]])
                wDn = [wload(Wd["w_down"][l][q * 1024:(q + 1) * 1024, i * 512:(i + 1) * 512], 8, 512) for i in range(2)]
                for oc in range(8):
                    ps, bp = psg.get()
                    wt, bw = wDn[oc // 4]
                    mm_fm(ps, 128, N, wt, bw, (oc % 4) * 128, 8, sq, Bsq, extra_w=[bp])
                    S.dve(lambda e, oc=oc: e.tensor_tensor(xT[:, oc, 0:N], xT[:, oc, 0:N], ps[:, 0:N], ALU.add),
                          reads=[bp], writes=[Bx[oc]])

        def fox_qkv(l, N, tok0, kdst, vdst, lfdst, carry=True):
            wQ, bQ = wload(Wd["w_in"][l][:, C_Q:C_Q + 512], 8, 512)
            wK, bK = wload(Wd["w_in"][l][:, C_K:C_K + 512], 8, 512)
            wV, bV = wload(Wd["w_in"][l][:, C_V:C_V + 520], 8, 520)
            for hh in range(8):
                ps, bp = psg.get()
                mm_fm(ps, 64, N, wQ, bQ, hh * 64, 8, hT, Bh, extra_w=[bp])
                S.act(lambda e, hh=hh: e.activation(Qa[0:64, hh, 0:N], ps[0:64, 0:N], AF.Copy, scale=0.125),
                      reads=[bp], writes=[BQ[hh]])
                ps2, bp2 = psg.get()
                mm_fm(ps2, 64, N, wK, bK, hh * 64, 8, hT, Bh, extra_w=[bp2])
                S.act(lambda e, hh=hh: e.activation(Ka[0:64, hh, 0:N], ps2[0:64, 0:N], AF.Copy), reads=[bp2],
                      writes=[BK[hh]])
            ps, bp = psg.get()
            mm_fm(ps, 8, N, wV, bV, 512, 8, hT, Bh, extra_w=[bp])
            S.act(lambda e: e.activation(fT[:, 0:N], ps[0:8, 0:N], AF.Exp, bias=nbf[:, l:l + 1], scale=-1.0),
                  reads=[bp, Bc], writes=[Bf])
            S.act(lambda e: e.activation(fT[:, 0:N], fT[:, 0:N], AF.Ln, bias=1.0), writes=[Bf])
            return wK, bK, wV, bV

        def fox_scan_rows(l, N, c0, n, init_ap, Binit):
            S.dve(lambda e: e.tensor_tensor_scan(csp[:, c0:c0 + n], ones[0:8, 0:n], fT[:, c0:c0 + n], init_ap,
                                                 ALU.mult, ALU.add), reads=[Bf, Bc] + Binit, writes=[Bf])

        def fox_split_rows(N):
            S.dve(lambda e: e.tensor_copy(chi[:, 0:N], csp[:, 0:N]), writes=[Bf])
            S.dve(lambda e: e.tensor_tensor(clo[:, 0:N], csp[:, 0:N], chi[:, 0:N], ALU.subtract), writes=[Bf])
            S.dve(lambda e: e.tensor_scalar(nhi[:, 0:N], chi[:, 0:N], -1.0, None, ALU.mult), writes=[Bf])
            S.dve(lambda e: e.tensor_scalar(nlo[:, 0:N], clo[:, 0:N], -1.0, None, ALU.mult), writes=[Bf])
            pairs = []
            for hh in range(8):
                pairs.append((Qa[64:65, hh, 0:N], nhi[hh:hh + 1, 0:N]))
                pairs.append((Qa[65:66, hh, 0:N], nlo[hh:hh + 1, 0:N]))
                pairs.append((Ka[66:67, hh, 0:N], chi[hh:hh + 1, 0:N]))
                pairs.append((Ka[67:68, hh, 0:N], clo[hh:hh + 1, 0:N]))
            S.dma_group(PQ, rowsem, pairs, reads=[Bf], writes=BQ + BK)

        def fox_tok_outputs(l, N, wK, bK, wV, bV, kdst_fn, vdst_fn, lfdst_fn, va_fn):
            nsub = (N + 127) // 128
            for i in range(nsub):
                n = min(128, N - i * 128)
                stt, bst_ = stg.get()
                psk, bpk = psg.get()
                for k in range(8):
                    S.pe(lambda e, k=k, i=i, n=n, psk=psk: e.matmul(psk[0:n, :], lhsT=hT[:, k, i * 128:i * 128 + n],
                                                           rhs=wK[:, k, 0:512], start=(k == 0), stop=(k == 7)),
                         reads=[bK, Bh[k]], writes=[bpk])
                S.act(lambda e, n=n, psk=psk, stt=stt: e.activation(stt[0:n, 0:512], psk[0:n, :], AF.Copy),
                      reads=[bpk], writes=[bst_])
                psv, bpv = psg.get()
                for k in range(8):
                    S.pe(lambda e, k=k, i=i, n=n, psv=psv: e.matmul(psv[0:n, :], lhsT=hT[:, k, i * 128:i * 128 + n],
                                                           rhs=wV[:, k, 0:512], start=(k == 0), stop=(k == 7)),
                         reads=[bV, Bh[k]], writes=[bpv])
                S.act(lambda e, n=n, psv=psv, stt=stt: e.activation(stt[0:n, 512:1024], psv[0:n, :], AF.Copy),
                      reads=[bpv], writes=[bst_])
                if "nova" not in DBG2:
                    va_fn(i, n, stt, bst_)
                if "nolf" in DBG2:
                    kdst_fn(i, n, stt, bst_)
                    vdst_fn(i, n, stt, bst_)
                    continue
                psf, bpf = psg.get()
                S.pe(lambda e, i=i, n=n, psf=psf: e.transpose(psf[0:n, 0:8], fT[0:8, i * 128:i * 128 + n],
                                                              ident[0:8, 0:8]), reads=[Bf, Bc], writes=[bpf])
                S.act(lambda e, i=i, n=n, psf=psf: e.activation(lftok[0:n, i, :], psf[0:n, 0:8], AF.Copy, scale=-1.0),
                      reads=[bpf], writes=[Blftok])
                if "nokv" not in DBG2:
                    kdst_fn(i, n, stt, bst_)
                    vdst_fn(i, n, stt, bst_)
                if "nolfdma" not in DBG2:
                    lfdst_fn(i, n)

        def fox_attend_prompt(l, j, groups=(0, 1, 2, 3)):
            for g in groups:
                pairs = []
                for jj in range(j + 1):
                    for hl in range(2):
                        for kt in range(4):
                            pairs.append((jj, hl, kt))
                src = {}
                state = {}

                def get_src(jj):
                    if jj not in src:
                        kt_t, bkt = ksrc.get()
                        ksm = kvsem[(ksrc.i - 1) % 2]
                        vt_t, _unused = vsrc.get()
                        S.dma_group(WQ, ksm, [(kt_t[:], kscr[l, :, 2 * g:2 * g + 2, jj * T:(jj + 1) * T]),
                                              (vt_t[:], vscr[l, :, 2 * g:2 * g + 2, jj * 4:(jj + 1) * 4, :])],
                                    raw=[Bkscr[l][jj], Bvscr[l][jj]], writes=[bkt])
                        src[jj] = (kt_t, vt_t, bkt)
                    return src[jj]

                def emit_qk(n):
                    jj, hl, kt = pairs[n]
                    kt_t, vt_t, bkt = get_src(jj)
                    hh = 2 * g + hl
                    q0 = kt * 128 if jj == j else 0
                    ps, bp = psS.get()
                    S.pe(lambda e: e.matmul(ps[:, q0:T], lhsT=kt_t[:, hl, kt * 128:(kt + 1) * 128],
                                            rhs=Qa[:, hh, q0:T], start=True, stop=True),
                         reads=[bkt, BQ[hh]], writes=[bp])
                    state[n] = (ps, bp, q0)

                def emit_rest(n):
                    jj, hl, kt = pairs[n]
                    kt_t, vt_t, bkt = get_src(jj)
                    ps, bp, q0 = state.pop(n)
                    P, bP = Prot.get()
                    S.act(lambda e: e.activation(P[:, q0:T], ps[:, q0:T], AF.Exp), reads=[bp], writes=[bP])
                    if jj == j:
                        S.dve(lambda e: e.tensor_tensor(P[:, q0:q0 + 128], P[:, q0:q0 + 128], triub[:], ALU.mult),
                              reads=[Bc], writes=[bP])
                    for i in range(q0 // 128, 4):
                        S.pe(lambda e, i=i: e.matmul(psO[hl][:, i, 0:65], lhsT=P[:, i * 128:(i + 1) * 128],
                                                     rhs=vt_t[:, hl, kt, 0:65], start=first[hl], stop=False,
                                                     skip_group_check=True),
                             reads=[bP, bkt], writes=[BpsO[hl]])
                        first[hl] = False

                first = [True, True]
                emit_qk(0)
                for n in range(len(pairs)):
                    if n + 1 < len(pairs):
                        emit_qk(n + 1)
                    emit_rest(n)
                for hl in range(2):
                    hh = 2 * g + hl
                    r, br = rc.get()
                    S.dve(lambda e, hl=hl, r=r: e.reciprocal(r[:], psO[hl][:, :, 64]), reads=[BpsO[hl]], writes=[br])
                    for i in range(4):
                        S.dve(lambda e, hl=hl, hh=hh, i=i, r=r: e.tensor_scalar(
                            otok[:, i, hh * 64:(hh + 1) * 64], psO[hl][:, i, 0:64], r[:, i:i + 1], None, ALU.mult),
                              reads=[BpsO[hl], br], writes=[Botok[i]])

        def fox_o_transpose(N):
            nsub = (N + 127) // 128
            for fc in range(4):
                ps, bp = psg.get()
                for i in range(nsub):
                    n = min(128, N - i * 128)
                    S.pe(lambda e, i=i, n=n, fc=fc: e.transpose(ps[:, i * 128:i * 128 + n],
                                                                otok[0:n, i, fc * 128:(fc + 1) * 128],
                                                                ident[0:n, 0:n]),
                         reads=[Botok[i], Bc], writes=[bp])
                S.act(lambda e, fc=fc: e.activation(brT[:, fc, 0:N], ps[:, 0:N], AF.Copy), reads=[bp],
                      writes=[Bbr[fc]])

        def hgrn_branch(l, N, C, segs, S_in_fn, S_out_fn, between=None):
            nch = N // C
            wi, bi = wload(Wd["w_in"][l][:, C_HI:C_HI + 512], 8, 512)
            wq, bq = wload(Wd["w_in"][l][:, C_HQ:C_HQ + 512], 8, 512)
            wf, bf_ = wload(Wd["w_in"][l][:, C_HF:C_HF + 512], 8, 512)
            for ci in range(nch):
                ps, bp = psg.get()
                for k in range(8):
                    S.pe(lambda e, k=k, ci=ci: e.matmul(ps[0:C, :], lhsT=hT[:, k, ci * C:(ci + 1) * C],
                                                        rhs=wi[:, k, :], start=(k == 0), stop=(k == 7)),
                         reads=[bi, Bh[k]], writes=[bp])
                S.act(lambda e, ci=ci: e.activation(vtok[0:C, ci, :], ps[0:C, :], AF.Copy), reads=[bp],
                      writes=[Bvtok[ci]])
            wo_, bo = wload(Wd["w_in"][l][:, C_HO:C_HO + 512], 8, 512)
            mid = C // 2 - 1
            for hd in range(4):
                ps, bp = psg.get()
                mm_fm(ps, 128, N, wq, bq, hd * 128, 8, hT, Bh, extra_w=[bp])
                S.act(lambda e: e.activation(qs[:, 0:N], ps[:, 0:N], AF.Silu), reads=[bp], writes=[Bqs])
                ps2, bp2 = psg.get()
                mm_fm(ps2, 128, N, wf, bf_, hd * 128, 8, hT, Bh, extra_w=[bp2])
                sg, bsg = ft.get()
                S.act(lambda e: e.activation(sg[:, 0:N], ps2[:, 0:N], AF.Sigmoid), reads=[bp2], writes=[bsg])
                lg, blg = ft.get()
                S.act(lambda e, hd=hd: e.activation(lg[:, 0:N], sg[:, 0:N], AF.Ln, bias=lb[:, hd, l:l + 1],
                                                    scale=oml[:, hd, l:l + 1]), reads=[bsg, Bc], writes=[blg])
                S.dve(lambda e, hd=hd: e.tensor_scalar(kk[:, 0:N], sg[:, 0:N], noml[:, hd, l:l + 1],
                                                       oml[:, hd, l:l + 1], ALU.mult, ALU.add),
                      reads=[bsg, Bc], writes=[Bkk])
                for (c0, ncs, si) in segs:
                    S.dve(lambda e, c0=c0, ncs=ncs: e.tensor_tensor_scan(
                        bg[:, c0 * C:(c0 + ncs) * C], ones[:, 0:ncs * C], lg[:, c0 * C:(c0 + ncs) * C], 0.0,
                        ALU.mult, ALU.add), reads=[blg, Bc], writes=[Bbg])
                bg3 = bg[:, 0:N].rearrange("p (c s) -> p c s", s=C)
                dm, bdm = ft.get()
                dm3 = dm[:, 0:N].rearrange("p (c s) -> p c s", s=C)
                S.dve(lambda e: e.tensor_tensor(dm3, bg3, bg[:, mid:N:C].unsqueeze(2).to_broadcast([128, nch, C]),
                                                ALU.subtract), reads=[Bbg], writes=[bdm])
                bst, bbst = ft.get()
                S.dve(lambda e: e.tensor_tensor(bst[:, 0:nch], bg[:, 0:N:C], lg[:, 0:N:C], ALU.subtract),
                      reads=[Bbg, blg], writes=[bbst])
                ds, bds = ft.get()
                ds3 = ds[:, 0:N].rearrange("p (c s) -> p c s", s=C)
                S.dve(lambda e: e.tensor_tensor(ds3, bg3, bst[:, 0:nch].unsqueeze(2).to_broadcast([128, nch, C]),
                                                ALU.subtract), reads=[bbst, Bbg], writes=[bds])
                dl, bdl = ft.get()
                dl3 = dl[:, 0:N].rearrange("p (c s) -> p c s", s=C)
                S.dve(lambda e: e.tensor_tensor(dl3, bg3, bg[:, C - 1:N:C].unsqueeze(2).to_broadcast([128, nch, C]),
                                                ALU.subtract), reads=[Bbg], writes=[bdl])
                if between is not None:
                    between(hd)
                e1, be1 = ft.get()
                S.act(lambda e: e.activation(e1[:, 0:N], dm[:, 0:N], AF.Exp), reads=[bdm], writes=[be1])
                e2, be2 = ft.get()
                S.act(lambda e: e.activation(e2[:, 0:N], dm[:, 0:N], AF.Exp, scale=-1.0), reads=[bdm], writes=[be2])
                e3, be3 = ft.get()
                S.act(lambda e: e.activation(e3[:, 0:N], ds[:, 0:N], AF.Exp), reads=[bds], writes=[be3])
                e4, be4 = ft.get()
                S.act(lambda e: e.activation(e4[:, 0:N], dl[:, 0:N], AF.Exp, scale=-1.0), reads=[bdl], writes=[be4])
                S.dve(lambda e: e.tensor_tensor(qt[:, 0:N], qs[:, 0:N], e1[:, 0:N], ALU.mult), reads=[Bqs, be1],
                      writes=[Bqt])
                S.dve(lambda e: e.tensor_tensor(kt_[:, 0:N], kk[:, 0:N], e2[:, 0:N], ALU.mult), reads=[Bkk, be2],
                      writes=[Bkt])
                S.dve(lambda e: e.tensor_tensor(qh[:, 0:N], qs[:, 0:N], e3[:, 0:N], ALU.mult), reads=[Bqs, be3],
                      writes=[Bqh])
                S.dve(lambda e: e.tensor_tensor(kh[:, 0:N], kk[:, 0:N], e4[:, 0:N], ALU.mult), reads=[Bkk, be4],
                      writes=[Bkh])
                for ci in range(nch):
                    pst_, bpt = psg.get()
                    S.pe(lambda e, ci=ci: e.transpose(pst_[0:C, 0:128], kh[:, ci * C:(ci + 1) * C], ident[:]),
                         reads=[Bkh, Bc], writes=[bpt])
                    S.act(lambda e, ci=ci: e.activation(khtok[0:C, ci, :], pst_[0:C, 0:128], AF.Copy), reads=[bpt],
                          writes=[Bkhtok[ci]])
                psA, bpA = psg.get()
                for ci in range(nch):
                    S.pe(lambda e, ci=ci: e.matmul(psA[0:C, ci * C:(ci + 1) * C], lhsT=kt_[:, ci * C:(ci + 1) * C],
                                                   rhs=qt[:, ci * C:(ci + 1) * C], start=True, stop=True),
                         reads=[Bkt, Bqt], writes=[bpA])
                S.dve(lambda e: e.tensor_tensor(Abf[0:C, 0:nch, 0:C],
                                                psA[0:C, 0:nch * C].rearrange("p (c s) -> p c s", s=C),
                                                triu[0:C, 0:C].unsqueeze(1).to_broadcast([C, nch, C]), ALU.mult),
                      reads=[bpA, Bc], writes=[BAbf])
                pso, bpo = psS.get()
                for (c0, ncs, si) in segs:
                    Sap, BSt = S_in_fn(hd, si)
                    sbf, bsbf = Sbf.get()
                    S.dve(lambda e, Sap=Sap, sbf=sbf: e.tensor_copy(sbf[:], Sap), reads=[BSt], writes=[bsbf])
                    for ci in range(c0, c0 + ncs):
                        psd, bpd = psg.get()
                        S.pe(lambda e, ci=ci, hd=hd, psd=psd: e.matmul(psd[:, 0:128], lhsT=khtok[0:C, ci, :],
                                                                       rhs=vtok[0:C, ci, hd * 128:(hd + 1) * 128],
                                                                       start=True, stop=True),
                             reads=[Bkhtok[ci], Bvtok[ci]], writes=[bpd])
                        S.pe(lambda e, ci=ci, hd=hd: e.matmul(pso[:, ci * C:(ci + 1) * C],
                                                              lhsT=vtok[0:C, ci, hd * 128:(hd + 1) * 128],
                                                              rhs=Abf[0:C, ci, 0:C], start=True, stop=False),
                             reads=[Bvtok[ci], BAbf], writes=[bpo])
                        S.pe(lambda e, ci=ci, sbf=sbf: e.matmul(pso[:, ci * C:(ci + 1) * C], lhsT=sbf[:],
                                                                rhs=qh[:, ci * C:(ci + 1) * C], start=False, stop=True),
                             reads=[bsbf, Bqh], writes=[bpo])
                        S.dve(lambda e, ci=ci, Sap=Sap, psd=psd, e3=e3: e.scalar_tensor_tensor(
                            Sap, Sap, e3[:, (ci + 1) * C - 1:(ci + 1) * C], psd[:, 0:128], ALU.mult, ALU.add),
                              reads=[bpd, be3], writes=[BSt])
                        if ci < c0 + ncs - 1:
                            sbf, bsbf = Sbf.get()
                            S.dve(lambda e, Sap=Sap, sbf=sbf: e.tensor_copy(sbf[:], Sap), reads=[BSt],
                                  writes=[bsbf])
                    S_out_fn(hd, si, Sap, BSt)
                S.act(lambda e: e.activation(sqo[:, 0:N], pso[:, 0:N], AF.Square), reads=[bpo], writes=[Bsqo])
                psm, bpm = psg.get()
                S.pe(lambda e: e.matmul(psm[:, 0:N], lhsT=onesV[:], rhs=sqo[:, 0:N], start=True, stop=True),
                     reads=[Bsqo, Bc], writes=[bpm])
                rs, brs = ft.get()
                rsqrt_eps(rs[:, 0:N], psm[:, 0:N], [bpm], brs)
                on, bon = ft.get()
                S.dve(lambda e: e.scalar_tensor_tensor(on[:, 0:N], pso[:, 0:N], hng[:, l:l + 1], rs[:, 0:N], ALU.mult,
                                                       ALU.mult), reads=[bpo, brs, Bc], writes=[bon])
                psq, bpq = psg.get()
                mm_fm(psq, 128, N, wo_, bo, hd * 128, 8, hT, Bh, extra_w=[bpq])
                go, bgo = ft.get()
                S.act(lambda e: e.activation(go[:, 0:N], psq[:, 0:N], AF.Silu), reads=[bpq], writes=[bgo])
                S.dve(lambda e, hd=hd: e.tensor_tensor(brT[:, hd, 0:N], on[:, 0:N], go[:, 0:N], ALU.mult),
                      reads=[bon, bgo], writes=[Bbr[hd]])

        init_consts()

        import os as _os
        DBG = _os.environ.get("KDBG", "full")
        DBG2 = _os.environ.get("KDBG2", "")

        def prompt_tile(j):
            N = T
            tok0 = j * T
            load_x_tile(xp[tok0:tok0 + T, :], N)
            for l in range(L if DBG != "load" else 0):
                rmsnorm(g1, l, N)
                def segs(l_, c, psv, bpv, sg, bsg):
                    S.act(lambda e: e.activation(ubuf[:, c, 0:30], hist[:, l_, c, :], AF.Copy), reads=[Bhist[l_][c]],
                          writes=[Bub[c]])
                    S.dve(lambda e: e.tensor_tensor(ubuf[:, c, 30:30 + N], psv[:, 0:N], sg[:, 0:N], ALU.mult),
                          reads=[bpv, bsg], writes=[Bub[c]])
                    S.dve(lambda e: e.tensor_copy(hist[:, l_, c, :], ubuf[:, c, N:N + 30]), reads=[Bub[c]],
                          writes=[Bhist[l_][c]])
                    conv_taps(l_, c, 0, 0, N)
                if DBG == "norm":
                    continue
                conv_branch(l, N, segs)
                wK, bK, wV, bV = fox_qkv(l, N, tok0, None, None, None)
                def va_fn(i, n, stt, bst_):
                    S.act(lambda e: e.activation(Va[0:n, :, i, 0:64],
                                                 stt[0:n, 512:1024].rearrange("p (h d) -> p h d", d=64), AF.Copy),
                          reads=[bst_], writes=[BV])

                def kdst(i, n, stt, bst_):
                    S.dma(PQ, None, kp[l, tok0 + i * 128: tok0 + i * 128 + n, :], stt[0:n, 0:512],
                          reads=[bst_])

                def vdst(i, n, stt, bst_):
                    S.dma(PQ, None, vp[l, tok0 + i * 128: tok0 + i * 128 + n, :], stt[0:n, 512:1024],
                          reads=[bst_])

                def lfdst(i, n):
                    if i == NSUB - 1:
                        S.dma(PQ, lfsem, lfp[l, tok0:tok0 + N, :].rearrange("(s p) h -> p s h", p=128), lftok[:],
                              reads=[Blftok])
                fox_tok_outputs(l, N, wK, bK, wV, bV, kdst, vdst, lfdst, va_fn)
                conv_post(l, N)
                branch_out(l, "w_a_out", C_GA, N, True, False)
                if j == NT - 1:
                    for c in range(4):
                        S.dma(PQ, osem, cp[l][:, c * 128:(c + 1) * 128].rearrange("j p -> p j"), hist[:, l, c, :],
                              reads=[Bhist[l][c]])
                fox_scan_rows(l, N, 0, N, ccar[:, l:l + 1], [Bccar[l]])
                S.dve(lambda e: e.tensor_copy(ccar[:, l:l + 1], csp[:, N - 1:N]), reads=[Bf], writes=[Bccar[l]])
                fox_split_rows(N)
                S.dma(PQ, ksem_, kscr[l, :, :, tok0:tok0 + N], Ka[:], reads=BK, writes=[Bkscr[l][j]])
                S.dma(PQ, vsem_, vscr[l, :, :, j * NSUB:(j + 1) * NSUB, :], Va[:], reads=[BV], writes=[Bvscr[l][j]])
                def S_in(hd, si):
                    return Sst[:, l, hd, :], BS[l][hd]

                def S_out(hd, si, Sap, BSt):
                    if j == NT - 1:
                        S.dma(PQ, osem, hp[l, hd], Sap, reads=[BSt])
                hgrn_branch(l, N, 64, [(0, NCH, 0)], S_in, S_out,
                            between=lambda hd: fox_attend_prompt(l, j, [hd]))
                branch_out(l, "w_c_out", C_GC, N, False, False)
                fox_o_transpose(N)
                branch_out(l, "w_b_out", C_GB, N, False, True)
                if DBG == "hgrn":
                    continue
                ffn_and_wo(l, N)
            store_y_tile(yp[tok0:tok0 + T, :], N)


        def sample_tile():
            N = NSS * LS
            for i_ in range(2):
                kct = ksrc.tiles[i_][:].rearrange("p a (b k) -> p (a b) k", k=128)
                vct = vsrc.tiles[i_][:].rearrange("p a b w -> p (a b) w")
                S.pool(lambda e, kct=kct: e.memset(kct[64:68, :, :], 0.0), writes=[ksrc.bufs[i_]])
                S.pool(lambda e, kct=kct: e.memset(kct[64:66, :, :], 1.0), writes=[ksrc.bufs[i_]])
                S.pool(lambda e, vct=vct: e.memset(vct[:, :, 64:VW], 1.0), writes=[ksrc.bufs[i_]])
            load_x_tile(xs, N)
            ub4 = ubuf[:, :, 0:4 * 46].rearrange("p c (s w) -> p c s w", w=46)
            for l in range(L):
                rmsnorm(g1, l, N)
                for s_ in range(NSS):
                    stt, bst_ = stg.get()
                    S.dma(PQ, None, stt[0:30, 0:512], sconv[l, s_], writes=[bst_])
                    ps, bp = psg.get()
                    for c in range(4):
                        S.pe(lambda e, c=c: e.transpose(ps[:, c * 32:c * 32 + 30], stt[0:30, c * 128:(c + 1) * 128],
                                                        ident[0:30, 0:30]), reads=[bst_, Bc], writes=[bp])
                    S.act(lambda e, s_=s_: e.activation(ub4[:, :, s_, 0:30],
                                                        ps[:, 0:128].rearrange("p (c w) -> p c w", w=32)[:, :, 0:30],
                                                        AF.Copy), reads=[bp], writes=Bub)

                def segs(l_, c, psv, bpv, sg, bsg):
                    S.dve(lambda e: e.tensor_tensor(ub4[:, c, :, 30:46],
                                                    psv[:, 0:N].rearrange("p (s t) -> p s t", t=LS),
                                                    sg[:, 0:N].rearrange("p (s t) -> p s t", t=LS), ALU.mult),
                          reads=[bpv, bsg], writes=[Bub[c]])
                    y4 = yc[:, c, 0:N].rearrange("p (s t) -> p s t", t=LS)
                    S.dve(lambda e: e.tensor_scalar(y4, ub4[:, c, :, 0:LS], cw[:, l_, c, 0:1], cb[:, l_, c:c + 1],
                                                    ALU.mult, ALU.add), reads=[Bub[c], Bc], writes=[Byc[c]])
                    for j in range(1, 31):
                        S.dve(lambda e, j=j: e.scalar_tensor_tensor(y4, ub4[:, c, :, j:j + LS], cw[:, l_, c, j:j + 1],
                                                                    y4, ALU.mult, ALU.add),
                              reads=[Bub[c]], writes=[Byc[c]])
                conv_branch(l, N, segs)
                for s_ in range(NSS):
                    ps, bp = psg.get()
                    for c in range(4):
                        S.pe(lambda e, c=c, s_=s_: e.transpose(ps[0:30, c * 128:(c + 1) * 128], ub4[:, c, s_, 16:46],
                                                               ident[:]), reads=[Bub[c], Bc], writes=[bp])
                    stt, bst_ = stg.get()
                    S.act(lambda e: e.activation(stt[0:30, 0:512], ps[0:30, :], AF.Copy), reads=[bp], writes=[bst_])
                    S.dma(PQ, None, cs[l, s_], stt[0:30, 0:512], reads=[bst_])
                conv_post(l, N)
                branch_out(l, "w_a_out", C_GA, N, True, False)
                wK, bK, wV, bV = fox_qkv(l, N, 0, None, None, None)
                for s_ in range(NSS):
                    fox_scan_rows(l, N, s_ * LS, LS, 0.0, [])
                fox_split_rows(N)

                def va_fn(i, n, stt, bst_):
                    S.act(lambda e: e.activation(Va[0:n, :, i, 0:64],
                                                 stt[0:n, 512:1024].rearrange("p (h d) -> p h d", d=64), AF.Copy),
                          reads=[bst_], writes=[BV])

                def kdst(i, n, stt, bst_):
                    S.dma(PQ, None, ks[l].rearrange("s t c -> (s t) c"), stt[0:n, 0:512], reads=[bst_])

                def vdst(i, n, stt, bst_):
                    S.dma(PQ, None, vs[l].rearrange("s t c -> (s t) c"), stt[0:n, 512:1024], reads=[bst_])

                def lfdst(i, n):
                    S.dma(PQ, lfsem, lfs[l].rearrange("s t h -> (s t) h"), lftok[0:n, 0, :], reads=[Blftok])
                fox_tok_outputs(l, N, wK, bK, wV, bV, kdst, vdst, lfdst, va_fn)
                fox_attend_sample(l)
                for fc in range(4):
                    ps, bp = psg.get()
                    for s_ in range(NSS):
                        S.pe(lambda e, s_=s_, fc=fc: e.transpose(ps[:, s_ * LS:(s_ + 1) * LS],
                                                                 otok[0:LS, s_, fc * 128:(fc + 1) * 128],
                                                                 ident[0:LS, 0:LS]),
                             reads=[Botok[s_], Bc], writes=[bp])
                    S.act(lambda e, fc=fc: e.activation(brT[:, fc, 0:N], ps[:, 0:N], AF.Copy), reads=[bp],
                          writes=[Bbr[fc]])
                branch_out(l, "w_b_out", C_GB, N, False, False)
                def S_in(hd, si):
                    t_, b_ = Ssm.get()
                    S.dma(PQ, None, t_[:], shg[l, si, hd], writes=[b_])
                    return t_[:], b_

                def S_out(hd, si, Sap, BSt):
                    S.dma(PQ, None, hs[l, si, hd], Sap, reads=[BSt])
                hgrn_branch(l, N, LS, [(s_, 1, s_) for s_ in range(NSS)], S_in, S_out)
                branch_out(l, "w_c_out", C_GC, N, False, True)
                ffn_and_wo(l, N)
            store_y_tile(ys, N)

        def fox_attend_sample(l):
            N = NSS * LS
            psn, bpn = psS.get()
            for hh in range(8):
                S.pe(lambda e, hh=hh: e.matmul(psn[0:N, hh * N:(hh + 1) * N], lhsT=Ka[:, hh, 0:N], rhs=Qa[:, hh, 0:N],
                                               start=True, stop=True), reads=[BK[hh], BQ[hh]], writes=[bpn])
            Pn, bPn = sqo, Bsqo
            S.act(lambda e: e.activation(Pn[0:N, :], psn[0:N, :], AF.Exp), reads=[bpn], writes=[bPn])
            S.dve(lambda e: e.tensor_tensor(Pn[0:N, :].rearrange("p (h q) -> p h q", q=N),
                                            Pn[0:N, :].rearrange("p (h q) -> p h q", q=N),
                                            bmask[:, :].unsqueeze(1).to_broadcast([N, 8, N]), ALU.mult),
                  reads=[Bc], writes=[bPn])
            for s_ in range(NSS):
                S.dma(PQ, None, lfc[:], clf[l, s_].rearrange("(kt p) h -> p kt h", p=128), writes=[Brk])
                lfc2 = lfc[:].rearrange("p k h -> p (k h)")
                ps1, bp1 = psg.get()
                S.pe(lambda e: e.matmul(ps1[:, 0:128], lhsT=slm[:], rhs=lfc2, start=True, stop=True),
                     reads=[Brk, Bc], writes=[bp1])
                ps2, bp2 = psg.get()
                S.pe(lambda e: e.matmul(ps2[:, 0:128], lhsT=ones[:, 0:128], rhs=lfc2, start=True, stop=True),
                     reads=[Brk, Bc], writes=[bp2])
                S.dve(lambda e: e.tensor_copy(rkt[:].rearrange("p k h -> p (k h)"), ps2[:, 0:128]), reads=[bp2],
                      writes=[Brk])
                S.dve(lambda e: e.tensor_copy(rk[:].rearrange("p k h -> p (k h)"), ps1[:, 0:128]), reads=[bp1],
                      writes=[Brk])
                S.dve(lambda e: e.tensor_copy(lfc[:, 15, :], rkt[:, 15, :]), writes=[Brk])
                S.dve(lambda e: e.memset(rkt[:, 15, :], 0.0), writes=[Brk])
                for kt in range(14, -1, -1):
                    S.dve(lambda e, kt=kt: e.tensor_copy(lfc[:, kt, :], rkt[:, kt, :]), writes=[Brk])
                    S.dve(lambda e, kt=kt: e.tensor_tensor(rkt[:, kt, :], rkt[:, kt + 1, :], lfc[:, kt + 1, :], ALU.add),
                          writes=[Brk])
                S.dve(lambda e: e.tensor_tensor(rk[:], rk[:], rkt[:], ALU.add), writes=[Brk])
                first = [True, True]
                for kt in range(NPAST // 128):
                    stt, bst_ = stg.get()
                    S.dma_group(PQ, None if False else S.auto_sem((), [bst_]),
                                [(stt[:, 0:512], ck[l, s_, kt * 128:(kt + 1) * 128, :]),
                                 (stt[:, 512:1024], cv[l, s_, kt * 128:(kt + 1) * 128, :])], writes=[bst_])
                    kct_t, bkc = ksrc.get()
                    vct_t, _u = vsrc.get()
                    kct = kct_t[:].rearrange("p a (b k) -> p (a b) k", k=128)
                    vct = vct_t[:].rearrange("p a b w -> p (a b) w")
                    for half in range(2):
                        pst_, bpt = psg.get()
                        for hq in range(4):
                            hh = half * 4 + hq
                            S.pe(lambda e, hh=hh, hq=hq, pst_=pst_, stt=stt: e.transpose(
                                pst_[0:64, hq * 128:(hq + 1) * 128], stt[:, hh * 64:(hh + 1) * 64], ident[:]),
                                 reads=[bst_, Bc], writes=[bpt])
                        S.act(lambda e, half=half, pst_=pst_, kct=kct: e.activation(
                            kct[0:64, half * 4:half * 4 + 4, :], pst_[0:64, :].rearrange("p (h k) -> p h k", k=128),
                            AF.Copy), reads=[bpt], writes=[bkc])
                    S.dve(lambda e, vct=vct, stt=stt: e.tensor_copy(
                        vct[:, :, 0:64], stt[:, 512:1024].rearrange("p (h d) -> p h d", d=64)),
                          reads=[bst_], writes=[bkc])
                    pss, bps = psS.get()
                    for hh in range(8):
                        S.pe(lambda e, hh=hh, pss=pss, kct=kct: e.matmul(
                            pss[:, hh * LS:(hh + 1) * LS], lhsT=kct[:, hh, :], rhs=Qa[:, hh, s_ * LS:(s_ + 1) * LS],
                            start=True, stop=True), reads=[bkc, BQ[hh]], writes=[bps])
                    Pt, bPt = Prot.get()
                    Pf, bPf = ft.get()
                    S.dve(lambda e, kt=kt, pss=pss, Pf=Pf: e.tensor_tensor(
                        Pf[:, 0:128].rearrange("p (h q) -> p h q", q=LS),
                        pss[:, 0:128].rearrange("p (h q) -> p h q", q=LS),
                        rk[:, kt, :].unsqueeze(2).to_broadcast([128, 8, LS]), ALU.add),
                          reads=[bps, Brk], writes=[bPf])
                    S.act(lambda e, Pt=Pt, Pf=Pf: e.activation(Pt[:, 0:128], Pf[:, 0:128], AF.Exp), reads=[bPf],
                          writes=[bPt])
                    for hh in range(8):
                        b_ = hh // 4
                        S.pe(lambda e, hh=hh, b_=b_, Pt=Pt, vct=vct, f=first[b_]: e.matmul(
                            psO[b_][0:LS, hh % 4, 0:65], lhsT=Pt[:, hh * LS:(hh + 1) * LS], rhs=vct[:, hh, 0:65],
                            start=f, stop=False, skip_group_check=True), reads=[bPt, bkc], writes=[BpsO[b_]])
                        first[b_] = False
                for hh in range(8):
                    b_ = hh // 4
                    S.pe(lambda e, hh=hh, b_=b_: e.matmul(
                        psO[b_][0:LS, hh % 4, 0:65], lhsT=Pn[0:N, hh * N + s_ * LS: hh * N + (s_ + 1) * LS],
                        rhs=Va[0:N, hh, 0, 0:65], start=False, stop=True, skip_group_check=True),
                         reads=[bPn, BV], writes=[BpsO[b_]])
                for b_ in range(2):
                    r, br = rc.get()
                    S.dve(lambda e, b_=b_, r=r: e.reciprocal(r[0:LS, :], psO[b_][0:LS, :, 64]), reads=[BpsO[b_]],
                          writes=[br])
                    for hq in range(4):
                        hh = b_ * 4 + hq
                        S.dve(lambda e, b_=b_, hq=hq, hh=hh, r=r: e.tensor_scalar(
                            otok[0:LS, s_, hh * 64:(hh + 1) * 64], psO[b_][0:LS, hq, 0:64], r[0:LS, hq:hq + 1], None,
                            ALU.mult), reads=[BpsO[b_], br], writes=[Botok[s_]])

        if PROMPT and DBG != "init":
            for j in range(NT):
                prompt_tile(j)

        if SAMPLE:
            sample_tile()

        with nc.allow_non_contiguous_dma(reason="small strided param/state transfers"):
            S.emit()
    return nc, S


_CACHE = {}


def kernel(**inputs):
    inp = {k: np.ascontiguousarray(np.asarray(v, dtype=np.float32)) for k, v in inputs.items()}
    L = inp["w_in"].shape[0]
    B, SEQ, _ = inp["x_prompt"].shape
    key = (SEQ, L)
    if key not in _CACHE:
        _CACHE[key] = build_program(SEQ=SEQ, L=L)[0]
    nc = _CACHE[key]
    ncores = 8
    in_maps = []
    for c in range(ncores):
        m = {
            "xp": inp["x_prompt"][c % B],
            "xs": inp["x_sample"][4 * c:4 * c + 4].reshape(NSS * LS, D),
            "ck": inp["cache_fox_k"][:, 4 * c:4 * c + 4].reshape(L, NSS, NPAST, 512),
            "cv": inp["cache_fox_v"][:, 4 * c:4 * c + 4].reshape(L, NSS, NPAST, 512),
            "clf": inp["cache_fox_logf"][:, 4 * c:4 * c + 4],
            "sconv": inp["state_conv"][:, 4 * c:4 * c + 4],
            "shg": inp["state_hgrn"][:, 4 * c:4 * c + 4],
        }
        for n in WNAMES:
            m[n] = inp[n] if n != "final_g" else inp[n].reshape(1, D)
        in_maps.append({k: np.ascontiguousarray(v) for k, v in m.items()})
    res = run_bass_kernel_spmd(nc, in_maps, core_ids=list(range(ncores)))
    R = res.results
    y_prompt = np.stack([R[b]["yp"] for b in range(B)])
    y_sample = np.concatenate([R[c]["ys"].reshape(NSS, LS, D) for c in range(ncores)], axis=0)
    kpo = np.stack([R[b]["kp"] for b in range(B)], axis=1).reshape(L, B, SEQ, 8, 64)
    vpo = np.stack([R[b]["vp"] for b in range(B)], axis=1).reshape(L, B, SEQ, 8, 64)
    lfpo = np.stack([R[b]["lfp"] for b in range(B)], axis=1)
    cpo = np.stack([R[b]["cp"] for b in range(B)], axis=1)
    hpo = np.stack([R[b]["hp"] for b in range(B)], axis=1)
    kso = np.concatenate([R[c]["ks"] for c in range(ncores)], axis=1).reshape(L, 4 * ncores, LS, 8, 64)
    vso = np.concatenate([R[c]["vs"] for c in range(ncores)], axis=1).reshape(L, 4 * ncores, LS, 8, 64)
    lfso = np.concatenate([R[c]["lfs"] for c in range(ncores)], axis=1)
    cso = np.concatenate([R[c]["cs"] for c in range(ncores)], axis=1)
    hso = np.concatenate([R[c]["hs"] for c in range(ncores)], axis=1)
    return (y_prompt, y_sample, kpo, vpo, lfpo, cpo, hpo, kso, vso, lfso, cso, hso)
```
